# Optimizing a Trainium2 kernel written in Bass

```python
import jax, jax.numpy as jnp
from jax import lax
import numpy as np

D_MODEL = 2048
BATCH = 1
SEQ = 8192
DEPTH = 2

GLA_HEADS = 4
GLA_DK = 64
GLA_DV = 128
GLA_RANK = 16
GLA_TAU = 16.0
GLA_CHUNK = 64
GLA_QK = GLA_HEADS * GLA_DK
GLA_V = GLA_HEADS * GLA_DV
FOX_HEADS = 6
FOX_DH = 128
FOX_W = FOX_HEADS * FOX_DH
SB_HEADS = 6
SB_DH = 128
SB_W = SB_HEADS * SB_DH
Q_BLOCK = 128
D_FF = 5632
N_BRANCH = 3
LN_EPS = 1e-5
ALPHA = (2 * DEPTH) ** 0.25
BETA = (8 * DEPTH) ** -0.25
IN_COLS = 2 * GLA_QK + 2 * GLA_V + GLA_RANK + 3 * FOX_W + FOX_HEADS + 3 * SB_W + N_BRANCH * D_MODEL

kernel_name = "hybrid_gla_fox_stickbreaking_macaron_deepnorm"


def _layernorm(x, g, b):
    xf = x.astype(jnp.float32)
    mu = jnp.mean(xf, axis=-1, keepdims=True)
    var = jnp.mean(jnp.square(xf - mu), axis=-1, keepdims=True)
    return ((xf - mu) * lax.rsqrt(var + LN_EPS) * g + b).astype(x.dtype)


def _swiglu(x, w_gate, w_up, w_down):
    return (jax.nn.silu(x @ w_gate) * (x @ w_up)) @ w_down


def _gla(q, k, v, log_a):
    B, S, H, dk = q.shape
    dv = v.shape[-1]
    nc = S // GLA_CHUNK

    def chunked(t):
        t = jnp.moveaxis(t.astype(jnp.float32).reshape(B, nc, GLA_CHUNK, H, t.shape[-1]), 1, 0)
        return t.transpose(0, 1, 3, 2, 4)

    qc = chunked(q) * (dk ** -0.5)
    kc = chunked(k)
    vc = chunked(v)
    bc = lax.cumsum(chunked(log_a), axis=3)
    causal = jnp.tril(jnp.ones((GLA_CHUNK, GLA_CHUNK), dtype=bool))[:, :, None]

    def step(state, inp):
        qi, ki, vi, bi = inp
        o_inter = jnp.einsum('bhtk,bhkv->bhtv', qi * jnp.exp(bi), state)
        diff = bi[:, :, :, None, :] - bi[:, :, None, :, :]
        decay = jnp.exp(jnp.where(causal, diff, -jnp.inf))
        att = jnp.einsum('bhtk,bhsk,bhtsk->bhts', qi, ki, decay)
        o_intra = jnp.einsum('bhts,bhsv->bhtv', att, vi)
        b_last = bi[:, :, -1:, :]
        new_state = state * jnp.exp(b_last[:, :, 0, :])[..., None] + jnp.einsum(
            'bhsk,bhsv->bhkv', ki * jnp.exp(b_last - bi), vi)
        return new_state, o_inter + o_intra

    state0 = jnp.zeros((B, H, dk, dv), jnp.float32)
    _, o = lax.scan(step, state0, (qc, kc, vc, bc))
    return o.transpose(1, 0, 3, 2, 4).reshape(B, S, H, dv)


def _fox(q, k, v, log_f):
    B, S, H, d = q.shape
    nb = S // Q_BLOCK
    scale = d ** -0.5
    c = jnp.cumsum(log_f.astype(jnp.float32), axis=1)
    ck = c.transpose(0, 2, 1)
    kf = k.astype(jnp.float32)
    vf = v.astype(jnp.float32)
    qb = jnp.moveaxis(q.astype(jnp.float32).reshape(B, nb, Q_BLOCK, H, d), 1, 0)
    cb = jnp.moveaxis(c.reshape(B, nb, Q_BLOCK, H), 1, 0)
    tb = jnp.arange(S).reshape(nb, Q_BLOCK)
    kpos = jnp.arange(S)

    def block(args):
        qi, ci, ti = args
        s = jnp.einsum('bqhd,bkhd->bhqk', qi, kf) * scale
        s = s + ci.transpose(0, 2, 1)[..., None] - ck[:, :, None, :]
        s = jnp.where(kpos[None, :] <= ti[:, None], s, -jnp.inf)
        p = jax.nn.softmax(s, axis=-1)
        return jnp.einsum('bhqk,bkhd->bqhd', p, vf)

    o = lax.map(block, (qb, cb, tb))
    return jnp.moveaxis(o, 0, 1).reshape(B, S, H, d)


def _stick_breaking(q, k, v):
    B, S, H, d = q.shape
    nb = S // Q_BLOCK
    scale = d ** -0.5
    kf = k.astype(jnp.float32)
    vf = v.astype(jnp.float32)
    qb = jnp.moveaxis(q.astype(jnp.float32).reshape(B, nb, Q_BLOCK, H, d), 1, 0)
    tb = jnp.arange(S).reshape(nb, Q_BLOCK)
    kpos = jnp.arange(S)

    def block(args):
        qi, ti = args
        z = jnp.einsum('bqhd,bkhd->bhqk', qi, kf) * scale
        mask = kpos[None, :] < ti[:, None]
        log_1mb = jnp.where(mask, jax.nn.log_sigmoid(-z), 0.0)
        rest = lax.cumsum(log_1mb, axis=3, reverse=True) - log_1mb
        a = jnp.where(mask, jnp.exp(jax.nn.log_sigmoid(z) + rest), 0.0)
        return jnp.einsum('bhqk,bkhd->bqhd', a, vf)

    o = lax.map(block, (qb, tb))
    return jnp.moveaxis(o, 0, 1).reshape(B, S, H, d)


def _mixer(h, w_in, gla_w_a2, gla_b_a, gla_norm_g, fox_b_f, w_br_gla, w_br_fox, w_br_sb, w_out):
    B, S, _ = h.shape
    u = h @ w_in
    sizes = (GLA_QK, GLA_QK, GLA_V, GLA_V, GLA_RANK, FOX_W, FOX_W, FOX_W, FOX_HEADS,
             SB_W, SB_W, SB_W, N_BRANCH * D_MODEL)
    idx = [int(i) for i in np.cumsum(sizes)[:-1]]
    gq, gk, gv, gr, ga, fq, fk, fv, ff, sq, sk, sv, gates = jnp.split(u, idx, axis=-1)

    log_a = jax.nn.log_sigmoid((ga @ gla_w_a2 + gla_b_a).astype(jnp.float32)) / GLA_TAU
    o = _gla(gq.reshape(B, S, GLA_HEADS, GLA_DK), gk.reshape(B, S, GLA_HEADS, GLA_DK),
             gv.reshape(B, S, GLA_HEADS, GLA_DV), log_a.reshape(B, S, GLA_HEADS, GLA_DK))
    o = o * lax.rsqrt(jnp.mean(jnp.square(o), axis=-1, keepdims=True) + LN_EPS) * gla_norm_g
    o_gla = (o.reshape(B, S, GLA_V) * jax.nn.silu(gr.astype(jnp.float32))).astype(h.dtype)

    log_f = jax.nn.log_sigmoid((ff + fox_b_f).astype(jnp.float32))
    o_fox = _fox(fq.reshape(B, S, FOX_HEADS, FOX_DH), fk.reshape(B, S, FOX_HEADS, FOX_DH),
                 fv.reshape(B, S, FOX_HEADS, FOX_DH), log_f).reshape(B, S, FOX_W).astype(h.dtype)

    o_sb = _stick_breaking(sq.reshape(B, S, SB_HEADS, SB_DH), sk.reshape(B, S, SB_HEADS, SB_DH),
                           sv.reshape(B, S, SB_HEADS, SB_DH)).reshape(B, S, SB_W).astype(h.dtype)

    g = jax.nn.sigmoid(gates.reshape(B, S, N_BRANCH, D_MODEL))
    m = g[:, :, 0] * (o_gla @ w_br_gla) + g[:, :, 1] * (o_fox @ w_br_fox) + g[:, :, 2] * (o_sb @ w_br_sb)
    return m @ w_out


def setup_inputs(seed: int = 0) -> dict:
    key = jax.random.key(seed)
    ks = jax.random.split(key, 22)
    f32 = jnp.float32
    L, D, F = DEPTH, D_MODEL, D_FF

    def nrm(k, shape, scale):
        return jax.random.normal(k, shape, f32) * scale

    return {
        "x": nrm(ks[0], (BATCH, SEQ, D), 1.0),
        "ffn1_w_gate": nrm(ks[1], (L, D, F), D ** -0.5),
        "ffn1_w_up": nrm(ks[2], (L, D, F), D ** -0.5),
        "ffn1_w_down": nrm(ks[3], (L, F, D), BETA * F ** -0.5),
        "ln1_g": 1.0 + nrm(ks[4], (L, D), 0.02),
        "ln1_b": nrm(ks[5], (L, D), 0.02),
        "w_in": nrm(ks[6], (L, D, IN_COLS), D ** -0.5),
        "gla_w_a2": nrm(ks[7], (L, GLA_RANK, GLA_QK), GLA_RANK ** -0.5),
        "gla_b_a": nrm(ks[8], (L, GLA_QK), 0.02),
        "gla_norm_g": 1.0 + nrm(ks[9], (L, GLA_DV), 0.02),
        "fox_b_f": 2.0 + nrm(ks[10], (L, FOX_HEADS), 0.5),
        "w_br_gla": nrm(ks[11], (L, GLA_V, D), BETA * GLA_V ** -0.5),
        "w_br_fox": nrm(ks[12], (L, FOX_W, D), BETA * FOX_W ** -0.5),
        "w_br_sb": nrm(ks[13], (L, SB_W, D), BETA * SB_W ** -0.5),
        "w_out": nrm(ks[14], (L, D, D), BETA * D ** -0.5),
        "ln2_g": 1.0 + nrm(ks[15], (L, D), 0.02),
        "ln2_b": nrm(ks[16], (L, D), 0.02),
        "ffn2_w_gate": nrm(ks[17], (L, D, F), D ** -0.5),
        "ffn2_w_up": nrm(ks[18], (L, D, F), D ** -0.5),
        "ffn2_w_down": nrm(ks[19], (L, F, D), BETA * F ** -0.5),
        "ln3_g": 1.0 + nrm(ks[20], (L, D), 0.02),
        "ln3_b": nrm(ks[21], (L, D), 0.02),
    }


def reference(x, ffn1_w_gate, ffn1_w_up, ffn1_w_down, ln1_g, ln1_b, w_in, gla_w_a2, gla_b_a,
              gla_norm_g, fox_b_f, w_br_gla, w_br_fox, w_br_sb, w_out, ln2_g, ln2_b,
              ffn2_w_gate, ffn2_w_up, ffn2_w_down, ln3_g, ln3_b):
    for l in range(DEPTH):
        x = _layernorm(ALPHA * x + 0.5 * _swiglu(x, ffn1_w_gate[l], ffn1_w_up[l], ffn1_w_down[l]),
                       ln1_g[l], ln1_b[l])
        x = _layernorm(ALPHA * x + _mixer(x, w_in[l], gla_w_a2[l], gla_b_a[l], gla_norm_g[l], fox_b_f[l],
                                          w_br_gla[l], w_br_fox[l], w_br_sb[l], w_out[l]),
                       ln2_g[l], ln2_b[l])
        x = _layernorm(ALPHA * x + 0.5 * _swiglu(x, ffn2_w_gate[l], ffn2_w_up[l], ffn2_w_down[l]),
                       ln3_g[l], ln3_b[l])
    return x
```

```python
import numpy as np
import ml_dtypes
from contextlib import ExitStack
import concourse.bass as bass
import concourse.mybir as mybir
from concourse.bass_utils import run_bass_kernel_spmd

F32 = mybir.dt.float32
BF16 = mybir.dt.bfloat16
AF = mybir.ActivationFunctionType
ALU = mybir.AluOpType

NCORES = 8
DM = 2048
S = 8192
T = 1024
DEPTH = 2
DFF = 5632
NFC = 44
NG = 11
ALPHA = (2 * DEPTH) ** 0.25
LN_EPS = 1e-5
NEG = -30000.0
O_GQ, O_GK, O_GV, O_GR, O_GA = 0, 256, 512, 1024, 1536
O_FQ, O_FK, O_FV, O_FF = 1552, 2320, 3088, 3856
O_SQ, O_SK, O_SV, O_GATE = 3862, 4630, 5398, 6166

ENGS = ("pe", "act", "dve", "pool", "sp")


class Op:
    __slots__ = ("eng", "fn", "reads", "writes", "dma_key", "deps", "has_dep", "seq", "idx")

    def __init__(self, eng, fn, reads, writes, dma_key):
        self.eng = eng
        self.fn = fn
        self.reads = reads
        self.writes = writes
        self.dma_key = dma_key
        self.deps = []
        self.has_dep = False
        self.seq = None
        self.idx = None


class Prog:
    def __init__(self, nc):
        self.nc = nc
        self.ops = []
        self.last_writer = {}
        self.readers = {}
        self.fence_op = None
        self.dram_dep = {}

    def fence(self, eng="dve"):
        keep = lambda k: isinstance(k, tuple) and k[0] == "ccw"
        keys = set(k for k in (set(self.last_writer.keys()) | set(self.readers.keys())) if not keep(k))
        o = self.op(eng, lambda e: e.engine_nop(), reads=(), writes=tuple(keys))
        self.last_writer = {k: v for k, v in self.last_writer.items() if keep(k)}
        self.readers = {k: v for k, v in self.readers.items() if keep(k)}
        self.fence_op = o
        return o

    def op(self, eng, fn, reads=(), writes=(), dma_key=None):
        o = Op(eng, fn, tuple(reads), tuple(writes), dma_key)
        o.idx = len(self.ops)
        deps = set()
        for k in o.reads:
            w = self.last_writer.get(k, None if (isinstance(k, tuple) and k[0] == "ccw") else self.fence_op)
            if w is not None:
                deps.add(w)
        for k in o.writes:
            w = self.last_writer.get(k, self.fence_op)
            if w is not None:
                deps.add(w)
            for r in self.readers.get(k, ()):
                deps.add(r)
        o.deps = sorted(deps, key=lambda d: d.idx)
        for k in o.writes:
            self.last_writer[k] = o
            self.readers[k] = []
        for k in o.reads:
            lst = self.readers.setdefault(k, [])
            if o.dma_key is None:
                lst[:] = [r for r in lst if not (r.dma_key is None and r.eng == o.eng)]
            lst.append(o)
        self.ops.append(o)
        return o

    def dma(self, eng, out, in_, key, reads=(), writes=()):
        reads = list(reads)
        try:
            nm = in_.tensor.name
            if nm in self.dram_dep:
                reads.append(self.dram_dep[nm])
        except AttributeError:
            pass
        return self.op(eng, lambda e: e.dma_start(out=out, in_=in_), reads, writes, dma_key=key)

    def emit(self):
        nc = self.nc
        ops = self.ops
        for o in ops:
            real = []
            for d in o.deps:
                if d.dma_key is None and o.dma_key is None and d.eng == o.eng and d.eng == "pe":
                    continue
                real.append(d)
            o.deps = real
            for d in real:
                d.has_dep = True
        cnt = {e: 0 for e in ENGS}
        dcnt = {}
        self.cc_inc = lambda k: 1 if (isinstance(k, str) and k.endswith("_sem")) else 16
        for o in ops:
            if o.dma_key is not None:
                dcnt[o.dma_key] = dcnt.get(o.dma_key, 0) + self.cc_inc(o.dma_key)
                o.seq = dcnt[o.dma_key]
            elif o.has_dep:
                cnt[o.eng] += 1
                o.seq = cnt[o.eng]
        with ExitStack() as st:
            esem = {e: st.enter_context(nc.semaphore("s_" + e)) for e in ENGS}
            dsem = {}
            for i, k in enumerate(dcnt.keys()):
                dsem[k] = st.enter_context(nc.semaphore("d%d" % i))
            block = st.enter_context(nc.Block())
            by_eng = {e: [o for o in ops if o.eng == e] for e in ENGS}

            def run(eng_name, eng):
                waited = {}
                for o in by_eng[eng_name]:
                    for d in o.deps:
                        if d.dma_key is not None:
                            s, v, key = dsem[d.dma_key], d.seq, ("d", d.dma_key)
                        else:
                            s, v, key = esem[d.eng], d.seq, ("e", d.eng)
                        if waited.get(key, 0) >= v:
                            continue
                        waited[key] = v
                        eng.wait_ge(s, v)
                    ins = o.fn(eng)
                    if o.dma_key is not None:
                        ins.then_inc(dsem[o.dma_key], self.cc_inc(o.dma_key))
                    elif o.has_dep:
                        ins.then_inc(esem[o.eng], 1)
                fin = {}
                for o in by_eng[eng_name]:
                    if o.dma_key is not None:
                        fin[o.dma_key] = max(fin.get(o.dma_key, 0), o.seq)
                for k, v in fin.items():
                    if waited.get(("d", k), 0) < v:
                        eng.wait_ge(dsem[k], v)

            @block.tensor
            def _(e):
                run("pe", e)

            @block.scalar
            def _(e):
                run("act", e)

            @block.vector
            def _(e):
                run("dve", e)

            @block.gpsimd
            def _(e):
                run("pool", e)

            @block.sync
            def _(e):
                run("sp", e)


class Ctx:
    pass


def _mm(C, out, lhsT, rhs, start, stop, reads, writes, skip=False):
    if skip:
        C.P.op("pe", lambda e: e.matmul(out, lhsT, rhs, start=start, stop=stop, skip_group_check=True), reads, writes)
    else:
        C.P.op("pe", lambda e: e.matmul(out, lhsT, rhs, start=start, stop=stop), reads, writes)


def _act(C, out, in_, func, reads, writes, bias=None, scale=None):
    kw = {}
    if bias is not None:
        kw["bias"] = bias
    if scale is not None:
        kw["scale"] = scale
    C.P.op("act", lambda e: e.activation(out, in_, func, **kw), reads, writes)


def _tt(C, eng, out, in0, in1, op, reads, writes):
    C.P.op(eng, lambda e: e.tensor_tensor(out, in0, in1, op), reads, writes)


def _stt(C, eng, out, in0, scalar, in1, op0, op1, reads, writes):
    C.P.op(eng, lambda e: e.scalar_tensor_tensor(out, in0, scalar, in1, op0, op1), reads, writes)


def _ts(C, eng, out, in0, s1, s2, op0, op1, reads, writes):
    if s2 is None:
        C.P.op(eng, lambda e: e.tensor_scalar(out, in0, s1, None, op0), reads, writes)
    else:
        C.P.op(eng, lambda e: e.tensor_scalar(out, in0, s1, s2, op0, op1), reads, writes)


def _cp(C, eng, out, in_, reads, writes):
    if eng == "act":
        C.P.op(eng, lambda e: e.copy(out, in_), reads, writes)
    else:
        C.P.op(eng, lambda e: e.tensor_copy(out, in_), reads, writes)


def setup_ctx(nc):
    C = Ctx()
    C.nc = nc
    C.P = Prog(nc)
    A = nc.alloc_sbuf_tensor
    C.acc = A("sb_acc", [128, 16, T], F32)
    C.xbf = A("sb_xbf", [128, 16, T], BF16)
    C.kv = A("sb_kv", [128, 16384], BF16)
    C.W = A("sb_W", [128, 20480], BF16)
    C.wk = A("sb_wk", [128, 5120], F32)
    C.cst = A("sb_cst", [128, 8, 128], F32)
    C.cbf = A("sb_cbf", [128, 4, 128], BF16)
    C.lnp = A("sb_lnp", [128, 2 * 3 * 2 * 16], F32)
    C.sel = A("sb_sel", [128, 8], F32)
    C.sm = A("sb_sm", [128, 1664], F32)
    C.ps = [nc.alloc_psum_tensor("ps%d" % i, [128, 512], F32) for i in range(8)]
    C.cnt = 0
    return C


def wk_tile(C, i):
    return C.wk[:, i * 512:(i + 1) * 512]


def wk_bf(C, i):
    return C.wk.bitcast(BF16)[:, i * 1024:(i + 1) * 1024]


def wkv(C, t, a, b):
    return C.wk[:, t * 512 + a:t * 512 + b]


def wkb(C, t, a, b):
    return C.wk.bitcast(BF16)[:, t * 1024 + a:t * 1024 + b]


def WK(t):
    return ("wk", t)


def psk(i):
    return ("ps", i)


def acck(c, h):
    return ("acc", c, h)


def xbk(c, h):
    return ("xbf", c, h)


def emit_consts(C, cst_d, cbf_d, lnp_d, msk_d, sel_d):
    P = C.P
    P.dma("sp", C.cst[:, :, :], cst_d, key="cst", writes=["cst"])
    P.dma("sp", C.cbf[:, :, :], cbf_d, key="cbf", writes=["cbf"])
    P.dma("sp", C.lnp[:, :], lnp_d, key="lnp", writes=["lnp"])
    P.dma("sp", C.sel[:, :], sel_d, key="sel", writes=["sel"])


CI_ONESD, CI_TINCL, CI_TGT, CI_ONESS, CI_MASKINC01, CI_ONES128 = 0, 1, 2, 3, 4, 5
BI_ONES, BI_ZERO, BI_MSTRICT = 0, 1, 2


def host_consts(core):
    cst = np.zeros((128, 8, 128), np.float32)
    s = np.arange(128)[:, None]
    t = np.arange(128)[None, :]
    cst[:, CI_ONESD, :] = 1.0 / DM
    cst[:, CI_TINCL, :] = np.where(s <= t, -1.0 / 16, 0.0)
    cst[:, CI_TGT, :] = np.where(s > t, -1.0 / 16, 0.0)
    cst[:, CI_ONESS, :] = -1.0 / 16
    cst[:, CI_MASKINC01, :] = np.where(s <= t, 1.0, 0.0)
    cst[:, CI_ONES128, :] = 1.0 / 128
    cst[:, 6, :] = np.where(s <= t, 1.0, 0.0)
    cst[:, 7, :] = 1.0
    cbf = np.zeros((128, 4, 128), np.float32)
    cbf[:, BI_ONES, :] = 1.0
    cbf[:, BI_MSTRICT, :] = np.where(s > t, 1.0, 0.0)
    cbf = cbf.astype(ml_dtypes.bfloat16)
    msk = np.zeros((128, 3, 8, 128), np.float32)
    for m in range(8):
        if m < core:
            inc = np.zeros((128, 128)); stn = np.zeros((128, 128)); st01 = np.ones((128, 128))
        elif m == core:
            inc = np.where(s <= t, 0.0, NEG); stn = np.where(s < t, 0.0, NEG); st01 = np.where(s < t, 1.0, 0.0)
        else:
            inc = np.full((128, 128), NEG); stn = np.full((128, 128), NEG); st01 = np.zeros((128, 128))
        msk[:, 0, m, :] = inc
        msk[:, 1, m, :] = stn
        msk[:, 2, m, :] = st01
    sel = np.zeros((128, 8), np.float32)
    sel[:, :core] = 1.0
    return cst, cbf, msk.astype(ml_dtypes.bfloat16), sel


def emit_ffn(C, wgu_d, wd_d):
    P = C.P
    Wv = C.W
    wgu = [Wv[:, s * 4096:(s + 1) * 4096].rearrange("p (a c f) -> p a c f", a=2, c=16) for s in range(3)]
    wd = [Wv[:, (3 + s) * 4096:(4 + s) * 4096].rearrange("p (j n) -> p j n", j=4) for s in range(2)]
    hT = [C.kv[:, s * 4096:(s + 1) * 4096].rearrange("p (j t) -> p j t", j=4) for s in range(2)]
    for c in range(16):
        for h in range(2):
            _ts(C, "pool", C.acc[:, c, h * 512:(h + 1) * 512], C.acc[:, c, h * 512:(h + 1) * 512], ALPHA, None, ALU.mult, None,
                reads=[acck(c, h)], writes=[acck(c, h)])
    dcount = 0
    ugcount = 0
    for g in range(NG):
        hs = g % 2
        for j in range(4):
            fc = 4 * g + j
            sl = fc % 3
            P.dma("pool", wgu[sl], wgu_d[fc], key=("wgu", sl), writes=[("W", 2 * sl), ("W", 2 * sl + 1)])
            for h in range(2):
                pi = ugcount % 2
                ugcount += 1
                psg, psu = C.ps[2 * pi], C.ps[2 * pi + 1]
                hsl = slice(h * 512, (h + 1) * 512)
                for c in range(16):
                    _mm(C, psg[:, :], wgu[sl][:, 0, c, :], C.xbf[:, c, hsl], c == 0, c == 15,
                        reads=[("W", 2 * sl), xbk(c, h)], writes=[psk(2 * pi)])
                for c in range(16):
                    _mm(C, psu[:, :], wgu[sl][:, 1, c, :], C.xbf[:, c, hsl], c == 0, c == 15,
                        reads=[("W", 2 * sl + 1), xbk(c, h)], writes=[psk(2 * pi + 1)])
                sg = wk_tile(C, pi)
                _act(C, sg, psg[:, :], AF.Silu, reads=[psk(2 * pi)], writes=[("wk", pi)])
                _tt(C, "dve", hT[hs][:, j, hsl], sg, psu[:, :], ALU.mult,
                    reads=[("wk", pi), psk(2 * pi + 1)], writes=[("hT", hs, j, h)])
        for dh in range(2):
            sl = dh
            P.dma("pool", wd[sl], wd_d[g, dh], key=("wd", sl), writes=[("W", 6 + 2 * sl), ("W", 7 + 2 * sl)])
            for dcl in range(8):
                dc = dh * 8 + dcl
                for h in range(2):
                    pb = 4 + dcount % 4
                    dcount += 1
                    hsl = slice(h * 512, (h + 1) * 512)
                    for j in range(4):
                        _mm(C, C.ps[pb][:, :], wd[sl][:, j, dcl * 128:(dcl + 1) * 128], hT[hs][:, j, hsl], j == 0, j == 3,
                            reads=[("W", 6 + 2 * sl), ("W", 7 + 2 * sl), ("hT", hs, j, h)], writes=[psk(pb)])
                    _stt(C, "dve", C.acc[:, dc, hsl], C.ps[pb][:, :], 0.5, C.acc[:, dc, hsl], ALU.mult, ALU.add,
                         reads=[psk(pb), acck(dc, h)], writes=[acck(dc, h)])


def emit_ln(C, li, want_bf=True):
    gcol = lambda c: C.lnp[:, (li * 2 + 0) * 16 + c:(li * 2 + 0) * 16 + c + 1]
    bcol = lambda c: C.lnp[:, (li * 2 + 1) * 16 + c:(li * 2 + 1) * 16 + c + 1]
    onesD = C.cst[:, CI_ONESD, :]
    for h in range(2):
        hsl = slice(h * 512, (h + 1) * 512)
        o = 0
        s1, s2, tq, mean, rstd, tmp = [wk_tile(C, o + i) for i in range(6)]
        k = lambda i: ("wk", o + i)
        _tt(C, "dve", s1, C.acc[:, 0, hsl], C.acc[:, 1, hsl], ALU.add, reads=[acck(0, h), acck(1, h)], writes=[k(0)])
        for c in range(2, 16):
            _tt(C, "dve", s1, s1, C.acc[:, c, hsl], ALU.add, reads=[k(0), acck(c, h)], writes=[k(0)])
        _act(C, s2, C.acc[:, 0, hsl], AF.Square, reads=[acck(0, h)], writes=[k(1)])
        for c in range(1, 16):
            _act(C, tq, C.acc[:, c, hsl], AF.Square, reads=[acck(c, h)], writes=[k(2)])
            _tt(C, "pool", s2, s2, tq, ALU.add, reads=[k(1), k(2)], writes=[k(1)])
        pm, pe2 = 2 * h, 2 * h + 1
        _mm(C, C.ps[pm][:, :], onesD, s1, True, True, reads=["cst", k(0)], writes=[psk(pm)])
        _mm(C, C.ps[pe2][:, :], onesD, s2, True, True, reads=["cst", k(1)], writes=[psk(pe2)])
        _cp(C, "act", mean, C.ps[pm][:, :], reads=[psk(pm)], writes=[k(3)])
        _tt(C, "dve", tmp, mean, mean, ALU.mult, reads=[k(3)], writes=[k(5)])
        _tt(C, "dve", tmp, C.ps[pe2][:, :], tmp, ALU.subtract, reads=[psk(pe2), k(5)], writes=[k(5)])
        _ts(C, "dve", tmp, tmp, LN_EPS, None, ALU.add, None, reads=[k(5)], writes=[k(5)])
        _act(C, tmp, tmp, AF.Sqrt, reads=[k(5)], writes=[k(5)])
        C.P.op("dve", lambda e, a=rstd, b=tmp: e.reciprocal(a, b), reads=[k(5)], writes=[k(4)])
        for c in range(16):
            tb = [s1, s2][c % 2]
            kb = [k(0), k(1)][c % 2]
            _tt(C, "dve", tb, C.acc[:, c, hsl], mean, ALU.subtract, reads=[acck(c, h), k(3)], writes=[kb])
            _tt(C, "dve", tb, tb, rstd, ALU.mult, reads=[kb, k(4)], writes=[kb])
            _act(C, C.acc[:, c, hsl], tb, AF.Identity, reads=[kb, "lnp"], writes=[acck(c, h)], bias=bcol(c), scale=gcol(c))
            if want_bf:
                _cp(C, "pool", C.xbf[:, c, hsl], C.acc[:, c, hsl], reads=[acck(c, h)], writes=[xbk(c, h)])


WALL = [("W", i) for i in range(12)]


def emit_gla_load_weights(C, wl):
    P = C.P
    Wv = C.W
    C.wgfm = Wv[:, 0:16 * 528].rearrange("p (c n) -> p c n", c=16)
    C.wgtm = Wv[:, 10240:10240 + 16 * 512].rearrange("p (c n) -> p c n", c=16)
    P.dma("pool", C.wgfm, wl["w_gfm"], key="wgfm", writes=[("W", i) for i in range(5)])
    P.dma("pool", C.wgtm, wl["w_gtm"], key="wgtm", writes=[("W", i) for i in range(5, 9)])
    C.wa2 = C.sm[0:16, 0:256]
    C.babc = C.sm[:, 256:512]
    P.dma("sp", C.wa2, wl["wa2"], key="wa2", writes=["wa2"])
    P.dma("sp", C.babc, wl["ba"].partition_broadcast(128), key="babc", writes=["babc"])
    C.k_wgfm = [("W", i) for i in range(5)]
    C.k_wgtm = [("W", i) for i in range(5, 9)]


def emit_gla_tile(C, i, need_q):
    tsl = slice(i * 128, (i + 1) * 128)
    gaT = C.wk[0:16, 0:128]
    dcol = wkv(C, 0, 128, 130)
    xa = wkv(C, 1, 0, 256)
    lap = wkv(C, 1, 256, 512)
    ek = wkv(C, 2, 0, 256)
    kf = wkv(C, 2, 256, 512)
    ebT = wkv(C, 3, 0, 256)
    enbT = wkv(C, 3, 256, 512)
    qT = wkv(C, 4, 0, 256)
    kT = wkv(C, 4, 256, 512)
    khat = wkb(C, 5, 0, 256)
    vtm = wkb(C, 5, 256, 768)
    qtil = wkb(C, 8, 0, 256)
    ktil = wkb(C, 8, 256, 512)
    ps0, ps1, ps2, ps3 = C.ps[0], C.ps[1], C.ps[2], C.ps[3]
    xr = [xbk(c, i // 4) for c in range(16)]
    for u in range(4):
        if u < 2 and not need_q:
            continue
        for c in range(16):
            _mm(C, ps0[:, u * 128:(u + 1) * 128], C.wgfm[:, c, u * 128:(u + 1) * 128], C.xbf[:, c, tsl], c == 0, c == 15,
                reads=C.k_wgfm + xr, writes=[psk(0)])
    for c in range(16):
        _mm(C, ps1[0:16, 0:128], C.wgfm[:, c, 512:528], C.xbf[:, c, tsl], c == 0, c == 15, reads=C.k_wgfm + xr, writes=[psk(1)])
    for c in range(16):
        _mm(C, ps2[:, 0:256], C.xbf[:, c, tsl], C.wgfm[:, c, 256:512], c == 0, c == 15, reads=C.k_wgfm + xr, writes=[psk(2)])
    for c in range(16):
        _mm(C, ps3[:, :], C.xbf[:, c, tsl], C.wgtm[:, c, :], c == 0, c == 15, reads=C.k_wgtm + xr, writes=[psk(3)])
    _cp(C, "act", gaT, ps1[0:16, 0:128], reads=[psk(1)], writes=[WK(0)])
    if need_q:
        _act(C, qT, ps0[:, 0:256], AF.Identity, reads=[psk(0)], writes=[WK(4)], scale=0.125)
    _cp(C, "dve", kT, ps0[:, 256:512], reads=[psk(0)], writes=[WK(4)])
    _cp(C, "dve", kf, ps2[:, 0:256], reads=[psk(2)], writes=[WK(2)])
    _cp(C, "act", vtm, ps3[:, :], reads=[psk(3)], writes=[WK(5)])
    _mm(C, ps1[:, 256:512], gaT, C.wa2, True, True, reads=[WK(0), "wa2"], writes=[psk(1)])
    _tt(C, "dve", xa, ps1[:, 256:512], C.babc, ALU.add, reads=[psk(1), "babc"], writes=[WK(1)])
    _act(C, xa, xa, AF.Exp, reads=[WK(1)], writes=[WK(1)], scale=-1.0)
    _act(C, lap, xa, AF.Ln, reads=[WK(1)], writes=[WK(1)], bias=1.0)
    _mm(C, ps2[:, 256:512], C.cst[:, CI_TGT, :], lap, True, True, reads=["cst", WK(1)], writes=[psk(2)])
    _act(C, ek, ps2[:, 256:512], AF.Exp, reads=[psk(2)], writes=[WK(2)])
    _tt(C, "dve", khat, kf, ek, ALU.mult, reads=[WK(2)], writes=[WK(5)])
    for kc in range(2):
        _mm(C, ps1[:, 128 + kc:129 + kc], lap[:, kc * 128:(kc + 1) * 128], C.cst[:, CI_ONESS, 0:1], True, True,
            reads=["cst", WK(1)], writes=[psk(1)])
    _act(C, dcol, ps1[:, 128:130], AF.Exp, reads=[psk(1)], writes=[WK(0)])
    res = dict(khat=khat, vtm=vtm, dcol=dcol)
    if need_q:
        for kc in range(2):
            _mm(C, ps0[:, kc * 128:(kc + 1) * 128], lap[:, kc * 128:(kc + 1) * 128], C.cst[:, CI_TINCL, :], True, True,
                reads=["cst", WK(1)], writes=[psk(0)])
        _act(C, ebT, ps0[:, 0:256], AF.Exp, reads=[psk(0)], writes=[WK(3)])
        _act(C, enbT, ps0[:, 0:256], AF.Exp, reads=[psk(0)], writes=[WK(3)], scale=-1.0)
        _tt(C, "dve", qtil, qT, ebT, ALU.mult, reads=[WK(4), WK(3)], writes=[WK(8)])
        _tt(C, "dve", ktil, kT, enbT, ALU.mult, reads=[WK(4), WK(3)], writes=[WK(8)])
        res.update(qtil=qtil, ktil=ktil)
    return res


def emit_gla_dS(C, r, psa, psb):
    for kc in range(2):
        for hh in range(2):
            hd = 2 * kc + hh
            pb = psa if hh == 0 else psb
            _mm(C, C.ps[pb][:, kc * 128:(kc + 1) * 128], r["khat"][:, kc * 128:(kc + 1) * 128], r["vtm"][:, hd * 128:(hd + 1) * 128],
                True, True, reads=[WK(5)], writes=[psk(pb)])


def emit_phaseA_proj(C, wl, KT_o, V_o, lf_o, dS_o, dc_o):
    P = C.P
    Wv = C.W
    P.fence()
    wslot = [Wv[:, s * 2048:(s + 1) * 2048].rearrange("p (c f) -> p c f", c=16) for s in range(2)]
    for kc in range(12):
        sl = kc % 2
        P.dma("pool", wslot[sl], wl["w_kT"][kc], key=("Wm", sl), writes=[("W", sl)])
        for h in range(2):
            pb = (2 * kc + h) % 4
            hsl = slice(h * 512, (h + 1) * 512)
            for c in range(16):
                _mm(C, C.ps[pb][:, :], wslot[sl][:, c, :], C.xbf[:, c, hsl], c == 0, c == 15,
                    reads=[("W", sl), xbk(c, h)], writes=[psk(pb)])
            ot = wk_bf(C, pb)[:, 0:512]
            _cp(C, "act", ot, C.ps[pb][:, :], reads=[psk(pb)], writes=[WK(pb)])
            P.dma("sp", KT_o[kc, :, hsl], ot, key=("kto", pb), reads=[WK(pb)])
    wv = Wv[:, 4096:4096 + 8192].rearrange("p (c n) -> p c n", c=16)
    kwv = [("W", i) for i in range(2, 6)]
    for vt in range(3):
        P.dma("pool", wv, wl["w_v"][vt], key="wv", writes=kwv)
        for i in range(8):
            pb = 4 + (vt * 8 + i) % 4
            tsl = slice(i * 128, (i + 1) * 128)
            for c in range(16):
                _mm(C, C.ps[pb][:, :], C.xbf[:, c, tsl], wv[:, c, :], c == 0, c == 15,
                    reads=kwv + [xbk(c, i // 4)], writes=[psk(pb)])
            ot = wk_bf(C, pb)[:, 0:512]
            _cp(C, "dve", ot, C.ps[pb][:, :], reads=[psk(pb)], writes=[WK(pb)])
            P.dma("sp", V_o[vt * 4:(vt + 1) * 4, :, i, :].rearrange("h p d -> p h d"), ot.rearrange("p (h d) -> p h d", h=4),
                  key=("vo", pb), reads=[WK(pb)])
    wff = Wv[:, 12288:12288 + 96].rearrange("p (c n) -> p c n", c=16)
    P.dma("pool", wff, wl["w_ff"], key="wff", writes=[("W", 6)])
    bfbc = C.sm[:, 512:518]
    P.dma("sp", bfbc, wl["bf"].partition_broadcast(128), key="bfbc", writes=["bfbc"])
    for i in range(8):
        tsl = slice(i * 128, (i + 1) * 128)
        for c in range(16):
            _mm(C, C.ps[0][:, i * 6:(i + 1) * 6], C.xbf[:, c, tsl], wff[:, c, :], c == 0, c == 15,
                reads=[("W", 6), xbk(c, i // 4)], writes=[psk(0)])
    lft = C.sm[:, 520:568]
    for i in range(8):
        _tt(C, "dve", lft[:, i * 6:(i + 1) * 6], C.ps[0][:, i * 6:(i + 1) * 6], bfbc, ALU.add, reads=[psk(0), "bfbc"], writes=["lft"])
    _act(C, lft, lft, AF.Exp, reads=["lft"], writes=["lft"], scale=-1.0)
    _act(C, lft, lft, AF.Ln, reads=["lft"], writes=["lft"], bias=1.0)
    _ts(C, "dve", lft, lft, -1.0, None, ALU.mult, None, reads=["lft"], writes=["lft"])
    P.dma("sp", lf_o, lft, key="lfo", reads=["lft"])
    P.fence()
    emit_gla_load_weights(C, wl)
    for i in range(8):
        r = emit_gla_tile(C, i, need_q=False)
        emit_gla_dS(C, r, 4, 5)
        dst = C.sm[:, 1024 + (i % 2) * 256:1024 + (i % 2) * 256 + 256]
        kk = ("dSs", i % 2)
        for hh in range(2):
            psl = slice(hh * 64, (hh + 1) * 64)
            _cp(C, "dve", dst[psl, :], C.ps[4 + hh][psl, 0:256], reads=[psk(4 + hh)], writes=[kk])
        P.dma("sp", dS_o[i], dst, key=("dSo", i % 2), reads=[kk])
        dcs = C.sm[:, 1536 + 2 * i:1538 + 2 * i]
        _cp(C, "dve", dcs, r["dcol"], reads=[WK(0)], writes=["dcs"])
    P.dma("sp", dc_o, C.sm[:, 1536:1552], key="dco", reads=["dcs"])


def emit_mixer(C, wl, KT_all, V_all, lf_all, lfown_d, dS_all, dc_all):
    P = C.P
    Wv = C.W
    accb = C.acc.bitcast(BF16)
    oT = accb[:, 0:8, :].rearrange("p a (b t) -> p (a b) t", b=2)
    rowf = lambda r: C.acc[:, 8 + r, :]
    rowb = lambda r: accb[:, 8 + r, :]
    P.fence()
    emit_gla_load_weights(C, wl)
    Sf = rowf(0)[:, 0:256]
    Sg = rowf(0)[:, 256:512]
    tmpS = rowf(0)[:, 512:768]
    Sbf_all = rowb(1)
    QT = rowb(2)[:, 0:1024]
    gb = C.acc[:, 11:13, :].rearrange("p a t -> p (a t)")
    Aq = rowf(5)
    r67 = C.acc[:, 14:16, :].rearrange("p a t -> p (a t)")
    C.msk = r67[:, 512:2048].bitcast(BF16).rearrange("p (a m t) -> p a m t", a=3, m=8)
    P.dma("sp", C.msk, C.msk_d, key="msk", writes=["msk"])
    dcg = C.sm[:, 1024:1024 + 128].rearrange("p (r i k) -> p r i k", r=8, i=8)
    P.dma("sp", C.sm[:, 1024:1024 + 128].rearrange("p (r x) -> p r x", r=8), dc_all.rearrange("r p x -> p r x"), key="dcg", writes=["dcg"])
    P.op("dve", lambda e: e.memset(Sf, 0.0), reads=[], writes=["Sf"])
    oms = C.sm[:, 1160:1168]
    _ts(C, "dve", oms, C.sel[:, :], -1.0, 1.0, ALU.mult, ALU.add, reads=["sel"], writes=["oms"])
    dm = C.sm[:, 1168:1170]
    for i in range(8 if "scan" in C.stages else 0):
        for m in range(8):
            P.dma("sp", gb[:, m * 256:(m + 1) * 256], dS_all[m, i], key=("dSl", m), writes=[("gb", m)])
        _cp(C, "dve", Sg, Sf, reads=["Sf"], writes=["Sg"])
        for m in range(8):
            _stt(C, "dve", dm, dcg[:, m, i, :], C.sel[:, m:m + 1], oms[:, m:m + 1].to_broadcast([128, 2]), ALU.mult, ALU.add,
                 reads=["dcg", "sel", "oms"], writes=["dm"])
            _ts(C, "dve", tmpS, gb[:, m * 256:(m + 1) * 256], C.sel[:, m:m + 1], None, ALU.mult, None,
                reads=[("gb", m), "sel"], writes=["tmpS"])
            for kc in range(2):
                ksl = slice(kc * 128, (kc + 1) * 128)
                _stt(C, "dve", Sg[:, ksl], Sg[:, ksl], dm[:, kc:kc + 1], tmpS[:, ksl], ALU.mult, ALU.add,
                     reads=["Sg", "dm", "tmpS"], writes=["Sg"])
                _stt(C, "dve", Sf[:, ksl], Sf[:, ksl], dcg[:, m, i, kc:kc + 1], gb[:, m * 256 + kc * 128:m * 256 + (kc + 1) * 128],
                     ALU.mult, ALU.add, reads=["Sf", "dcg", ("gb", m)], writes=["Sf"])
        _cp(C, "dve", Sbf_all[:, i * 256:(i + 1) * 256], Sg, reads=["Sg"], writes=[("Sbf", i)])
    ng = C.sm[:, 1170:1171]
    P.dma("sp", ng, wl["ng"], key="ng", writes=["ng"])
    for i in range(8 if "tiles" in C.stages else 0):
        r = emit_gla_tile(C, i, need_q=True)
        tsl = slice(i * 128, (i + 1) * 128)
        attm = wkb(C, 9, 0, 512)
        if "tiles1" in C.stages:
            continue
        ab = lambda hh: 4 if hh == 0 else 7
        ob = lambda hh: 5 if hh == 0 else 6
        for hd in range(4):
            kc, hh = hd // 2, hd % 2
            psl = slice(hh * 64, (hh + 1) * 64)
            _mm(C, C.ps[ab(hh)][:, hd * 128:(hd + 1) * 128], r["ktil"][psl, kc * 128:(kc + 1) * 128], r["qtil"][psl, kc * 128:(kc + 1) * 128],
                True, True, reads=[WK(8)], writes=[psk(ab(hh))])
        for hd in range(4):
            hh = hd % 2
            _tt(C, "dve", attm[:, hd * 128:(hd + 1) * 128], C.ps[ab(hh)][:, hd * 128:(hd + 1) * 128], C.cst[:, CI_MASKINC01, :], ALU.mult,
                reads=[psk(ab(hh)), "cst"], writes=[WK(9)])
        if "t_att" in C.stages:
            continue
        for hd in range(4):
            kc, hh = hd // 2, hd % 2
            psl = slice(hh * 64, (hh + 1) * 64)
            _mm(C, C.ps[ob(hh)][:, hd * 128:(hd + 1) * 128], r["vtm"][:, hd * 128:(hd + 1) * 128], attm[:, hd * 128:(hd + 1) * 128],
                True, False, reads=[WK(5), WK(9)], writes=[psk(ob(hh))])
            _mm(C, C.ps[ob(hh)][:, hd * 128:(hd + 1) * 128], Sbf_all[psl, i * 256 + kc * 128:i * 256 + (kc + 1) * 128],
                r["qtil"][psl, kc * 128:(kc + 1) * 128], False, True, reads=[("Sbf", i), WK(8)], writes=[psk(ob(hh))])
        if "t_o" in C.stages:
            continue
        sq = wk_tile(C, 6)
        for hd in range(4):
            hh = hd % 2
            _act(C, sq[:, hd * 128:(hd + 1) * 128], C.ps[ob(hh)][:, hd * 128:(hd + 1) * 128], AF.Square, reads=[psk(ob(hh))], writes=[WK(6)])
        _mm(C, C.ps[3][:, :], C.cst[:, CI_ONES128, :], sq, True, True, reads=["cst", WK(6)], writes=[psk(3)])
        rs = wk_tile(C, 7)
        _ts(C, "dve", rs, C.ps[3][:, :], LN_EPS, None, ALU.add, None, reads=[psk(3)], writes=[WK(7)])
        _act(C, rs, rs, AF.Sqrt, reads=[WK(7)], writes=[WK(7)])
        P.op("dve", lambda e, a=rs: e.reciprocal(a, a), reads=[WK(7)], writes=[WK(7)])
        for hd in range(4):
            hh = hd % 2
            _tt(C, "dve", rs[:, hd * 128:(hd + 1) * 128], C.ps[ob(hh)][:, hd * 128:(hd + 1) * 128], rs[:, hd * 128:(hd + 1) * 128], ALU.mult,
                reads=[psk(ob(hh)), WK(7)], writes=[WK(7)])
        for hd in range(4):
            _ts(C, "dve", oT[:, hd, tsl], rs[:, hd * 128:(hd + 1) * 128], ng, None, ALU.mult, None,
                reads=[WK(7), "ng"], writes=[("oTg", hd, i)])
    P.fence()
    wslot = [Wv[:, s * 2048:(s + 1) * 2048].rearrange("p (c f) -> p c f", c=16) for s in range(10)]
    for hd in range(4 if "gr" in C.stages else 0):
        sl = hd % 2
        P.dma("pool", wslot[sl], wl["w_gr"][hd], key=("Wm", sl), writes=[("W", sl)])
        for h in range(2):
            pb = (2 * hd + h) % 4
            hsl = slice(h * 512, (h + 1) * 512)
            for c in range(16):
                _mm(C, C.ps[pb][:, :], wslot[sl][:, c, :], C.xbf[:, c, hsl], c == 0, c == 15, reads=[("W", sl), xbk(c, h)], writes=[psk(pb)])
            sg = wk_tile(C, pb)
            _act(C, sg, C.ps[pb][:, :], AF.Silu, reads=[psk(pb)], writes=[WK(pb)])
            _tt(C, "dve", oT[:, hd, hsl], oT[:, hd, hsl], sg, ALU.mult, reads=[WK(pb), ("oTh", hd, h)], writes=[("oTh", hd, h)])

    if "heads" not in C.stages:
        return oT
    lfn4 = C.sm[:, 568:568 + 384].rearrange("p (i r h) -> p i r h", i=8, r=8)
    for r_ in range(8):
        P.dma("sp", lfn4[:, :, r_, :], lf_all[r_].rearrange("p (i h) -> p i h", h=6), key=("lfn", r_), writes=["lfn"])
    lff = C.sm[:, 568:568 + 384]
    ea = r67[:, 0:384]
    eb = wk_tile(C, 9)[:, 0:384]
    _cp(C, "dve", ea, lff, reads=["lfn"], writes=["ea"])
    src, dst, ks, kd = ea, eb, "ea", WK(9)
    sh = 1
    while sh < 64:
        n = sh * 6
        _cp(C, "dve", dst[:, 0:n], src[:, 0:n], reads=[ks], writes=[kd])
        _tt(C, "dve", dst[:, n:384], src[:, n:384], src[:, 0:384 - n], ALU.add, reads=[ks], writes=[kd])
        src, dst, ks, kd = dst, src, kd, ks
        sh *= 2
    _tt(C, "dve", dst, src, lff, ALU.subtract, reads=[ks, "lfn"], writes=[kd])
    E, kE = dst, kd
    _mm(C, C.ps[0][:, 0:384], C.cst[:, 6, :], lff, True, False, reads=["cst", "lfn"], writes=[psk(0)])
    _mm(C, C.ps[0][:, 0:384], C.cst[:, 7, :], E, False, True, reads=["cst", kE], writes=[psk(0)])
    Eown = C.sm[:, 1224:1272].rearrange("p (i h) -> p i h", h=6)
    E4 = E.rearrange("p (i r h) -> p i r h", i=8, r=8)
    _cp(C, "dve", Eown, E4[:, :, 0, :], reads=[kE], writes=["Eown"])
    for m in range(8):
        _stt(C, "dve", Eown, lfn4[:, :, m, :], C.sel[:, m:m + 1], Eown, ALU.mult, ALU.add, reads=["lfn", "sel", "Eown"], writes=["Eown"])
    nctm = C.sm[:, 1272:1272 + 384].rearrange("p (j h) -> p j h", h=6)
    _ts(C, "dve", C.sm[:, 1272:1272 + 384], C.ps[0][:, 0:384], -1.0, None, ALU.mult, None, reads=[psk(0)], writes=["nctm"])
    lfo = C.sm[:, 1176:1224].rearrange("p (i h) -> p i h", h=6)
    P.dma("sp", C.sm[:, 1176:1224], lfown_d, key="lfo", writes=["lfo"])

    P.fence()
    KTs = C.kv[:, 0:8192]
    Vs = C.kv[:, 8192:16384]
    for hh in range(12):
        is_sb = hh >= 6
        hd = hh % 6
        osl = 4 + hh
        P.dma("sp", KTs.rearrange("p (r t) -> p r t", r=8), KT_all[:, hh].rearrange("r d t -> d r t"), key="ktl", writes=["KTs"])
        P.dma("sp", Vs.rearrange("p (r x) -> p r x", r=8), V_all[:, hh].rearrange("r p i d -> p r (i d)"), key="vl", writes=["Vs"])
        sl = hh % 2
        P.dma("pool", wslot[sl], wl["w_qT"][hh], key=("Wm", sl), writes=[("W", sl)])
        for h in range(2):
            hsl = slice(h * 512, (h + 1) * 512)
            for c in range(16):
                _mm(C, C.ps[h][:, :], wslot[sl][:, c, :], C.xbf[:, c, hsl], c == 0, c == 15, reads=[("W", sl), xbk(c, h)], writes=[psk(h)])
            _act(C, QT[:, hsl], C.ps[h][:, :], AF.Identity, reads=[psk(h)], writes=[("QT", h)], scale=128 ** -0.5)
        if not is_sb:
            for i in range(8):
                lb_ = wkv(C, 4 + i % 2, 0, 128)
                eb_ = wkv(C, 4 + i % 2, 128, 256)
                _cp(C, "dve", lb_, lfo[:, i, hd:hd + 1].to_broadcast([128, 128]), reads=["lfo"], writes=[WK(4 + i % 2)])
                _cp(C, "dve", eb_, Eown[:, i, hd:hd + 1].to_broadcast([128, 128]), reads=["Eown"], writes=[WK(4 + i % 2)])
                pa = 2 + i // 4
                osl_ = slice((i % 4) * 128, (i % 4 + 1) * 128)
                _mm(C, C.ps[pa][:, osl_], lb_, C.cst[:, 6, :], True, False, reads=[WK(4 + i % 2), "cst"], writes=[psk(pa)])
                _mm(C, C.ps[pa][:, osl_], eb_, C.cst[:, 7, :], False, True, reads=[WK(4 + i % 2), "cst"], writes=[psk(pa)])
            for h in range(2):
                _cp(C, "act", Aq[:, h * 512:(h + 1) * 512], C.ps[2 + h][:, :], reads=[psk(2 + h)], writes=[("Aq", h)])
        for ch in range(2):
            lo = ch * 512
            jmax = 32 if ch == 0 else 64
            steps = []
            for j in range(jmax):
                cs = max(lo, 128 * (j // 8))
                steps.append((j, cs, lo + 512, 128 * (j // 8) >= lo))
            if is_sb:
                steps = steps[::-1]
            po, pl = 6, 7
            _mm(C, C.ps[po][:, :], C.cbf[:, BI_ZERO, :], QT[:, lo:lo + 512], True, False, reads=["cbf", ("QT", ch)], writes=[psk(po)], skip=True)
            _mm(C, C.ps[pl][:, :], C.cbf[:, BI_ZERO, :], QT[:, lo:lo + 512], True, False, reads=["cbf", ("QT", ch)], writes=[psk(pl)], skip=True)
            for si, (j, cs, ce, diag) in enumerate(steps):
                r_, i_ = j % 8, j // 8
                m = r_
                n = ce - cs
                kt = KTs[:, r_ * 1024 + i_ * 128: r_ * 1024 + (i_ + 1) * 128]
                vt = Vs[:, r_ * 1024 + i_ * 128: r_ * 1024 + (i_ + 1) * 128]
                pz = si % 2
                w0 = 4 * (si % 2)
                hb = si % 2
                cl = slice(cs - lo, ce - lo)
                last = si == len(steps) - 1
                _mm(C, C.ps[pz][:, 0:n], kt, QT[:, cs:ce], True, True, reads=["KTs", ("QT", ch)], writes=[psk(pz)])
                if not is_sb:
                    tt_ = wk_tile(C, w0)[:, 0:n]
                    pt = wkb(C, 8, hb * 512, hb * 512 + n)
                    _tt(C, "dve", tt_, C.ps[pz][:, 0:n], Aq[:, cs:ce], ALU.add, reads=[psk(pz), ("Aq", ch)], writes=[WK(w0)])
                    if diag:
                        _tt(C, "pool", tt_[:, 0:128], tt_[:, 0:128], C.msk[:, 0, m, :], ALU.add, reads=[WK(w0), "msk"], writes=[WK(w0)])
                    _act(C, pt, tt_, AF.Exp, reads=[WK(w0), "nctm"], writes=[("wkh", 8, hb)], bias=nctm[:, j, hd:hd + 1])
                    _mm(C, C.ps[po][:, cl], vt, pt, False, last, reads=["Vs", ("wkh", 8, hb)], writes=[psk(po)], skip=True)
                    _mm(C, C.ps[pl][:, cl], C.cbf[:, BI_ONES, :], pt, False, last, reads=["cbf", ("wkh", 8, hb)], writes=[psk(pl)], skip=True)
                else:
                    e_ = wk_tile(C, w0)[:, 0:n]
                    sp_ = wk_tile(C, w0 + 1)[:, 0:n]
                    zl = wk_tile(C, w0 + 2)[:, 0:n]
                    t1 = wk_tile(C, w0 + 3)[:, 0:n]
                    lb = wkb(C, 8, hb * 512, hb * 512 + n)
                    a_ = wkb(C, 9, hb * 512, hb * 512 + n)
                    _act(C, e_, C.ps[pz][:, 0:n], AF.Exp, reads=[psk(pz)], writes=[WK(w0)])
                    _act(C, sp_, e_, AF.Ln, reads=[WK(w0)], writes=[WK(w0 + 1)], bias=1.0)
                    _ts(C, "pool", lb, sp_, -1.0, None, ALU.mult, None, reads=[WK(w0 + 1)], writes=[("wkh", 8, hb)])
                    _tt(C, "dve", zl, C.ps[pz][:, 0:n], sp_, ALU.subtract, reads=[psk(pz), WK(w0 + 1)], writes=[WK(w0 + 2)])
                    if diag:
                        _tt(C, "pool", lb[:, 0:128], lb[:, 0:128], C.msk[:, 2, m, :], ALU.mult, reads=[("wkh", 8, hb), "msk"], writes=[("wkh", 8, hb)])
                        _tt(C, "pool", zl[:, 0:128], zl[:, 0:128], C.msk[:, 1, m, :], ALU.add, reads=[WK(w0 + 2), "msk"], writes=[WK(w0 + 2)])
                    pr = 2 + si % 2
                    _mm(C, C.ps[pr][:, 0:n], C.cbf[:, BI_MSTRICT, :], lb, True, True, reads=["cbf", ("wkh", 8, hb)], writes=[psk(pr)])
                    _tt(C, "dve", t1, C.ps[pr][:, 0:n], zl, ALU.add, reads=[psk(pr), WK(w0 + 2)], writes=[WK(w0 + 3)])
                    _tt(C, "dve", t1, C.ps[pl][:, cl], t1, ALU.add, reads=[psk(pl), WK(w0 + 3)], writes=[WK(w0 + 3)])
                    _mm(C, C.ps[pl][:, cl], C.cbf[:, BI_ONES, :], lb, False, last, reads=["cbf", ("wkh", 8, hb)], writes=[psk(pl)], skip=True)
                    _act(C, a_, t1, AF.Exp, reads=[WK(w0 + 3)], writes=[("wkh", 9, hb)])
                    _mm(C, C.ps[po][:, cl], vt, a_, False, last, reads=["Vs", ("wkh", 9, hb)], writes=[psk(po)], skip=True)
            if not is_sb:
                rl = wk_tile(C, 3)
                P.op("dve", lambda e, a=rl, b=C.ps[pl][:, :]: e.reciprocal(a, b), reads=[psk(pl)], writes=[WK(3)])
                _tt(C, "dve", oT[:, osl, lo:lo + 512], C.ps[po][:, :], rl, ALU.mult, reads=[psk(po), WK(3)], writes=[("oTh", osl, ch)])
            else:
                _cp(C, "dve", oT[:, osl, lo:lo + 512], C.ps[po][:, :], reads=[psk(po)], writes=[("oTh", osl, ch)])
    return oT


def emit_merge_out(C, wl, oT, x1_d):
    P = C.P
    Wv = C.W
    P.fence()
    mT = C.kv.rearrange("p (c t) -> p c t", c=16)
    wgate = [Wv[:, s * 6144:(s + 1) * 6144].rearrange("p (b c f) -> p b c f", b=3, c=16) for s in range(2)]
    wbr = [Wv[:, 12288 + s * 2048:12288 + (s + 1) * 2048].rearrange("p (c f) -> p c f", c=16) for s in range(2)]
    wout = [Wv[:, 16384 + s * 2048:16384 + (s + 1) * 2048].rearrange("p (c f) -> p c f", c=16) for s in range(2)]
    cnt = 0
    for dc in range(16):
        s = dc % 2
        kg = [("W", 3 * s + q) for q in range(3)]
        P.dma("pool", wgate[s], wl["w_gate"][dc], key=("wg", s), writes=kg)
        P.dma("pool", wbr[s], wl["w_br"][dc], key=("wb", s), writes=[("W", 6 + s)])
        for h in range(2):
            hsl = slice(h * 512, (h + 1) * 512)
            q = cnt % 2
            cnt += 1
            gb0 = 3 * q
            for b in range(3):
                for c in range(16):
                    _mm(C, C.ps[gb0 + b][:, :], wgate[s][:, b, c, :], C.xbf[:, c, hsl], c == 0, c == 15,
                        reads=kg + [xbk(c, h)], writes=[psk(gb0 + b)])
                _act(C, wk_tile(C, gb0 + b), C.ps[gb0 + b][:, :], AF.Sigmoid, reads=[psk(gb0 + b)], writes=[WK(gb0 + b)])
            tA = wk_tile(C, 6 + 2 * q)
            tB = wk_tile(C, 7 + 2 * q)
            branches = [(0, range(0, 4)), (1, range(4, 10)), (2, range(10, 16))]
            for b, chs in branches:
                pb = 6 + (b % 2)
                chs = list(chs)
                for ci, ch_ in enumerate(chs):
                    _mm(C, C.ps[pb][:, :], wbr[s][:, ch_, :], oT[:, ch_, hsl], ci == 0, ci == len(chs) - 1,
                        reads=[("W", 6 + s), ("oTh", ch_, h)], writes=[psk(pb)])
                if b == 0:
                    _tt(C, "dve", tA, C.ps[pb][:, :], wk_tile(C, gb0 + b), ALU.mult, reads=[psk(pb), WK(gb0 + b)], writes=[WK(6 + 2 * q)])
                elif b == 1:
                    _tt(C, "dve", tB, C.ps[pb][:, :], wk_tile(C, gb0 + b), ALU.mult, reads=[psk(pb), WK(gb0 + b)], writes=[WK(7 + 2 * q)])
                    _tt(C, "dve", tA, tA, tB, ALU.add, reads=[WK(6 + 2 * q), WK(7 + 2 * q)], writes=[WK(6 + 2 * q)])
                else:
                    _tt(C, "dve", tB, C.ps[pb][:, :], wk_tile(C, gb0 + b), ALU.mult, reads=[psk(pb), WK(gb0 + b)], writes=[WK(7 + 2 * q)])
                    _tt(C, "dve", mT[:, dc, hsl], tA, tB, ALU.add, reads=[WK(6 + 2 * q), WK(7 + 2 * q)], writes=[("mT", dc, h)])
    P.fence()
    for dc in range(16):
        P.dma("sp", C.acc[:, dc, :], x1_d[dc * 128:(dc + 1) * 128, :], key=("x1l", dc), writes=[acck(dc, 0), acck(dc, 1)])
    cnt = 0
    for dc in range(16):
        s = dc % 2
        P.dma("pool", wout[s], wl["w_out"][dc], key=("wo", s), writes=[("W", 8 + s)])
        for h in range(2):
            hsl = slice(h * 512, (h + 1) * 512)
            pb = cnt % 4
            cnt += 1
            for c in range(16):
                _mm(C, C.ps[pb][:, :], wout[s][:, c, :], mT[:, c, hsl], c == 0, c == 15, reads=[("W", 8 + s), ("mT", c, h)], writes=[psk(pb)])
            _stt(C, "dve", C.acc[:, dc, hsl], C.acc[:, dc, hsl], ALPHA, C.ps[pb][:, :], ALU.mult, ALU.add,
                 reads=[acck(dc, h), psk(pb)], writes=[acck(dc, h)])
    P.fence()


W_A = dict(f_wgu=[NFC, 128, 2, 16, 128], f_wd=[NG, 2, 128, 4, 1024], w_kT=[12, 128, 16, 128], w_v=[3, 128, 16, 512],
           w_ff=[128, 16, 6], bf=[1, 6], w_gfm=[128, 16, 528], w_gtm=[128, 16, 512], wa2=[16, 256], ba=[1, 256])
W_B = dict(f_wgu=[NFC, 128, 2, 16, 128], f_wd=[NG, 2, 128, 4, 1024], w_qT=[12, 128, 16, 128], w_gr=[4, 128, 16, 128],
           w_gfm=[128, 16, 528], w_gtm=[128, 16, 512], wa2=[16, 256], ba=[1, 256], ng=[128, 1],
           w_gate=[16, 128, 3, 16, 128], w_br=[16, 128, 16, 128], w_out=[16, 128, 16, 128])


def _decl_common(nc):
    d = {}
    d["cst"] = nc.dram_tensor("cst", [128, 8, 128], F32, kind="ExternalInput").ap()
    d["cbf"] = nc.dram_tensor("cbf", [128, 4, 128], BF16, kind="ExternalInput").ap()
    d["lnp"] = nc.dram_tensor("lnp", [128, 192], F32, kind="ExternalInput").ap()
    d["msk"] = nc.dram_tensor("msk", [128, 3, 8, 128], BF16, kind="ExternalInput").ap()
    d["sel"] = nc.dram_tensor("sel", [128, 8], F32, kind="ExternalInput").ap()
    return d


def build_A(layer):
    nc = bass.Bass("TRN2", target_bir_lowering=False)
    d = _decl_common(nc)
    xT = nc.dram_tensor("xT", [DM, T], F32, kind="ExternalInput").ap()
    wl = {k: nc.dram_tensor(k, shp, F32, kind="ExternalInput").ap() for k, shp in W_A.items()}
    x1_o = nc.dram_tensor("x1_o", [DM, T], F32, kind="ExternalOutput").ap()
    KT_o = nc.dram_tensor("KT_o", [12, 128, T], BF16, kind="ExternalOutput").ap()
    V_o = nc.dram_tensor("V_o", [12, 128, 8, 128], BF16, kind="ExternalOutput").ap()
    lf_o = nc.dram_tensor("lf_o", [128, 48], F32, kind="ExternalOutput").ap()
    dS_o = nc.dram_tensor("dS_o", [8, 128, 256], F32, kind="ExternalOutput").ap()
    dc_o = nc.dram_tensor("dc_o", [128, 16], F32, kind="ExternalOutput").ap()
    C = setup_ctx(nc)
    P = C.P
    C.msk_d = d["msk"]
    emit_consts(C, d["cst"], d["cbf"], d["lnp"], d["msk"], d["sel"])
    xv = xT.rearrange("(c p) t -> p c t", p=128)
    for q in range(4):
        ks = [acck(c, h) for c in range(4 * q, 4 * q + 4) for h in range(2)]
        P.dma("sp", C.acc[:, 4 * q:4 * q + 4, :], xv[:, 4 * q:4 * q + 4, :], key=("xl", q), writes=ks)
        kb = [xbk(c, h) for c in range(4 * q, 4 * q + 4) for h in range(2)]
        P.dma("pool", C.xbf[:, 4 * q:4 * q + 4, :], xv[:, 4 * q:4 * q + 4, :], key=("xlb", q), writes=kb)
    emit_ffn(C, wl["f_wgu"], wl["f_wd"])
    emit_ln(C, layer * 3 + 0)
    xo = x1_o.rearrange("(c p) t -> p c t", p=128)
    for q in range(4):
        ks = [acck(c, h) for c in range(4 * q, 4 * q + 4) for h in range(2)]
        P.dma("sp", xo[:, 4 * q:4 * q + 4, :], C.acc[:, 4 * q:4 * q + 4, :], key=("xo", q), reads=ks)
    emit_phaseA_proj(C, wl, KT_o, V_o, lf_o, dS_o, dc_o)
    P.emit()
    return nc


class LazyW(dict):
    def __init__(self, nc, shapes):
        super().__init__()
        self.nc = nc
        self.shapes = shapes

    def __missing__(self, k):
        t = self.nc.dram_tensor(k, self.shapes[k], F32, kind="ExternalInput").ap()
        self[k] = t
        return t


def build_B(layer, dbg=False, stages=("gla", "scan", "tiles", "gr", "heads", "merge", "ffn")):
    nc = bass.Bass("TRN2", target_bir_lowering=False)
    d = _decl_common(nc)
    x1 = nc.dram_tensor("x1", [DM, T], F32, kind="ExternalInput").ap()
    wl = LazyW(nc, W_B)
    KT_all = V_all = lf_all = lf_own = None
    if "heads" in stages:
        KT_all = nc.dram_tensor("KT_all", [8, 12, 128, T], BF16, kind="ExternalInput").ap()
        V_all = nc.dram_tensor("V_all", [8, 12, 128, 8, 128], BF16, kind="ExternalInput").ap()
        lf_all = nc.dram_tensor("lf_all", [8, 128, 48], F32, kind="ExternalInput").ap()
        lf_own = nc.dram_tensor("lf_own", [128, 48], F32, kind="ExternalInput").ap()
    dS_all = nc.dram_tensor("dS_all", [8, 8, 128, 256], F32, kind="ExternalInput").ap()
    dc_all = nc.dram_tensor("dc_all", [8, 128, 16], F32, kind="ExternalInput").ap()
    x3_o = nc.dram_tensor("x3_o", [DM, T], F32, kind="ExternalOutput").ap()
    C = setup_ctx(nc)
    C.stages = stages
    P = C.P
    C.msk_d = d["msk"]
    emit_consts(C, d["cst"], d["cbf"], d["lnp"], d["msk"], d["sel"])
    xv = x1.rearrange("(c p) t -> p c t", p=128)
    for q in range(4):
        kb = [xbk(c, h) for c in range(4 * q, 4 * q + 4) for h in range(2)]
        P.dma("pool", C.xbf[:, 4 * q:4 * q + 4, :], xv[:, 4 * q:4 * q + 4, :], key=("xlb", q), writes=kb)
    oT = emit_mixer(C, wl, KT_all, V_all, lf_all, lf_own, dS_all, dc_all)
    if dbg:
        P.fence()
        dbg_oT = nc.dram_tensor("dbg_oT", [128, 16, T], BF16, kind="ExternalOutput").ap()
        P.dma("sp", dbg_oT, oT, key="dbg1", reads=[("oTh", c, h) for c in range(16) for h in range(2)])
        P.fence()
    if "merge" in stages:
        emit_merge_out(C, wl, oT, x1)
    if dbg:
        dbg_y = nc.dram_tensor("dbg_y", [128, 16, T], F32, kind="ExternalOutput").ap()
        P.dma("sp", dbg_y, C.acc[:, :, :], key="dbg2", reads=[acck(c, h) for c in range(16) for h in range(2)])
        P.fence()
    if "ffn" in stages:
        emit_ln(C, layer * 3 + 1)
        P.fence()
        emit_ffn(C, wl["f_wgu"], wl["f_wd"])
        emit_ln(C, layer * 3 + 2, want_bf=False)
    P.fence()
    xo = x3_o.rearrange("(c p) t -> p c t", p=128)
    for q in range(4):
        ks = [acck(c, h) for c in range(4 * q, 4 * q + 4) for h in range(2)]
        P.dma("sp", xo[:, 4 * q:4 * q + 4, :], C.acc[:, 4 * q:4 * q + 4, :], key=("xo", q), reads=ks)
    P.emit()
    return nc


RG = [list(range(NCORES))]
TINY_W = ("bf", "ba", "ng", "wa2")


def w2d(k, shp):
    nm = 3 if k == "f_wd" else (2 if len(shp) >= 4 else 1)
    rows = int(np.prod(shp[:nm]))
    cols = int(np.prod(shp[nm:]))
    assert rows % NCORES == 0
    return rows, cols


def build_fused():
    nc = bass.Bass("TRN2", target_bir_lowering=False)
    d = _decl_common(nc)
    xT = nc.dram_tensor("xT", [DM, T], F32, kind="ExternalInput").ap()
    x_o = nc.dram_tensor("x_o", [DM, T], F32, kind="ExternalOutput").ap()
    C = setup_ctx(nc)
    C.stages = ("gla", "scan", "tiles", "gr", "heads", "merge", "ffn")
    P = C.P
    C.msk_d = d["msk"]
    emit_consts(C, d["cst"], d["cbf"], d["lnp"], d["msk"], d["sel"])
    xv = xT.rearrange("(c p) t -> p c t", p=128)
    for q in range(4):
        ks = [acck(c, h) for c in range(4 * q, 4 * q + 4) for h in range(2)]
        P.dma("sp", C.acc[:, 4 * q:4 * q + 4, :], xv[:, 4 * q:4 * q + 4, :], key=("xl", q), writes=ks)
        kb = [xbk(c, h) for c in range(4 * q, 4 * q + 4) for h in range(2)]
        P.dma("pool", C.xbf[:, 4 * q:4 * q + 4, :], xv[:, 4 * q:4 * q + 4, :], key=("xlb", q), writes=kb)
    wsets = []
    pend = []
    for l in range(DEPTH):
        for pre, shapes in (("a%d_" % l, W_A), ("b%d_" % l, W_B)):
            ws = {}
            for k, shp in shapes.items():
                if k in TINY_W:
                    ws[k] = nc.dram_tensor(pre + k, shp, F32, kind="ExternalInput").ap()
                    continue
                rows, cols = w2d(k, shp)
                sh = nc.dram_tensor(pre + k, [rows // NCORES, cols], F32, kind="ExternalInput").ap()
                loc = nc.dram_tensor(pre + k + "_s", [rows // NCORES, cols], F32).ap()
                full = nc.dram_tensor(pre + k + "_g", [rows, cols], F32)
                P.dma("sp", loc, sh, key="wcp", writes=[("wsh", len(pend))])
                pend.append((pre + k, loc, full))
                ws[k] = full.reshape(list(shp)).ap()
            wsets.append(ws)
    allsh = [("wsh", i) for i in range(len(pend))]
    for nm, loc, full in pend:
        P.op("pool", lambda e, s_=loc, d_=full.ap(): e.collective_compute("AllGather", ALU.bypass, replica_groups=RG, ins=[s_.opt()], outs=[d_.opt()]),
             reads=allsh, writes=[("ccw", nm)], dma_key="ccw_sem")
        P.dram_dep[nm + "_g"] = ("ccw", nm)
    for l in range(DEPTH):
        wa = wsets[2 * l]
        wb = wsets[2 * l + 1]
        x1_d = nc.dram_tensor("x1_d%d" % l, [DM, T], F32).ap()
        KT_o = nc.dram_tensor("KT_o%d" % l, [12 * 128, T], BF16).ap()
        V_o = nc.dram_tensor("V_o%d" % l, [12 * 128, 8 * 128], BF16).ap()
        lf_o = nc.dram_tensor("lf_o%d" % l, [128, 48], F32).ap()
        dS_o = nc.dram_tensor("dS_o%d" % l, [8 * 128, 256], F32).ap()
        dc_o = nc.dram_tensor("dc_o%d" % l, [128, 16], F32).ap()
        KT_g = nc.dram_tensor("KT_g%d" % l, [8 * 12 * 128, T], BF16).ap()
        V_g = nc.dram_tensor("V_g%d" % l, [8 * 12 * 128, 8 * 128], BF16).ap()
        lf_g = nc.dram_tensor("lf_g%d" % l, [8 * 128, 48], F32).ap()
        dS_g = nc.dram_tensor("dS_g%d" % l, [8 * 8 * 128, 256], F32).ap()
        dc_g = nc.dram_tensor("dc_g%d" % l, [8 * 128, 16], F32).ap()
        if l > 0:
            P.fence()
        emit_ffn(C, wa["f_wgu"], wa["f_wd"])
        emit_ln(C, l * 3 + 0)
        xo = x1_d.rearrange("(c p) t -> p c t", p=128)
        for q in range(4):
            ks = [acck(c, h) for c in range(4 * q, 4 * q + 4) for h in range(2)]
            P.dma("sp", xo[:, 4 * q:4 * q + 4, :], C.acc[:, 4 * q:4 * q + 4, :], key=("xo", q), reads=ks)
        emit_phaseA_proj(C, wa, KT_o.rearrange("(h p) t -> h p t", p=128), V_o.rearrange("(h p) (i d) -> h p i d", p=128, d=128),
                         lf_o, dS_o.rearrange("(i p) x -> i p x", p=128), dc_o)
        P.fence()
        for nm, src, dst in (("KT", KT_o, KT_g), ("V", V_o, V_g), ("lf", lf_o, lf_g), ("dS", dS_o, dS_g), ("dc", dc_o, dc_g)):
            P.op("pool", lambda e, s_=src, d_=dst: e.collective_compute("AllGather", ALU.bypass, replica_groups=RG, ins=[s_.opt()], outs=[d_.opt()]),
                 reads=[], writes=[("cc", nm)], dma_key="cc_sem")
        oT = emit_mixer(C, wb, KT_g.rearrange("(r h p) t -> r h p t", r=8, p=128),
                        V_g.rearrange("(r h p) (i d) -> r h p i d", r=8, p=128, d=128),
                        lf_g.rearrange("(r p) x -> r p x", p=128), lf_o,
                        dS_g.rearrange("(r i p) x -> r i p x", r=8, p=128), dc_g.rearrange("(r p) x -> r p x", p=128))
        emit_merge_out(C, wb, oT, x1_d)
        emit_ln(C, l * 3 + 1)
        P.fence()
        emit_ffn(C, wb["f_wgu"], wb["f_wd"])
        emit_ln(C, l * 3 + 2, want_bf=(l < DEPTH - 1))
    P.fence()
    xo = x_o.rearrange("(c p) t -> p c t", p=128)
    for q in range(4):
        ks = [acck(c, h) for c in range(4 * q, 4 * q + 4) for h in range(2)]
        P.dma("sp", xo[:, 4 * q:4 * q + 4, :], C.acc[:, 4 * q:4 * q + 4, :], key=("xo", q), reads=ks)
    P.emit()
    return nc


def _tile_fm(w):
    n = w.shape[1] // 128
    return np.ascontiguousarray(w.reshape(16, 128, n, 128).transpose(2, 1, 0, 3))


def _tile_tm(w):
    return np.ascontiguousarray(w.reshape(16, 128, w.shape[1]).transpose(1, 0, 2))


def prep_ffn(wg, wu, wd):
    g = wg.reshape(16, 128, NFC, 128).transpose(2, 1, 0, 3)
    u = wu.reshape(16, 128, NFC, 128).transpose(2, 1, 0, 3)
    wgu = np.ascontiguousarray(np.stack([g, u], axis=2))
    d = wd.reshape(NG, 4, 128, 2, 1024).transpose(0, 3, 2, 1, 4)
    return wgu, np.ascontiguousarray(d)


def prep_layer(inp, l):
    w_in = inp["w_in"][l]
    o = {}
    o["f1_wgu"], o["f1_wd"] = prep_ffn(inp["ffn1_w_gate"][l], inp["ffn1_w_up"][l], inp["ffn1_w_down"][l])
    o["f2_wgu"], o["f2_wd"] = prep_ffn(inp["ffn2_w_gate"][l], inp["ffn2_w_up"][l], inp["ffn2_w_down"][l])
    o["w_kT"] = _tile_fm(np.concatenate([w_in[:, O_FK:O_FK + 768], w_in[:, O_SK:O_SK + 768]], axis=1))
    o["w_qT"] = _tile_fm(np.concatenate([w_in[:, O_FQ:O_FQ + 768], w_in[:, O_SQ:O_SQ + 768]], axis=1))
    wv = np.concatenate([w_in[:, O_FV:O_FV + 768], w_in[:, O_SV:O_SV + 768]], axis=1)
    o["w_v"] = np.ascontiguousarray(wv.reshape(16, 128, 3, 512).transpose(2, 1, 0, 3))
    o["w_ff"] = _tile_tm(w_in[:, O_FF:O_FF + 6])
    o["bf"] = np.ascontiguousarray(inp["fox_b_f"][l].reshape(1, 6))
    o["w_gfm"] = _tile_tm(np.concatenate([w_in[:, O_GQ:O_GQ + 256], w_in[:, O_GK:O_GK + 256], w_in[:, O_GA:O_GA + 16]], axis=1))
    o["w_gtm"] = _tile_tm(w_in[:, O_GV:O_GV + 512])
    o["wa2"] = np.ascontiguousarray(inp["gla_w_a2"][l])
    o["ba"] = np.ascontiguousarray(inp["gla_b_a"][l].reshape(1, 256))
    o["ng"] = np.ascontiguousarray(inp["gla_norm_g"][l].reshape(128, 1))
    o["w_gr"] = _tile_fm(w_in[:, O_GR:O_GR + 512])
    gts = w_in[:, O_GATE:O_GATE + 3 * DM].reshape(16, 128, 3, 16, 128)
    o["w_gate"] = np.ascontiguousarray(gts.transpose(3, 1, 2, 0, 4))
    wbr = np.concatenate([inp["w_br_gla"][l], inp["w_br_fox"][l], inp["w_br_sb"][l]], axis=0)
    o["w_br"] = _tile_fm(wbr)
    o["w_out"] = _tile_fm(inp["w_out"][l])
    return o


def prep_lnp(inp):
    arr = np.zeros((128, 2 * 3 * 2 * 16), np.float32)
    for l in range(DEPTH):
        for wi, nm in enumerate(["ln1", "ln2", "ln3"]):
            li = l * 3 + wi
            arr[:, (li * 2 + 0) * 16:(li * 2 + 1) * 16] = inp[nm + "_g"][l].reshape(16, 128).T
            arr[:, (li * 2 + 1) * 16:(li * 2 + 2) * 16] = inp[nm + "_b"][l].reshape(16, 128).T
    return arr


_PROG = {}


FUSED = True


def kernel(**inputs):
    inp = {k: np.asarray(v) for k, v in inputs.items()}
    x = inp["x"][0]
    xt = x.reshape(8, 8, 128, DM)
    lnp = prep_lnp(inp)
    common = []
    for c in range(NCORES):
        cst, cbf, msk, sel = host_consts(c)
        common.append(dict(cst=cst, cbf=cbf, lnp=lnp, msk=msk, sel=sel))
    cur = [np.ascontiguousarray(xt[:, c].reshape(T, DM).T) for c in range(NCORES)]
    cores = list(range(NCORES))
    if FUSED:
        if "F" not in _PROG:
            _PROG["F"] = build_fused()
        wall = {}
        for l in range(DEPTH):
            wl = prep_layer(inp, l)
            for k in W_A:
                src = {"f_wgu": "f1_wgu", "f_wd": "f1_wd"}.get(k, k)
                wall["a%d_%s" % (l, k)] = wl[src]
            for k in W_B:
                src = {"f_wgu": "f2_wgu", "f_wd": "f2_wd"}.get(k, k)
                wall["b%d_%s" % (l, k)] = wl[src]
        in_maps = []
        for c in cores:
            m = dict(common[c], xT=cur[c])
            for nm, arr in wall.items():
                k = nm.split("_", 1)[1]
                if k in TINY_W:
                    m[nm] = arr
                else:
                    rows, cols = w2d(k, arr.shape)
                    m[nm] = arr.reshape(rows, cols)[c * (rows // NCORES):(c + 1) * (rows // NCORES)]
            in_maps.append(m)
        res = run_bass_kernel_spmd(_PROG["F"], in_maps, core_ids=cores).results
        cur = [r["x_o"] for r in res]
    else:
        for l in range(DEPTH):
            wl = prep_layer(inp, l)
            if ("A", l) not in _PROG:
                _PROG[("A", l)] = build_A(l)
            wa = {k: wl[k] for k in W_A if not k.startswith("f_")}
            wa["f_wgu"], wa["f_wd"] = wl["f1_wgu"], wl["f1_wd"]
            in_maps = [dict(common[c], xT=cur[c], **wa) for c in cores]
            resA = run_bass_kernel_spmd(_PROG[("A", l)], in_maps, core_ids=cores).results
            KT_all = np.stack([r["KT_o"] for r in resA])
            V_all = np.stack([r["V_o"] for r in resA])
            lf_all = np.stack([r["lf_o"] for r in resA])
            dS_all = np.stack([r["dS_o"] for r in resA])
            dc_all = np.stack([r["dc_o"] for r in resA])
            if ("B", l) not in _PROG:
                _PROG[("B", l)] = build_B(l)
            wb = {k: wl[k] for k in W_B if not k.startswith("f_")}
            wb["f_wgu"], wb["f_wd"] = wl["f2_wgu"], wl["f2_wd"]
            in_maps = [dict(common[c], x1=resA[c]["x1_o"], KT_all=KT_all, V_all=V_all, lf_all=lf_all, lf_own=resA[c]["lf_o"],
                            dS_all=dS_all, dc_all=dc_all, **wb) for c in cores]
            resB = run_bass_kernel_spmd(_PROG[("B", l)], in_maps, core_ids=cores).results
            cur = [r["x3_o"] for r in resB]
    out = np.zeros((8, 8, 128, DM), np.float32)
    for c in range(NCORES):
        out[:, c] = cur[c].T.reshape(8, 128, DM)
    return out.reshape(1, S, DM)
```

```python
import numpy as np
import ml_dtypes
from contextlib import ExitStack
import concourse.bass as bass
import concourse.mybir as mybir
from concourse.bass_utils import run_bass_kernel_spmd

F32 = mybir.dt.float32
BF16 = mybir.dt.bfloat16
AF = mybir.ActivationFunctionType
ALU = mybir.AluOpType

NCORES = 8
DM = 2048
S = 8192
T = 1024
DEPTH = 2
DFF = 5632
NFC = 44
NG = 11
ALPHA = (2 * DEPTH) ** 0.25
LN_EPS = 1e-5
NEG = -30000.0
O_GQ, O_GK, O_GV, O_GR, O_GA = 0, 256, 512, 1024, 1536
O_FQ, O_FK, O_FV, O_FF = 1552, 2320, 3088, 3856
O_SQ, O_SK, O_SV, O_GATE = 3862, 4630, 5398, 6166

ENGS = ("pe", "act", "dve", "pool", "sp")


class Op:
    __slots__ = ("eng", "fn", "reads", "writes", "dma_key", "deps", "has_dep", "seq", "idx")

    def __init__(self, eng, fn, reads, writes, dma_key):
        self.eng = eng
        self.fn = fn
        self.reads = reads
        self.writes = writes
        self.dma_key = dma_key
        self.deps = []
        self.has_dep = False
        self.seq = None
        self.idx = None


class Prog:
    def __init__(self, nc):
        self.nc = nc
        self.ops = []
        self.last_writer = {}
        self.readers = {}
        self.fence_op = None
        self.dram_dep = {}

    def fence(self, eng="dve"):
        keep = lambda k: isinstance(k, tuple) and k[0] == "ccw"
        keys = set(k for k in (set(self.last_writer.keys()) | set(self.readers.keys())) if not keep(k))
        o = self.op(eng, lambda e: e.engine_nop(), reads=(), writes=tuple(keys))
        self.last_writer = {k: v for k, v in self.last_writer.items() if keep(k)}
        self.readers = {k: v for k, v in self.readers.items() if keep(k)}
        self.fence_op = o
        return o

    def op(self, eng, fn, reads=(), writes=(), dma_key=None):
        o = Op(eng, fn, tuple(reads), tuple(writes), dma_key)
        o.idx = len(self.ops)
        deps = set()
        for k in o.reads:
            w = self.last_writer.get(k, None if (isinstance(k, tuple) and k[0] == "ccw") else self.fence_op)
            if w is not None:
                deps.add(w)
        for k in o.writes:
            w = self.last_writer.get(k, self.fence_op)
            if w is not None:
                deps.add(w)
            for r in self.readers.get(k, ()):
                deps.add(r)
        o.deps = sorted(deps, key=lambda d: d.idx)
        for k in o.writes:
            self.last_writer[k] = o
            self.readers[k] = []
        for k in o.reads:
            lst = self.readers.setdefault(k, [])
            if o.dma_key is None:
                lst[:] = [r for r in lst if not (r.dma_key is None and r.eng == o.eng)]
            lst.append(o)
        self.ops.append(o)
        return o

    def dma(self, eng, out, in_, key, reads=(), writes=()):
        reads = list(reads)
        try:
            nm = in_.tensor.name
            if nm in self.dram_dep:
                reads.append(self.dram_dep[nm])
        except AttributeError:
            pass
        return self.op(eng, lambda e: e.dma_start(out=out, in_=in_), reads, writes, dma_key=key)

    def emit(self):
        nc = self.nc
        ops = self.ops
        for o in ops:
            real = []
            for d in o.deps:
                if d.dma_key is None and o.dma_key is None and d.eng == o.eng and d.eng == "pe":
                    continue
                real.append(d)
            o.deps = real
            for d in real:
                d.has_dep = True
        cnt = {e: 0 for e in ENGS}
        dcnt = {}
        self.cc_inc = lambda k: 1 if (isinstance(k, str) and k.endswith("_sem")) else 16
        for o in ops:
            if o.dma_key is not None:
                dcnt[o.dma_key] = dcnt.get(o.dma_key, 0) + self.cc_inc(o.dma_key)
                o.seq = dcnt[o.dma_key]
            elif o.has_dep:
                cnt[o.eng] += 1
                o.seq = cnt[o.eng]
        with ExitStack() as st:
            esem = {e: st.enter_context(nc.semaphore("s_" + e)) for e in ENGS}
            dsem = {}
            for i, k in enumerate(dcnt.keys()):
                dsem[k] = st.enter_context(nc.semaphore("d%d" % i))
            block = st.enter_context(nc.Block())
            by_eng = {e: [o for o in ops if o.eng == e] for e in ENGS}

            def run(eng_name, eng):
                waited = {}
                for o in by_eng[eng_name]:
                    for d in o.deps:
                        if d.dma_key is not None:
                            s, v, key = dsem[d.dma_key], d.seq, ("d", d.dma_key)
                        else:
                            s, v, key = esem[d.eng], d.seq, ("e", d.eng)
                        if waited.get(key, 0) >= v:
                            continue
                        waited[key] = v
                        eng.wait_ge(s, v)
                    ins = o.fn(eng)
                    if o.dma_key is not None:
                        ins.then_inc(dsem[o.dma_key], self.cc_inc(o.dma_key))
                    elif o.has_dep:
                        ins.then_inc(esem[o.eng], 1)
                fin = {}
                for o in by_eng[eng_name]:
                    if o.dma_key is not None:
                        fin[o.dma_key] = max(fin.get(o.dma_key, 0), o.seq)
                for k, v in fin.items():
                    if waited.get(("d", k), 0) < v:
                        eng.wait_ge(dsem[k], v)

            @block.tensor
            def _(e):
                run("pe", e)

            @block.scalar
            def _(e):
                run("act", e)

            @block.vector
            def _(e):
                run("dve", e)

            @block.gpsimd
            def _(e):
                run("pool", e)

            @block.sync
            def _(e):
                run("sp", e)


class Ctx:
    pass


def _mm(C, out, lhsT, rhs, start, stop, reads, writes, skip=False):
    if skip:
        C.P.op("pe", lambda e: e.matmul(out, lhsT, rhs, start=start, stop=stop, skip_group_check=True), reads, writes)
    else:
        C.P.op("pe", lambda e: e.matmul(out, lhsT, rhs, start=start, stop=stop), reads, writes)


def _act(C, out, in_, func, reads, writes, bias=None, scale=None):
    kw = {}
    if bias is not None:
        kw["bias"] = bias
    if scale is not None:
        kw["scale"] = scale
    C.P.op("act", lambda e: e.activation(out, in_, func, **kw), reads, writes)


def _tt(C, eng, out, in0, in1, op, reads, writes):
    C.P.op(eng, lambda e: e.tensor_tensor(out, in0, in1, op), reads, writes)


def _stt(C, eng, out, in0, scalar, in1, op0, op1, reads, writes):
    C.P.op(eng, lambda e: e.scalar_tensor_tensor(out, in0, scalar, in1, op0, op1), reads, writes)


def _ts(C, eng, out, in0, s1, s2, op0, op1, reads, writes):
    if s2 is None:
        C.P.op(eng, lambda e: e.tensor_scalar(out, in0, s1, None, op0), reads, writes)
    else:
        C.P.op(eng, lambda e: e.tensor_scalar(out, in0, s1, s2, op0, op1), reads, writes)


def _cp(C, eng, out, in_, reads, writes):
    if eng == "act":
        C.P.op(eng, lambda e: e.copy(out, in_), reads, writes)
    else:
        C.P.op(eng, lambda e: e.tensor_copy(out, in_), reads, writes)


def setup_ctx(nc):
    C = Ctx()
    C.nc = nc
    C.P = Prog(nc)
    A = nc.alloc_sbuf_tensor
    C.acc = A("sb_acc", [128, 16, T], F32)
    C.xbf = A("sb_xbf", [128, 16, T], BF16)
    C.kv = A("sb_kv", [128, 16384], BF16)
    C.W = A("sb_W", [128, 20480], BF16)
    C.wk = A("sb_wk", [128, 5120], F32)
    C.cst = A("sb_cst", [128, 8, 128], F32)
    C.cbf = A("sb_cbf", [128, 4, 128], BF16)
    C.lnp = A("sb_lnp", [128, 2 * 3 * 2 * 16], F32)
    C.sel = A("sb_sel", [128, 8], F32)
    C.sm = A("sb_sm", [128, 1664], F32)
    C.ps = [nc.alloc_psum_tensor("ps%d" % i, [128, 512], F32) for i in range(8)]
    C.cnt = 0
    return C


def wk_tile(C, i):
    return C.wk[:, i * 512:(i + 1) * 512]


def wk_bf(C, i):
    return C.wk.bitcast(BF16)[:, i * 1024:(i + 1) * 1024]


def wkv(C, t, a, b):
    return C.wk[:, t * 512 + a:t * 512 + b]


def wkb(C, t, a, b):
    return C.wk.bitcast(BF16)[:, t * 1024 + a:t * 1024 + b]


def WK(t):
    return ("wk", t)


def psk(i):
    return ("ps", i)


def acck(c, h):
    return ("acc", c, h)


def xbk(c, h):
    return ("xbf", c, h)


def emit_consts(C, cst_d, cbf_d, lnp_d, msk_d, sel_d):
    P = C.P
    P.dma("sp", C.cst[:, :, :], cst_d, key="cst", writes=["cst"])
    P.dma("sp", C.cbf[:, :, :], cbf_d, key="cbf", writes=["cbf"])
    P.dma("sp", C.lnp[:, :], lnp_d, key="lnp", writes=["lnp"])
    P.dma("sp", C.sel[:, :], sel_d, key="sel", writes=["sel"])


CI_ONESD, CI_TINCL, CI_TGT, CI_ONESS, CI_MASKINC01, CI_ONES128 = 0, 1, 2, 3, 4, 5
BI_ONES, BI_ZERO, BI_MSTRICT = 0, 1, 2


def host_consts(core):
    cst = np.zeros((128, 8, 128), np.float32)
    s = np.arange(128)[:, None]
    t = np.arange(128)[None, :]
    cst[:, CI_ONESD, :] = 1.0 / DM
    cst[:, CI_TINCL, :] = np.where(s <= t, -1.0 / 16, 0.0)
    cst[:, CI_TGT, :] = np.where(s > t, -1.0 / 16, 0.0)
    cst[:, CI_ONESS, :] = -1.0 / 16
    cst[:, CI_MASKINC01, :] = np.where(s <= t, 1.0, 0.0)
    cst[:, CI_ONES128, :] = 1.0 / 128
    cst[:, 6, :] = np.where(s <= t, 1.0, 0.0)
    cst[:, 7, :] = 1.0
    cbf = np.zeros((128, 4, 128), np.float32)
    cbf[:, BI_ONES, :] = 1.0
    cbf[:, BI_MSTRICT, :] = np.where(s > t, 1.0, 0.0)
    cbf = cbf.astype(ml_dtypes.bfloat16)
    msk = np.zeros((128, 3, 8, 128), np.float32)
    for m in range(8):
        if m < core:
            inc = np.zeros((128, 128)); stn = np.zeros((128, 128)); st01 = np.ones((128, 128))
        elif m == core:
            inc = np.where(s <= t, 0.0, NEG); stn = np.where(s < t, 0.0, NEG); st01 = np.where(s < t, 1.0, 0.0)
        else:
            inc = np.full((128, 128), NEG); stn = np.full((128, 128), NEG); st01 = np.zeros((128, 128))
        msk[:, 0, m, :] = inc
        msk[:, 1, m, :] = stn
        msk[:, 2, m, :] = st01
    sel = np.zeros((128, 8), np.float32)
    sel[:, :core] = 1.0
    return cst, cbf, msk.astype(ml_dtypes.bfloat16), sel


def emit_ffn(C, wgu_d, wd_d):
    P = C.P
    Wv = C.W
    wgu = [Wv[:, s * 4096:(s + 1) * 4096].rearrange("p (a c f) -> p a c f", a=2, c=16) for s in range(3)]
    wd = [Wv[:, (3 + s) * 4096:(4 + s) * 4096].rearrange("p (j n) -> p j n", j=4) for s in range(2)]
    hT = [C.kv[:, s * 4096:(s + 1) * 4096].rearrange("p (j t) -> p j t", j=4) for s in range(2)]
    for c in range(16):
        for h in range(2):
            _ts(C, "pool", C.acc[:, c, h * 512:(h + 1) * 512], C.acc[:, c, h * 512:(h + 1) * 512], ALPHA, None, ALU.mult, None,
                reads=[acck(c, h)], writes=[acck(c, h)])
    dcount = 0
    ugcount = 0
    for g in range(NG):
        hs = g % 2
        for j in range(4):
            fc = 4 * g + j
            sl = fc % 3
            P.dma("pool", wgu[sl], wgu_d[fc], key=("wgu", sl), writes=[("W", 2 * sl), ("W", 2 * sl + 1)])
            for h in range(2):
                pi = ugcount % 2
                ugcount += 1
                psg, psu = C.ps[2 * pi], C.ps[2 * pi + 1]
                hsl = slice(h * 512, (h + 1) * 512)
                for c in range(16):
                    _mm(C, psg[:, :], wgu[sl][:, 0, c, :], C.xbf[:, c, hsl], c == 0, c == 15,
                        reads=[("W", 2 * sl), xbk(c, h)], writes=[psk(2 * pi)])
                for c in range(16):
                    _mm(C, psu[:, :], wgu[sl][:, 1, c, :], C.xbf[:, c, hsl], c == 0, c == 15,
                        reads=[("W", 2 * sl + 1), xbk(c, h)], writes=[psk(2 * pi + 1)])
                sg = wk_tile(C, pi)
                _act(C, sg, psg[:, :], AF.Silu, reads=[psk(2 * pi)], writes=[("wk", pi)])
                _tt(C, "dve", hT[hs][:, j, hsl], sg, psu[:, :], ALU.mult,
                    reads=[("wk", pi), psk(2 * pi + 1)], writes=[("hT", hs, j, h)])
        for dh in range(2):
            sl = dh
            P.dma("pool", wd[sl], wd_d[g, dh], key=("wd", sl), writes=[("W", 6 + 2 * sl), ("W", 7 + 2 * sl)])
            for dcl in range(8):
                dc = dh * 8 + dcl
                for h in range(2):
                    pb = 4 + dcount % 4
                    dcount += 1
                    hsl = slice(h * 512, (h + 1) * 512)
                    for j in range(4):
                        _mm(C, C.ps[pb][:, :], wd[sl][:, j, dcl * 128:(dcl + 1) * 128], hT[hs][:, j, hsl], j == 0, j == 3,
                            reads=[("W", 6 + 2 * sl), ("W", 7 + 2 * sl), ("hT", hs, j, h)], writes=[psk(pb)])
                    _stt(C, "dve", C.acc[:, dc, hsl], C.ps[pb][:, :], 0.5, C.acc[:, dc, hsl], ALU.mult, ALU.add,
                         reads=[psk(pb), acck(dc, h)], writes=[acck(dc, h)])


def emit_ln(C, li, want_bf=True):
    gcol = lambda c: C.lnp[:, (li * 2 + 0) * 16 + c:(li * 2 + 0) * 16 + c + 1]
    bcol = lambda c: C.lnp[:, (li * 2 + 1) * 16 + c:(li * 2 + 1) * 16 + c + 1]
    onesD = C.cst[:, CI_ONESD, :]
    for h in range(2):
        hsl = slice(h * 512, (h + 1) * 512)
        o = 0
        s1, s2, tq, mean, rstd, tmp = [wk_tile(C, o + i) for i in range(6)]
        k = lambda i: ("wk", o + i)
        _tt(C, "dve", s1, C.acc[:, 0, hsl], C.acc[:, 1, hsl], ALU.add, reads=[acck(0, h), acck(1, h)], writes=[k(0)])
        for c in range(2, 16):
            _tt(C, "dve", s1, s1, C.acc[:, c, hsl], ALU.add, reads=[k(0), acck(c, h)], writes=[k(0)])
        _act(C, s2, C.acc[:, 0, hsl], AF.Square, reads=[acck(0, h)], writes=[k(1)])
        for c in range(1, 16):
            _act(C, tq, C.acc[:, c, hsl], AF.Square, reads=[acck(c, h)], writes=[k(2)])
            _tt(C, "pool", s2, s2, tq, ALU.add, reads=[k(1), k(2)], writes=[k(1)])
        pm, pe2 = 2 * h, 2 * h + 1
        _mm(C, C.ps[pm][:, :], onesD, s1, True, True, reads=["cst", k(0)], writes=[psk(pm)])
        _mm(C, C.ps[pe2][:, :], onesD, s2, True, True, reads=["cst", k(1)], writes=[psk(pe2)])
        _cp(C, "act", mean, C.ps[pm][:, :], reads=[psk(pm)], writes=[k(3)])
        _tt(C, "dve", tmp, mean, mean, ALU.mult, reads=[k(3)], writes=[k(5)])
        _tt(C, "dve", tmp, C.ps[pe2][:, :], tmp, ALU.subtract, reads=[psk(pe2), k(5)], writes=[k(5)])
        _ts(C, "dve", tmp, tmp, LN_EPS, None, ALU.add, None, reads=[k(5)], writes=[k(5)])
        _act(C, tmp, tmp, AF.Sqrt, reads=[k(5)], writes=[k(5)])
        C.P.op("dve", lambda e, a=rstd, b=tmp: e.reciprocal(a, b), reads=[k(5)], writes=[k(4)])
        for c in range(16):
            tb = [s1, s2][c % 2]
            kb = [k(0), k(1)][c % 2]
            _tt(C, "dve", tb, C.acc[:, c, hsl], mean, ALU.subtract, reads=[acck(c, h), k(3)], writes=[kb])
            _tt(C, "dve", tb, tb, rstd, ALU.mult, reads=[kb, k(4)], writes=[kb])
            _act(C, C.acc[:, c, hsl], tb, AF.Identity, reads=[kb, "lnp"], writes=[acck(c, h)], bias=bcol(c), scale=gcol(c))
            if want_bf:
                _cp(C, "pool", C.xbf[:, c, hsl], C.acc[:, c, hsl], reads=[acck(c, h)], writes=[xbk(c, h)])


WALL = [("W", i) for i in range(12)]


def emit_gla_load_weights(C, wl):
    P = C.P
    Wv = C.W
    C.wgfm = Wv[:, 0:16 * 528].rearrange("p (c n) -> p c n", c=16)
    C.wgtm = Wv[:, 10240:10240 + 16 * 512].rearrange("p (c n) -> p c n", c=16)
    P.dma("pool", C.wgfm, wl["w_gfm"], key="wgfm", writes=[("W", i) for i in range(5)])
    P.dma("pool", C.wgtm, wl["w_gtm"], key="wgtm", writes=[("W", i) for i in range(5, 9)])
    C.wa2 = C.sm[0:16, 0:256]
    C.babc = C.sm[:, 256:512]
    P.dma("sp", C.wa2, wl["wa2"], key="wa2", writes=["wa2"])
    P.dma("sp", C.babc, wl["ba"].partition_broadcast(128), key="babc", writes=["babc"])
    C.k_wgfm = [("W", i) for i in range(5)]
    C.k_wgtm = [("W", i) for i in range(5, 9)]


def emit_gla_tile(C, i, need_q):
    tsl = slice(i * 128, (i + 1) * 128)
    gaT = C.wk[0:16, 0:128]
    dcol = wkv(C, 0, 128, 130)
    xa = wkv(C, 1, 0, 256)
    lap = wkv(C, 1, 256, 512)
    ek = wkv(C, 2, 0, 256)
    kf = wkv(C, 2, 256, 512)
    ebT = wkv(C, 3, 0, 256)
    enbT = wkv(C, 3, 256, 512)
    qT = wkv(C, 4, 0, 256)
    kT = wkv(C, 4, 256, 512)
    khat = wkb(C, 5, 0, 256)
    vtm = wkb(C, 5, 256, 768)
    qtil = wkb(C, 8, 0, 256)
    ktil = wkb(C, 8, 256, 512)
    ps0, ps1, ps2, ps3 = C.ps[0], C.ps[1], C.ps[2], C.ps[3]
    xr = [xbk(c, i // 4) for c in range(16)]
    for u in range(4):
        if u < 2 and not need_q:
            continue
        for c in range(16):
            _mm(C, ps0[:, u * 128:(u + 1) * 128], C.wgfm[:, c, u * 128:(u + 1) * 128], C.xbf[:, c, tsl], c == 0, c == 15,
                reads=C.k_wgfm + xr, writes=[psk(0)])
    for c in range(16):
        _mm(C, ps1[0:16, 0:128], C.wgfm[:, c, 512:528], C.xbf[:, c, tsl], c == 0, c == 15, reads=C.k_wgfm + xr, writes=[psk(1)])
    for c in range(16):
        _mm(C, ps2[:, 0:256], C.xbf[:, c, tsl], C.wgfm[:, c, 256:512], c == 0, c == 15, reads=C.k_wgfm + xr, writes=[psk(2)])
    for c in range(16):
        _mm(C, ps3[:, :], C.xbf[:, c, tsl], C.wgtm[:, c, :], c == 0, c == 15, reads=C.k_wgtm + xr, writes=[psk(3)])
    _cp(C, "act", gaT, ps1[0:16, 0:128], reads=[psk(1)], writes=[WK(0)])
    if need_q:
        _act(C, qT, ps0[:, 0:256], AF.Identity, reads=[psk(0)], writes=[WK(4)], scale=0.125)
    _cp(C, "dve", kT, ps0[:, 256:512], reads=[psk(0)], writes=[WK(4)])
    _cp(C, "dve", kf, ps2[:, 0:256], reads=[psk(2)], writes=[WK(2)])
    _cp(C, "act", vtm, ps3[:, :], reads=[psk(3)], writes=[WK(5)])
    _mm(C, ps1[:, 256:512], gaT, C.wa2, True, True, reads=[WK(0), "wa2"], writes=[psk(1)])
    _tt(C, "dve", xa, ps1[:, 256:512], C.babc, ALU.add, reads=[psk(1), "babc"], writes=[WK(1)])
    _act(C, xa, xa, AF.Exp, reads=[WK(1)], writes=[WK(1)], scale=-1.0)
    _act(C, lap, xa, AF.Ln, reads=[WK(1)], writes=[WK(1)], bias=1.0)
    _mm(C, ps2[:, 256:512], C.cst[:, CI_TGT, :], lap, True, True, reads=["cst", WK(1)], writes=[psk(2)])
    _act(C, ek, ps2[:, 256:512], AF.Exp, reads=[psk(2)], writes=[WK(2)])
    _tt(C, "dve", khat, kf, ek, ALU.mult, reads=[WK(2)], writes=[WK(5)])
    for kc in range(2):
        _mm(C, ps1[:, 128 + kc:129 + kc], lap[:, kc * 128:(kc + 1) * 128], C.cst[:, CI_ONESS, 0:1], True, True,
            reads=["cst", WK(1)], writes=[psk(1)])
    _act(C, dcol, ps1[:, 128:130], AF.Exp, reads=[psk(1)], writes=[WK(0)])
    res = dict(khat=khat, vtm=vtm, dcol=dcol)
    if need_q:
        for kc in range(2):
            _mm(C, ps0[:, kc * 128:(kc + 1) * 128], lap[:, kc * 128:(kc + 1) * 128], C.cst[:, CI_TINCL, :], True, True,
                reads=["cst", WK(1)], writes=[psk(0)])
        _act(C, ebT, ps0[:, 0:256], AF.Exp, reads=[psk(0)], writes=[WK(3)])
        _act(C, enbT, ps0[:, 0:256], AF.Exp, reads=[psk(0)], writes=[WK(3)], scale=-1.0)
        _tt(C, "dve", qtil, qT, ebT, ALU.mult, reads=[WK(4), WK(3)], writes=[WK(8)])
        _tt(C, "dve", ktil, kT, enbT, ALU.mult, reads=[WK(4), WK(3)], writes=[WK(8)])
        res.update(qtil=qtil, ktil=ktil)
    return res


def emit_gla_dS(C, r, psa, psb):
    for kc in range(2):
        for hh in range(2):
            hd = 2 * kc + hh
            pb = psa if hh == 0 else psb
            _mm(C, C.ps[pb][:, kc * 128:(kc + 1) * 128], r["khat"][:, kc * 128:(kc + 1) * 128], r["vtm"][:, hd * 128:(hd + 1) * 128],
                True, True, reads=[WK(5)], writes=[psk(pb)])


def emit_phaseA_proj(C, wl, KT_o, V_o, lf_o, dS_o, dc_o):
    P = C.P
    Wv = C.W
    P.fence()
    wslot = [Wv[:, s * 2048:(s + 1) * 2048].rearrange("p (c f) -> p c f", c=16) for s in range(2)]
    for kc in range(12):
        sl = kc % 2
        P.dma("pool", wslot[sl], wl["w_kT"][kc], key=("Wm", sl), writes=[("W", sl)])
        for h in range(2):
            pb = (2 * kc + h) % 4
            hsl = slice(h * 512, (h + 1) * 512)
            for c in range(16):
                _mm(C, C.ps[pb][:, :], wslot[sl][:, c, :], C.xbf[:, c, hsl], c == 0, c == 15,
                    reads=[("W", sl), xbk(c, h)], writes=[psk(pb)])
            ot = wk_bf(C, pb)[:, 0:512]
            _cp(C, "act", ot, C.ps[pb][:, :], reads=[psk(pb)], writes=[WK(pb)])
            P.dma("sp", KT_o[kc, :, hsl], ot, key=("kto", pb), reads=[WK(pb)])
    wv = Wv[:, 4096:4096 + 8192].rearrange("p (c n) -> p c n", c=16)
    kwv = [("W", i) for i in range(2, 6)]
    for vt in range(3):
        P.dma("pool", wv, wl["w_v"][vt], key="wv", writes=kwv)
        for i in range(8):
            pb = 4 + (vt * 8 + i) % 4
            tsl = slice(i * 128, (i + 1) * 128)
            for c in range(16):
                _mm(C, C.ps[pb][:, :], C.xbf[:, c, tsl], wv[:, c, :], c == 0, c == 15,
                    reads=kwv + [xbk(c, i // 4)], writes=[psk(pb)])
            ot = wk_bf(C, pb)[:, 0:512]
            _cp(C, "dve", ot, C.ps[pb][:, :], reads=[psk(pb)], writes=[WK(pb)])
            P.dma("sp", V_o[vt * 4:(vt + 1) * 4, :, i, :].rearrange("h p d -> p h d"), ot.rearrange("p (h d) -> p h d", h=4),
                  key=("vo", pb), reads=[WK(pb)])
    wff = Wv[:, 12288:12288 + 96].rearrange("p (c n) -> p c n", c=16)
    P.dma("pool", wff, wl["w_ff"], key="wff", writes=[("W", 6)])
    bfbc = C.sm[:, 512:518]
    P.dma("sp", bfbc, wl["bf"].partition_broadcast(128), key="bfbc", writes=["bfbc"])
    for i in range(8):
        tsl = slice(i * 128, (i + 1) * 128)
        for c in range(16):
            _mm(C, C.ps[0][:, i * 6:(i + 1) * 6], C.xbf[:, c, tsl], wff[:, c, :], c == 0, c == 15,
                reads=[("W", 6), xbk(c, i // 4)], writes=[psk(0)])
    lft = C.sm[:, 520:568]
    for i in range(8):
        _tt(C, "dve", lft[:, i * 6:(i + 1) * 6], C.ps[0][:, i * 6:(i + 1) * 6], bfbc, ALU.add, reads=[psk(0), "bfbc"], writes=["lft"])
    _act(C, lft, lft, AF.Exp, reads=["lft"], writes=["lft"], scale=-1.0)
    _act(C, lft, lft, AF.Ln, reads=["lft"], writes=["lft"], bias=1.0)
    _ts(C, "dve", lft, lft, -1.0, None, ALU.mult, None, reads=["lft"], writes=["lft"])
    P.dma("sp", lf_o, lft, key="lfo", reads=["lft"])
    P.fence()
    emit_gla_load_weights(C, wl)
    for i in range(8):
        r = emit_gla_tile(C, i, need_q=False)
        emit_gla_dS(C, r, 4, 5)
        dst = C.sm[:, 1024 + (i % 2) * 256:1024 + (i % 2) * 256 + 256]
        kk = ("dSs", i % 2)
        for hh in range(2):
            psl = slice(hh * 64, (hh + 1) * 64)
            _cp(C, "dve", dst[psl, :], C.ps[4 + hh][psl, 0:256], reads=[psk(4 + hh)], writes=[kk])
        P.dma("sp", dS_o[i], dst, key=("dSo", i % 2), reads=[kk])
        dcs = C.sm[:, 1536 + 2 * i:1538 + 2 * i]
        _cp(C, "dve", dcs, r["dcol"], reads=[WK(0)], writes=["dcs"])
    P.dma("sp", dc_o, C.sm[:, 1536:1552], key="dco", reads=["dcs"])


def emit_mixer(C, wl, KT_all, V_all, lf_all, lfown_d, dS_all, dc_all):
    P = C.P
    Wv = C.W
    accb = C.acc.bitcast(BF16)
    oT = accb[:, 0:8, :].rearrange("p a (b t) -> p (a b) t", b=2)
    rowf = lambda r: C.acc[:, 8 + r, :]
    rowb = lambda r: accb[:, 8 + r, :]
    P.fence()
    emit_gla_load_weights(C, wl)
    Sf = rowf(0)[:, 0:256]
    Sg = rowf(0)[:, 256:512]
    tmpS = rowf(0)[:, 512:768]
    Sbf_all = rowb(1)
    QT = rowb(2)[:, 0:1024]
    gb = C.acc[:, 11:13, :].rearrange("p a t -> p (a t)")
    Aq = rowf(5)
    r67 = C.acc[:, 14:16, :].rearrange("p a t -> p (a t)")
    C.msk = r67[:, 512:2048].bitcast(BF16).rearrange("p (a m t) -> p a m t", a=3, m=8)
    P.dma("sp", C.msk, C.msk_d, key="msk", writes=["msk"])
    dcg = C.sm[:, 1024:1024 + 128].rearrange("p (r i k) -> p r i k", r=8, i=8)
    P.dma("sp", C.sm[:, 1024:1024 + 128].rearrange("p (r x) -> p r x", r=8), dc_all.rearrange("r p x -> p r x"), key="dcg", writes=["dcg"])
    P.op("dve", lambda e: e.memset(Sf, 0.0), reads=[], writes=["Sf"])
    oms = C.sm[:, 1160:1168]
    _ts(C, "dve", oms, C.sel[:, :], -1.0, 1.0, ALU.mult, ALU.add, reads=["sel"], writes=["oms"])
    dm = C.sm[:, 1168:1170]
    for i in range(8 if "scan" in C.stages else 0):
        for m in range(8):
            P.dma("sp", gb[:, m * 256:(m + 1) * 256], dS_all[m, i], key=("dSl", m), writes=[("gb", m)])
        _cp(C, "dve", Sg, Sf, reads=["Sf"], writes=["Sg"])
        for m in range(8):
            _stt(C, "dve", dm, dcg[:, m, i, :], C.sel[:, m:m + 1], oms[:, m:m + 1].to_broadcast([128, 2]), ALU.mult, ALU.add,
                 reads=["dcg", "sel", "oms"], writes=["dm"])
            _ts(C, "dve", tmpS, gb[:, m * 256:(m + 1) * 256], C.sel[:, m:m + 1], None, ALU.mult, None,
                reads=[("gb", m), "sel"], writes=["tmpS"])
            for kc in range(2):
                ksl = slice(kc * 128, (kc + 1) * 128)
                _stt(C, "dve", Sg[:, ksl], Sg[:, ksl], dm[:, kc:kc + 1], tmpS[:, ksl], ALU.mult, ALU.add,
                     reads=["Sg", "dm", "tmpS"], writes=["Sg"])
                _stt(C, "dve", Sf[:, ksl], Sf[:, ksl], dcg[:, m, i, kc:kc + 1], gb[:, m * 256 + kc * 128:m * 256 + (kc + 1) * 128],
                     ALU.mult, ALU.add, reads=["Sf", "dcg", ("gb", m)], writes=["Sf"])
        _cp(C, "dve", Sbf_all[:, i * 256:(i + 1) * 256], Sg, reads=["Sg"], writes=[("Sbf", i)])
    ng = C.sm[:, 1170:1171]
    P.dma("sp", ng, wl["ng"], key="ng", writes=["ng"])
    for i in range(8 if "tiles" in C.stages else 0):
        r = emit_gla_tile(C, i, need_q=True)
        tsl = slice(i * 128, (i + 1) * 128)
        attm = wkb(C, 9, 0, 512)
        if "tiles1" in C.stages:
            continue
        ab = lambda hh: 4 if hh == 0 else 7
        ob = lambda hh: 5 if hh == 0 else 6
        for hd in range(4):
            kc, hh = hd // 2, hd % 2
            psl = slice(hh * 64, (hh + 1) * 64)
            _mm(C, C.ps[ab(hh)][:, hd * 128:(hd + 1) * 128], r["ktil"][psl, kc * 128:(kc + 1) * 128], r["qtil"][psl, kc * 128:(kc + 1) * 128],
                True, True, reads=[WK(8)], writes=[psk(ab(hh))])
        for hd in range(4):
            hh = hd % 2
            _tt(C, "dve", attm[:, hd * 128:(hd + 1) * 128], C.ps[ab(hh)][:, hd * 128:(hd + 1) * 128], C.cst[:, CI_MASKINC01, :], ALU.mult,
                reads=[psk(ab(hh)), "cst"], writes=[WK(9)])
        if "t_att" in C.stages:
            continue
        for hd in range(4):
            kc, hh = hd // 2, hd % 2
            psl = slice(hh * 64, (hh + 1) * 64)
            _mm(C, C.ps[ob(hh)][:, hd * 128:(hd + 1) * 128], r["vtm"][:, hd * 128:(hd + 1) * 128], attm[:, hd * 128:(hd + 1) * 128],
                True, False, reads=[WK(5), WK(9)], writes=[psk(ob(hh))])
            _mm(C, C.ps[ob(hh)][:, hd * 128:(hd + 1) * 128], Sbf_all[psl, i * 256 + kc * 128:i * 256 + (kc + 1) * 128],
                r["qtil"][psl, kc * 128:(kc + 1) * 128], False, True, reads=[("Sbf", i), WK(8)], writes=[psk(ob(hh))])
        if "t_o" in C.stages:
            continue
        sq = wk_tile(C, 6)
        for hd in range(4):
            hh = hd % 2
            _act(C, sq[:, hd * 128:(hd + 1) * 128], C.ps[ob(hh)][:, hd * 128:(hd + 1) * 128], AF.Square, reads=[psk(ob(hh))], writes=[WK(6)])
        _mm(C, C.ps[3][:, :], C.cst[:, CI_ONES128, :], sq, True, True, reads=["cst", WK(6)], writes=[psk(3)])
        rs = wk_tile(C, 7)
        _ts(C, "dve", rs, C.ps[3][:, :], LN_EPS, None, ALU.add, None, reads=[psk(3)], writes=[WK(7)])
        _act(C, rs, rs, AF.Sqrt, reads=[WK(7)], writes=[WK(7)])
        P.op("dve", lambda e, a=rs: e.reciprocal(a, a), reads=[WK(7)], writes=[WK(7)])
        for hd in range(4):
            hh = hd % 2
            _tt(C, "dve", rs[:, hd * 128:(hd + 1) * 128], C.ps[ob(hh)][:, hd * 128:(hd + 1) * 128], rs[:, hd * 128:(hd + 1) * 128], ALU.mult,
                reads=[psk(ob(hh)), WK(7)], writes=[WK(7)])
        for hd in range(4):
            _ts(C, "dve", oT[:, hd, tsl], rs[:, hd * 128:(hd + 1) * 128], ng, None, ALU.mult, None,
                reads=[WK(7), "ng"], writes=[("oTg", hd, i)])
    P.fence()
    wslot = [Wv[:, s * 2048:(s + 1) * 2048].rearrange("p (c f) -> p c f", c=16) for s in range(10)]
    for hd in range(4 if "gr" in C.stages else 0):
        sl = hd % 2
        P.dma("pool", wslot[sl], wl["w_gr"][hd], key=("Wm", sl), writes=[("W", sl)])
        for h in range(2):
            pb = (2 * hd + h) % 4
            hsl = slice(h * 512, (h + 1) * 512)
            for c in range(16):
                _mm(C, C.ps[pb][:, :], wslot[sl][:, c, :], C.xbf[:, c, hsl], c == 0, c == 15, reads=[("W", sl), xbk(c, h)], writes=[psk(pb)])
            sg = wk_tile(C, pb)
            _act(C, sg, C.ps[pb][:, :], AF.Silu, reads=[psk(pb)], writes=[WK(pb)])
            _tt(C, "dve", oT[:, hd, hsl], oT[:, hd, hsl], sg, ALU.mult, reads=[WK(pb), ("oTh", hd, h)], writes=[("oTh", hd, h)])

    if "heads" not in C.stages:
        return oT
    lfn4 = C.sm[:, 568:568 + 384].rearrange("p (i r h) -> p i r h", i=8, r=8)
    for r_ in range(8):
        P.dma("sp", lfn4[:, :, r_, :], lf_all[r_].rearrange("p (i h) -> p i h", h=6), key=("lfn", r_), writes=["lfn"])
    lff = C.sm[:, 568:568 + 384]
    ea = r67[:, 0:384]
    eb = wk_tile(C, 9)[:, 0:384]
    _cp(C, "dve", ea, lff, reads=["lfn"], writes=["ea"])
    src, dst, ks, kd = ea, eb, "ea", WK(9)
    sh = 1
    while sh < 64:
        n = sh * 6
        _cp(C, "dve", dst[:, 0:n], src[:, 0:n], reads=[ks], writes=[kd])
        _tt(C, "dve", dst[:, n:384], src[:, n:384], src[:, 0:384 - n], ALU.add, reads=[ks], writes=[kd])
        src, dst, ks, kd = dst, src, kd, ks
        sh *= 2
    _tt(C, "dve", dst, src, lff, ALU.subtract, reads=[ks, "lfn"], writes=[kd])
    E, kE = dst, kd
    _mm(C, C.ps[0][:, 0:384], C.cst[:, 6, :], lff, True, False, reads=["cst", "lfn"], writes=[psk(0)])
    _mm(C, C.ps[0][:, 0:384], C.cst[:, 7, :], E, False, True, reads=["cst", kE], writes=[psk(0)])
    Eown = C.sm[:, 1224:1272].rearrange("p (i h) -> p i h", h=6)
    E4 = E.rearrange("p (i r h) -> p i r h", i=8, r=8)
    _cp(C, "dve", Eown, E4[:, :, 0, :], reads=[kE], writes=["Eown"])
    for m in range(8):
        _stt(C, "dve", Eown, lfn4[:, :, m, :], C.sel[:, m:m + 1], Eown, ALU.mult, ALU.add, reads=["lfn", "sel", "Eown"], writes=["Eown"])
    nctm = C.sm[:, 1272:1272 + 384].rearrange("p (j h) -> p j h", h=6)
    _ts(C, "dve", C.sm[:, 1272:1272 + 384], C.ps[0][:, 0:384], -1.0, None, ALU.mult, None, reads=[psk(0)], writes=["nctm"])
    lfo = C.sm[:, 1176:1224].rearrange("p (i h) -> p i h", h=6)
    P.dma("sp", C.sm[:, 1176:1224], lfown_d, key="lfo", writes=["lfo"])

    P.fence()
    KTs = C.kv[:, 0:8192]
    Vs = C.kv[:, 8192:16384]
    for hh in range(12):
        is_sb = hh >= 6
        hd = hh % 6
        osl = 4 + hh
        P.dma("sp", KTs.rearrange("p (r t) -> p r t", r=8), KT_all[:, hh].rearrange("r d t -> d r t"), key="ktl", writes=["KTs"])
        P.dma("sp", Vs.rearrange("p (r x) -> p r x", r=8), V_all[:, hh].rearrange("r p i d -> p r (i d)"), key="vl", writes=["Vs"])
        sl = hh % 2
        P.dma("pool", wslot[sl], wl["w_qT"][hh], key=("Wm", sl), writes=[("W", sl)])
        for h in range(2):
            hsl = slice(h * 512, (h + 1) * 512)
            for c in range(16):
                _mm(C, C.ps[h][:, :], wslot[sl][:, c, :], C.xbf[:, c, hsl], c == 0, c == 15, reads=[("W", sl), xbk(c, h)], writes=[psk(h)])
            _act(C, QT[:, hsl], C.ps[h][:, :], AF.Identity, reads=[psk(h)], writes=[("QT", h)], scale=128 ** -0.5)
        if not is_sb:
            for i in range(8):
                lb_ = wkv(C, 4 + i % 2, 0, 128)
                eb_ = wkv(C, 4 + i % 2, 128, 256)
                _cp(C, "dve", lb_, lfo[:, i, hd:hd + 1].to_broadcast([128, 128]), reads=["lfo"], writes=[WK(4 + i % 2)])
                _cp(C, "dve", eb_, Eown[:, i, hd:hd + 1].to_broadcast([128, 128]), reads=["Eown"], writes=[WK(4 + i % 2)])
                pa = 2 + i // 4
                osl_ = slice((i % 4) * 128, (i % 4 + 1) * 128)
                _mm(C, C.ps[pa][:, osl_], lb_, C.cst[:, 6, :], True, False, reads=[WK(4 + i % 2), "cst"], writes=[psk(pa)])
                _mm(C, C.ps[pa][:, osl_], eb_, C.cst[:, 7, :], False, True, reads=[WK(4 + i % 2), "cst"], writes=[psk(pa)])
            for h in range(2):
                _cp(C, "act", Aq[:, h * 512:(h + 1) * 512], C.ps[2 + h][:, :], reads=[psk(2 + h)], writes=[("Aq", h)])
        for ch in range(2):
            lo = ch * 512
            jmax = 32 if ch == 0 else 64
            steps = []
            for j in range(jmax):
                cs = max(lo, 128 * (j // 8))
                steps.append((j, cs, lo + 512, 128 * (j // 8) >= lo))
            if is_sb:
                steps = steps[::-1]
            po, pl = 6, 7
            _mm(C, C.ps[po][:, :], C.cbf[:, BI_ZERO, :], QT[:, lo:lo + 512], True, False, reads=["cbf", ("QT", ch)], writes=[psk(po)], skip=True)
            _mm(C, C.ps[pl][:, :], C.cbf[:, BI_ZERO, :], QT[:, lo:lo + 512], True, False, reads=["cbf", ("QT", ch)], writes=[psk(pl)], skip=True)
            for si, (j, cs, ce, diag) in enumerate(steps):
                r_, i_ = j % 8, j // 8
                m = r_
                n = ce - cs
                kt = KTs[:, r_ * 1024 + i_ * 128: r_ * 1024 + (i_ + 1) * 128]
                vt = Vs[:, r_ * 1024 + i_ * 128: r_ * 1024 + (i_ + 1) * 128]
                dd = si % 4
                pz = (0, 1, 4, 5)[dd]
                hkey = ("wkh", 8 + dd // 2, dd % 2)
                hbuf = wkb(C, 8 + dd // 2, (dd % 2) * 512, (dd % 2) * 512 + n)
                cl = slice(cs - lo, ce - lo)
                last = si == len(steps) - 1
                _mm(C, C.ps[pz][:, 0:n], kt, QT[:, cs:ce], True, True, reads=["KTs", ("QT", ch)], writes=[psk(pz)])
                if not is_sb:
                    tt_ = wk_tile(C, dd)[:, 0:n]
                    pt = hbuf
                    _tt(C, "dve", tt_, C.ps[pz][:, 0:n], Aq[:, cs:ce], ALU.add, reads=[psk(pz), ("Aq", ch)], writes=[WK(dd)])
                    if diag:
                        _tt(C, "dve", tt_[:, 0:128], tt_[:, 0:128], C.msk[:, 0, m, :], ALU.add, reads=[WK(dd), "msk"], writes=[WK(dd)])
                    _act(C, pt, tt_, AF.Exp, reads=[WK(dd), "nctm"], writes=[hkey], bias=nctm[:, j, hd:hd + 1])
                    _mm(C, C.ps[po][:, cl], vt, pt, False, last, reads=["Vs", hkey], writes=[psk(po)], skip=True)
                    _mm(C, C.ps[pl][:, cl], C.cbf[:, BI_ONES, :], pt, False, last, reads=["cbf", hkey], writes=[psk(pl)], skip=True)
                else:
                    ta, tb = 2 * dd, 2 * dd + 1
                    e_ = wk_tile(C, ta)[:, 0:n]
                    zl = wk_tile(C, tb)[:, 0:n]
                    lb = hbuf
                    _act(C, e_, C.ps[pz][:, 0:n], AF.Exp, reads=[psk(pz)], writes=[WK(ta)])
                    _act(C, e_, e_, AF.Ln, reads=[WK(ta)], writes=[WK(ta)], bias=1.0)
                    _act(C, lb, e_, AF.Identity, reads=[WK(ta)], writes=[hkey], scale=-1.0)
                    _tt(C, "dve", zl, C.ps[pz][:, 0:n], e_, ALU.subtract, reads=[psk(pz), WK(ta)], writes=[WK(tb)])
                    if diag:
                        _tt(C, "dve", lb[:, 0:128], lb[:, 0:128], C.msk[:, 2, m, :], ALU.mult, reads=[hkey, "msk"], writes=[hkey])
                        _tt(C, "dve", zl[:, 0:128], zl[:, 0:128], C.msk[:, 1, m, :], ALU.add, reads=[WK(tb), "msk"], writes=[WK(tb)])
                    pr = 2 + si % 2
                    _mm(C, C.ps[pr][:, 0:n], C.cbf[:, BI_MSTRICT, :], lb, True, True, reads=["cbf", hkey], writes=[psk(pr)])
                    _tt(C, "dve", zl, C.ps[pr][:, 0:n], zl, ALU.add, reads=[psk(pr), WK(tb)], writes=[WK(tb)])
                    _tt(C, "dve", zl, C.ps[pl][:, cl], zl, ALU.add, reads=[psk(pl), WK(tb)], writes=[WK(tb)])
                    _mm(C, C.ps[pl][:, cl], C.cbf[:, BI_ONES, :], lb, False, last, reads=["cbf", hkey], writes=[psk(pl)], skip=True)
                    _act(C, lb, zl, AF.Exp, reads=[WK(tb)], writes=[hkey])
                    _mm(C, C.ps[po][:, cl], vt, lb, False, last, reads=["Vs", hkey], writes=[psk(po)], skip=True)
            if not is_sb:
                rl = wk_tile(C, 3)
                P.op("dve", lambda e, a=rl, b=C.ps[pl][:, :]: e.reciprocal(a, b), reads=[psk(pl)], writes=[WK(3)])
                _tt(C, "dve", oT[:, osl, lo:lo + 512], C.ps[po][:, :], rl, ALU.mult, reads=[psk(po), WK(3)], writes=[("oTh", osl, ch)])
            else:
                _cp(C, "dve", oT[:, osl, lo:lo + 512], C.ps[po][:, :], reads=[psk(po)], writes=[("oTh", osl, ch)])
    return oT


def emit_merge_out(C, wl, oT, x1_d):
    P = C.P
    Wv = C.W
    P.fence()
    mT = C.kv.rearrange("p (c t) -> p c t", c=16)
    wgate = [Wv[:, s * 6144:(s + 1) * 6144].rearrange("p (b c f) -> p b c f", b=3, c=16) for s in range(2)]
    wbr = [Wv[:, 12288 + s * 2048:12288 + (s + 1) * 2048].rearrange("p (c f) -> p c f", c=16) for s in range(2)]
    wout = [Wv[:, 16384 + s * 2048:16384 + (s + 1) * 2048].rearrange("p (c f) -> p c f", c=16) for s in range(2)]
    cnt = 0
    for dc in range(16):
        s = dc % 2
        kg = [("W", 3 * s + q) for q in range(3)]
        P.dma("pool", wgate[s], wl["w_gate"][dc], key=("wg", s), writes=kg)
        P.dma("pool", wbr[s], wl["w_br"][dc], key=("wb", s), writes=[("W", 6 + s)])
        for h in range(2):
            hsl = slice(h * 512, (h + 1) * 512)
            q = cnt % 2
            cnt += 1
            gb0 = 3 * q
            for b in range(3):
                for c in range(16):
                    _mm(C, C.ps[gb0 + b][:, :], wgate[s][:, b, c, :], C.xbf[:, c, hsl], c == 0, c == 15,
                        reads=kg + [xbk(c, h)], writes=[psk(gb0 + b)])
                _act(C, wk_tile(C, gb0 + b), C.ps[gb0 + b][:, :], AF.Sigmoid, reads=[psk(gb0 + b)], writes=[WK(gb0 + b)])
            tA = wk_tile(C, 6 + 2 * q)
            tB = wk_tile(C, 7 + 2 * q)
            branches = [(0, range(0, 4)), (1, range(4, 10)), (2, range(10, 16))]
            for b, chs in branches:
                pb = 6 + (b % 2)
                chs = list(chs)
                for ci, ch_ in enumerate(chs):
                    _mm(C, C.ps[pb][:, :], wbr[s][:, ch_, :], oT[:, ch_, hsl], ci == 0, ci == len(chs) - 1,
                        reads=[("W", 6 + s), ("oTh", ch_, h)], writes=[psk(pb)])
                if b == 0:
                    _tt(C, "dve", tA, C.ps[pb][:, :], wk_tile(C, gb0 + b), ALU.mult, reads=[psk(pb), WK(gb0 + b)], writes=[WK(6 + 2 * q)])
                elif b == 1:
                    _tt(C, "dve", tB, C.ps[pb][:, :], wk_tile(C, gb0 + b), ALU.mult, reads=[psk(pb), WK(gb0 + b)], writes=[WK(7 + 2 * q)])
                    _tt(C, "dve", tA, tA, tB, ALU.add, reads=[WK(6 + 2 * q), WK(7 + 2 * q)], writes=[WK(6 + 2 * q)])
                else:
                    _tt(C, "dve", tB, C.ps[pb][:, :], wk_tile(C, gb0 + b), ALU.mult, reads=[psk(pb), WK(gb0 + b)], writes=[WK(7 + 2 * q)])
                    _tt(C, "dve", mT[:, dc, hsl], tA, tB, ALU.add, reads=[WK(6 + 2 * q), WK(7 + 2 * q)], writes=[("mT", dc, h)])
    P.fence()
    for dc in range(16):
        P.dma("sp", C.acc[:, dc, :], x1_d[dc * 128:(dc + 1) * 128, :], key=("x1l", dc), writes=[acck(dc, 0), acck(dc, 1)])
    cnt = 0
    for dc in range(16):
        s = dc % 2
        P.dma("pool", wout[s], wl["w_out"][dc], key=("wo", s), writes=[("W", 8 + s)])
        for h in range(2):
            hsl = slice(h * 512, (h + 1) * 512)
            pb = cnt % 4
            cnt += 1
            for c in range(16):
                _mm(C, C.ps[pb][:, :], wout[s][:, c, :], mT[:, c, hsl], c == 0, c == 15, reads=[("W", 8 + s), ("mT", c, h)], writes=[psk(pb)])
            _stt(C, "dve", C.acc[:, dc, hsl], C.acc[:, dc, hsl], ALPHA, C.ps[pb][:, :], ALU.mult, ALU.add,
                 reads=[acck(dc, h), psk(pb)], writes=[acck(dc, h)])
    P.fence()


W_A = dict(f_wgu=[NFC, 128, 2, 16, 128], f_wd=[NG, 2, 128, 4, 1024], w_kT=[12, 128, 16, 128], w_v=[3, 128, 16, 512],
           w_ff=[128, 16, 6], bf=[1, 6], w_gfm=[128, 16, 528], w_gtm=[128, 16, 512], wa2=[16, 256], ba=[1, 256])
W_B = dict(f_wgu=[NFC, 128, 2, 16, 128], f_wd=[NG, 2, 128, 4, 1024], w_qT=[12, 128, 16, 128], w_gr=[4, 128, 16, 128],
           w_gfm=[128, 16, 528], w_gtm=[128, 16, 512], wa2=[16, 256], ba=[1, 256], ng=[128, 1],
           w_gate=[16, 128, 3, 16, 128], w_br=[16, 128, 16, 128], w_out=[16, 128, 16, 128])


def _decl_common(nc):
    d = {}
    d["cst"] = nc.dram_tensor("cst", [128, 8, 128], F32, kind="ExternalInput").ap()
    d["cbf"] = nc.dram_tensor("cbf", [128, 4, 128], BF16, kind="ExternalInput").ap()
    d["lnp"] = nc.dram_tensor("lnp", [128, 192], F32, kind="ExternalInput").ap()
    d["msk"] = nc.dram_tensor("msk", [128, 3, 8, 128], BF16, kind="ExternalInput").ap()
    d["sel"] = nc.dram_tensor("sel", [128, 8], F32, kind="ExternalInput").ap()
    return d


def build_A(layer):
    nc = bass.Bass("TRN2", target_bir_lowering=False)
    d = _decl_common(nc)
    xT = nc.dram_tensor("xT", [DM, T], F32, kind="ExternalInput").ap()
    wl = {k: nc.dram_tensor(k, shp, F32, kind="ExternalInput").ap() for k, shp in W_A.items()}
    x1_o = nc.dram_tensor("x1_o", [DM, T], F32, kind="ExternalOutput").ap()
    KT_o = nc.dram_tensor("KT_o", [12, 128, T], BF16, kind="ExternalOutput").ap()
    V_o = nc.dram_tensor("V_o", [12, 128, 8, 128], BF16, kind="ExternalOutput").ap()
    lf_o = nc.dram_tensor("lf_o", [128, 48], F32, kind="ExternalOutput").ap()
    dS_o = nc.dram_tensor("dS_o", [8, 128, 256], F32, kind="ExternalOutput").ap()
    dc_o = nc.dram_tensor("dc_o", [128, 16], F32, kind="ExternalOutput").ap()
    C = setup_ctx(nc)
    P = C.P
    C.msk_d = d["msk"]
    emit_consts(C, d["cst"], d["cbf"], d["lnp"], d["msk"], d["sel"])
    xv = xT.rearrange("(c p) t -> p c t", p=128)
    for q in range(4):
        ks = [acck(c, h) for c in range(4 * q, 4 * q + 4) for h in range(2)]
        P.dma("sp", C.acc[:, 4 * q:4 * q + 4, :], xv[:, 4 * q:4 * q + 4, :], key=("xl", q), writes=ks)
        kb = [xbk(c, h) for c in range(4 * q, 4 * q + 4) for h in range(2)]
        P.dma("pool", C.xbf[:, 4 * q:4 * q + 4, :], xv[:, 4 * q:4 * q + 4, :], key=("xlb", q), writes=kb)
    emit_ffn(C, wl["f_wgu"], wl["f_wd"])
    emit_ln(C, layer * 3 + 0)
    xo = x1_o.rearrange("(c p) t -> p c t", p=128)
    for q in range(4):
        ks = [acck(c, h) for c in range(4 * q, 4 * q + 4) for h in range(2)]
        P.dma("sp", xo[:, 4 * q:4 * q + 4, :], C.acc[:, 4 * q:4 * q + 4, :], key=("xo", q), reads=ks)
    emit_phaseA_proj(C, wl, KT_o, V_o, lf_o, dS_o, dc_o)
    P.emit()
    return nc


class LazyW(dict):
    def __init__(self, nc, shapes):
        super().__init__()
        self.nc = nc
        self.shapes = shapes

    def __missing__(self, k):
        t = self.nc.dram_tensor(k, self.shapes[k], F32, kind="ExternalInput").ap()
        self[k] = t
        return t


def build_B(layer, dbg=False, stages=("gla", "scan", "tiles", "gr", "heads", "merge", "ffn")):
    nc = bass.Bass("TRN2", target_bir_lowering=False)
    d = _decl_common(nc)
    x1 = nc.dram_tensor("x1", [DM, T], F32, kind="ExternalInput").ap()
    wl = LazyW(nc, W_B)
    KT_all = V_all = lf_all = lf_own = None
    if "heads" in stages:
        KT_all = nc.dram_tensor("KT_all", [8, 12, 128, T], BF16, kind="ExternalInput").ap()
        V_all = nc.dram_tensor("V_all", [8, 12, 128, 8, 128], BF16, kind="ExternalInput").ap()
        lf_all = nc.dram_tensor("lf_all", [8, 128, 48], F32, kind="ExternalInput").ap()
        lf_own = nc.dram_tensor("lf_own", [128, 48], F32, kind="ExternalInput").ap()
    dS_all = nc.dram_tensor("dS_all", [8, 8, 128, 256], F32, kind="ExternalInput").ap()
    dc_all = nc.dram_tensor("dc_all", [8, 128, 16], F32, kind="ExternalInput").ap()
    x3_o = nc.dram_tensor("x3_o", [DM, T], F32, kind="ExternalOutput").ap()
    C = setup_ctx(nc)
    C.stages = stages
    P = C.P
    C.msk_d = d["msk"]
    emit_consts(C, d["cst"], d["cbf"], d["lnp"], d["msk"], d["sel"])
    xv = x1.rearrange("(c p) t -> p c t", p=128)
    for q in range(4):
        kb = [xbk(c, h) for c in range(4 * q, 4 * q + 4) for h in range(2)]
        P.dma("pool", C.xbf[:, 4 * q:4 * q + 4, :], xv[:, 4 * q:4 * q + 4, :], key=("xlb", q), writes=kb)
    oT = emit_mixer(C, wl, KT_all, V_all, lf_all, lf_own, dS_all, dc_all)
    if dbg:
        P.fence()
        dbg_oT = nc.dram_tensor("dbg_oT", [128, 16, T], BF16, kind="ExternalOutput").ap()
        P.dma("sp", dbg_oT, oT, key="dbg1", reads=[("oTh", c, h) for c in range(16) for h in range(2)])
        P.fence()
    if "merge" in stages:
        emit_merge_out(C, wl, oT, x1)
    if dbg:
        dbg_y = nc.dram_tensor("dbg_y", [128, 16, T], F32, kind="ExternalOutput").ap()
        P.dma("sp", dbg_y, C.acc[:, :, :], key="dbg2", reads=[acck(c, h) for c in range(16) for h in range(2)])
        P.fence()
    if "ffn" in stages:
        emit_ln(C, layer * 3 + 1)
        P.fence()
        emit_ffn(C, wl["f_wgu"], wl["f_wd"])
        emit_ln(C, layer * 3 + 2, want_bf=False)
    P.fence()
    xo = x3_o.rearrange("(c p) t -> p c t", p=128)
    for q in range(4):
        ks = [acck(c, h) for c in range(4 * q, 4 * q + 4) for h in range(2)]
        P.dma("sp", xo[:, 4 * q:4 * q + 4, :], C.acc[:, 4 * q:4 * q + 4, :], key=("xo", q), reads=ks)
    P.emit()
    return nc


RG = [list(range(NCORES))]
TINY_W = ("bf", "ba", "ng", "wa2")


def w2d(k, shp):
    nm = 3 if k == "f_wd" else (2 if len(shp) >= 4 else 1)
    rows = int(np.prod(shp[:nm]))
    cols = int(np.prod(shp[nm:]))
    assert rows % NCORES == 0
    return rows, cols


def build_fused():
    nc = bass.Bass("TRN2", target_bir_lowering=False)
    d = _decl_common(nc)
    xT = nc.dram_tensor("xT", [DM, T], F32, kind="ExternalInput").ap()
    x_o = nc.dram_tensor("x_o", [DM, T], F32, kind="ExternalOutput").ap()
    C = setup_ctx(nc)
    C.stages = ("gla", "scan", "tiles", "gr", "heads", "merge", "ffn")
    P = C.P
    C.msk_d = d["msk"]
    emit_consts(C, d["cst"], d["cbf"], d["lnp"], d["msk"], d["sel"])
    xv = xT.rearrange("(c p) t -> p c t", p=128)
    for q in range(4):
        ks = [acck(c, h) for c in range(4 * q, 4 * q + 4) for h in range(2)]
        P.dma("sp", C.acc[:, 4 * q:4 * q + 4, :], xv[:, 4 * q:4 * q + 4, :], key=("xl", q), writes=ks)
        kb = [xbk(c, h) for c in range(4 * q, 4 * q + 4) for h in range(2)]
        P.dma("pool", C.xbf[:, 4 * q:4 * q + 4, :], xv[:, 4 * q:4 * q + 4, :], key=("xlb", q), writes=kb)
    wsets = []
    pend = []
    for l in range(DEPTH):
        for pre, shapes in (("a%d_" % l, W_A), ("b%d_" % l, W_B)):
            ws = {}
            for k, shp in shapes.items():
                if k in TINY_W:
                    ws[k] = nc.dram_tensor(pre + k, shp, F32, kind="ExternalInput").ap()
                    continue
                rows, cols = w2d(k, shp)
                sh = nc.dram_tensor(pre + k, [rows // NCORES, cols], F32, kind="ExternalInput").ap()
                loc = nc.dram_tensor(pre + k + "_s", [rows // NCORES, cols], F32).ap()
                full = nc.dram_tensor(pre + k + "_g", [rows, cols], F32)
                P.dma("sp", loc, sh, key="wcp", writes=[("wsh", len(pend))])
                pend.append((pre + k, loc, full))
                ws[k] = full.reshape(list(shp)).ap()
            wsets.append(ws)
    allsh = [("wsh", i) for i in range(len(pend))]
    for nm, loc, full in pend:
        P.op("pool", lambda e, s_=loc, d_=full.ap(): e.collective_compute("AllGather", ALU.bypass, replica_groups=RG, ins=[s_.opt()], outs=[d_.opt()]),
             reads=allsh, writes=[("ccw", nm)], dma_key="ccw_sem")
        P.dram_dep[nm + "_g"] = ("ccw", nm)
    for l in range(DEPTH):
        wa = wsets[2 * l]
        wb = wsets[2 * l + 1]
        x1_d = nc.dram_tensor("x1_d%d" % l, [DM, T], F32).ap()
        KT_o = nc.dram_tensor("KT_o%d" % l, [12 * 128, T], BF16).ap()
        V_o = nc.dram_tensor("V_o%d" % l, [12 * 128, 8 * 128], BF16).ap()
        lf_o = nc.dram_tensor("lf_o%d" % l, [128, 48], F32).ap()
        dS_o = nc.dram_tensor("dS_o%d" % l, [8 * 128, 256], F32).ap()
        dc_o = nc.dram_tensor("dc_o%d" % l, [128, 16], F32).ap()
        KT_g = nc.dram_tensor("KT_g%d" % l, [8 * 12 * 128, T], BF16).ap()
        V_g = nc.dram_tensor("V_g%d" % l, [8 * 12 * 128, 8 * 128], BF16).ap()
        lf_g = nc.dram_tensor("lf_g%d" % l, [8 * 128, 48], F32).ap()
        dS_g = nc.dram_tensor("dS_g%d" % l, [8 * 8 * 128, 256], F32).ap()
        dc_g = nc.dram_tensor("dc_g%d" % l, [8 * 128, 16], F32).ap()
        if l > 0:
            P.fence()
        emit_ffn(C, wa["f_wgu"], wa["f_wd"])
        emit_ln(C, l * 3 + 0)
        xo = x1_d.rearrange("(c p) t -> p c t", p=128)
        for q in range(4):
            ks = [acck(c, h) for c in range(4 * q, 4 * q + 4) for h in range(2)]
            P.dma("sp", xo[:, 4 * q:4 * q + 4, :], C.acc[:, 4 * q:4 * q + 4, :], key=("xo", q), reads=ks)
        emit_phaseA_proj(C, wa, KT_o.rearrange("(h p) t -> h p t", p=128), V_o.rearrange("(h p) (i d) -> h p i d", p=128, d=128),
                         lf_o, dS_o.rearrange("(i p) x -> i p x", p=128), dc_o)
        P.fence()
        for nm, src, dst in (("KT", KT_o, KT_g), ("V", V_o, V_g), ("lf", lf_o, lf_g), ("dS", dS_o, dS_g), ("dc", dc_o, dc_g)):
            P.op("pool", lambda e, s_=src, d_=dst: e.collective_compute("AllGather", ALU.bypass, replica_groups=RG, ins=[s_.opt()], outs=[d_.opt()]),
                 reads=[], writes=[("cc", nm)], dma_key="cc_sem")
        oT = emit_mixer(C, wb, KT_g.rearrange("(r h p) t -> r h p t", r=8, p=128),
                        V_g.rearrange("(r h p) (i d) -> r h p i d", r=8, p=128, d=128),
                        lf_g.rearrange("(r p) x -> r p x", p=128), lf_o,
                        dS_g.rearrange("(r i p) x -> r i p x", r=8, p=128), dc_g.rearrange("(r p) x -> r p x", p=128))
        emit_merge_out(C, wb, oT, x1_d)
        emit_ln(C, l * 3 + 1)
        P.fence()
        emit_ffn(C, wb["f_wgu"], wb["f_wd"])
        emit_ln(C, l * 3 + 2, want_bf=(l < DEPTH - 1))
    P.fence()
    xo = x_o.rearrange("(c p) t -> p c t", p=128)
    for q in range(4):
        ks = [acck(c, h) for c in range(4 * q, 4 * q + 4) for h in range(2)]
        P.dma("sp", xo[:, 4 * q:4 * q + 4, :], C.acc[:, 4 * q:4 * q + 4, :], key=("xo", q), reads=ks)
    P.emit()
    return nc


def _tile_fm(w):
    n = w.shape[1] // 128
    return np.ascontiguousarray(w.reshape(16, 128, n, 128).transpose(2, 1, 0, 3))


def _tile_tm(w):
    return np.ascontiguousarray(w.reshape(16, 128, w.shape[1]).transpose(1, 0, 2))


def prep_ffn(wg, wu, wd):
    g = wg.reshape(16, 128, NFC, 128).transpose(2, 1, 0, 3)
    u = wu.reshape(16, 128, NFC, 128).transpose(2, 1, 0, 3)
    wgu = np.ascontiguousarray(np.stack([g, u], axis=2))
    d = wd.reshape(NG, 4, 128, 2, 1024).transpose(0, 3, 2, 1, 4)
    return wgu, np.ascontiguousarray(d)


def prep_layer(inp, l):
    w_in = inp["w_in"][l]
    o = {}
    o["f1_wgu"], o["f1_wd"] = prep_ffn(inp["ffn1_w_gate"][l], inp["ffn1_w_up"][l], inp["ffn1_w_down"][l])
    o["f2_wgu"], o["f2_wd"] = prep_ffn(inp["ffn2_w_gate"][l], inp["ffn2_w_up"][l], inp["ffn2_w_down"][l])
    o["w_kT"] = _tile_fm(np.concatenate([w_in[:, O_FK:O_FK + 768], w_in[:, O_SK:O_SK + 768]], axis=1))
    o["w_qT"] = _tile_fm(np.concatenate([w_in[:, O_FQ:O_FQ + 768], w_in[:, O_SQ:O_SQ + 768]], axis=1))
    wv = np.concatenate([w_in[:, O_FV:O_FV + 768], w_in[:, O_SV:O_SV + 768]], axis=1)
    o["w_v"] = np.ascontiguousarray(wv.reshape(16, 128, 3, 512).transpose(2, 1, 0, 3))
    o["w_ff"] = _tile_tm(w_in[:, O_FF:O_FF + 6])
    o["bf"] = np.ascontiguousarray(inp["fox_b_f"][l].reshape(1, 6))
    o["w_gfm"] = _tile_tm(np.concatenate([w_in[:, O_GQ:O_GQ + 256], w_in[:, O_GK:O_GK + 256], w_in[:, O_GA:O_GA + 16]], axis=1))
    o["w_gtm"] = _tile_tm(w_in[:, O_GV:O_GV + 512])
    o["wa2"] = np.ascontiguousarray(inp["gla_w_a2"][l])
    o["ba"] = np.ascontiguousarray(inp["gla_b_a"][l].reshape(1, 256))
    o["ng"] = np.ascontiguousarray(inp["gla_norm_g"][l].reshape(128, 1))
    o["w_gr"] = _tile_fm(w_in[:, O_GR:O_GR + 512])
    gts = w_in[:, O_GATE:O_GATE + 3 * DM].reshape(16, 128, 3, 16, 128)
    o["w_gate"] = np.ascontiguousarray(gts.transpose(3, 1, 2, 0, 4))
    wbr = np.concatenate([inp["w_br_gla"][l], inp["w_br_fox"][l], inp["w_br_sb"][l]], axis=0)
    o["w_br"] = _tile_fm(wbr)
    o["w_out"] = _tile_fm(inp["w_out"][l])
    return o


def prep_lnp(inp):
    arr = np.zeros((128, 2 * 3 * 2 * 16), np.float32)
    for l in range(DEPTH):
        for wi, nm in enumerate(["ln1", "ln2", "ln3"]):
            li = l * 3 + wi
            arr[:, (li * 2 + 0) * 16:(li * 2 + 1) * 16] = inp[nm + "_g"][l].reshape(16, 128).T
            arr[:, (li * 2 + 1) * 16:(li * 2 + 2) * 16] = inp[nm + "_b"][l].reshape(16, 128).T
    return arr


_PROG = {}


FUSED = False


def kernel(**inputs):
    inp = {k: np.asarray(v) for k, v in inputs.items()}
    x = inp["x"][0]
    xt = x.reshape(8, 8, 128, DM)
    lnp = prep_lnp(inp)
    common = []
    for c in range(NCORES):
        cst, cbf, msk, sel = host_consts(c)
        common.append(dict(cst=cst, cbf=cbf, lnp=lnp, msk=msk, sel=sel))
    cur = [np.ascontiguousarray(xt[:, c].reshape(T, DM).T) for c in range(NCORES)]
    cores = list(range(NCORES))
    if FUSED:
        if "F" not in _PROG:
            _PROG["F"] = build_fused()
        wall = {}
        for l in range(DEPTH):
            wl = prep_layer(inp, l)
            for k in W_A:
                src = {"f_wgu": "f1_wgu", "f_wd": "f1_wd"}.get(k, k)
                wall["a%d_%s" % (l, k)] = wl[src]
            for k in W_B:
                src = {"f_wgu": "f2_wgu", "f_wd": "f2_wd"}.get(k, k)
                wall["b%d_%s" % (l, k)] = wl[src]
        in_maps = []
        for c in cores:
            m = dict(common[c], xT=cur[c])
            for nm, arr in wall.items():
                k = nm.split("_", 1)[1]
                if k in TINY_W:
                    m[nm] = arr
                else:
                    rows, cols = w2d(k, arr.shape)
                    m[nm] = arr.reshape(rows, cols)[c * (rows // NCORES):(c + 1) * (rows // NCORES)]
            in_maps.append(m)
        res = run_bass_kernel_spmd(_PROG["F"], in_maps, core_ids=cores).results
        cur = [r["x_o"] for r in res]
    else:
        for l in range(DEPTH):
            wl = prep_layer(inp, l)
            if ("A", l) not in _PROG:
                _PROG[("A", l)] = build_A(l)
            wa = {k: wl[k] for k in W_A if not k.startswith("f_")}
            wa["f_wgu"], wa["f_wd"] = wl["f1_wgu"], wl["f1_wd"]
            in_maps = [dict(common[c], xT=cur[c], **wa) for c in cores]
            resA = run_bass_kernel_spmd(_PROG[("A", l)], in_maps, core_ids=cores).results
            KT_all = np.stack([r["KT_o"] for r in resA])
            V_all = np.stack([r["V_o"] for r in resA])
            lf_all = np.stack([r["lf_o"] for r in resA])
            dS_all = np.stack([r["dS_o"] for r in resA])
            dc_all = np.stack([r["dc_o"] for r in resA])
            if ("B", l) not in _PROG:
                _PROG[("B", l)] = build_B(l)
            wb = {k: wl[k] for k in W_B if not k.startswith("f_")}
            wb["f_wgu"], wb["f_wd"] = wl["f2_wgu"], wl["f2_wd"]
            in_maps = [dict(common[c], x1=resA[c]["x1_o"], KT_all=KT_all, V_all=V_all, lf_all=lf_all, lf_own=resA[c]["lf_o"],
                            dS_all=dS_all, dc_all=dc_all, **wb) for c in cores]
            resB = run_bass_kernel_spmd(_PROG[("B", l)], in_maps, core_ids=cores).results
            cur = [r["x3_o"] for r in resB]
    out = np.zeros((8, 8, 128, DM), np.float32)
    for c in range(NCORES):
        out[:, c] = cur[c].T.reshape(8, 128, DM)
    return out.reshape(1, S, DM)
```

```python
import numpy as np
import ml_dtypes
from contextlib import ExitStack
import concourse.bass as bass
import concourse.mybir as mybir
from concourse.bass_utils import run_bass_kernel_spmd

F32 = mybir.dt.float32
BF16 = mybir.dt.bfloat16
AF = mybir.ActivationFunctionType
ALU = mybir.AluOpType

NCORES = 8
DM = 2048
S = 8192
T = 1024
DEPTH = 2
DFF = 5632
NFC = 44
NG = 11
ALPHA = (2 * DEPTH) ** 0.25
LN_EPS = 1e-5
NEG = -30000.0
O_GQ, O_GK, O_GV, O_GR, O_GA = 0, 256, 512, 1024, 1536
O_FQ, O_FK, O_FV, O_FF = 1552, 2320, 3088, 3856
O_SQ, O_SK, O_SV, O_GATE = 3862, 4630, 5398, 6166

ENGS = ("pe", "act", "dve", "pool", "sp")


class Op:
    __slots__ = ("eng", "fn", "reads", "writes", "dma_key", "deps", "has_dep", "seq", "idx")

    def __init__(self, eng, fn, reads, writes, dma_key):
        self.eng = eng
        self.fn = fn
        self.reads = reads
        self.writes = writes
        self.dma_key = dma_key
        self.deps = []
        self.has_dep = False
        self.seq = None
        self.idx = None


class Prog:
    def __init__(self, nc):
        self.nc = nc
        self.ops = []
        self.last_writer = {}
        self.readers = {}
        self.fence_op = None
        self.dram_dep = {}

    def fence(self, eng="dve"):
        keep = lambda k: isinstance(k, tuple) and k[0] == "ccw"
        keys = set(k for k in (set(self.last_writer.keys()) | set(self.readers.keys())) if not keep(k))
        o = self.op(eng, lambda e: e.engine_nop(), reads=(), writes=tuple(keys))
        self.last_writer = {k: v for k, v in self.last_writer.items() if keep(k)}
        self.readers = {k: v for k, v in self.readers.items() if keep(k)}
        self.fence_op = o
        return o

    def op(self, eng, fn, reads=(), writes=(), dma_key=None):
        o = Op(eng, fn, tuple(reads), tuple(writes), dma_key)
        o.idx = len(self.ops)
        deps = set()
        for k in o.reads:
            w = self.last_writer.get(k, None if (isinstance(k, tuple) and k[0] == "ccw") else self.fence_op)
            if w is not None:
                deps.add(w)
        for k in o.writes:
            w = self.last_writer.get(k, self.fence_op)
            if w is not None:
                deps.add(w)
            for r in self.readers.get(k, ()):
                deps.add(r)
        o.deps = sorted(deps, key=lambda d: d.idx)
        for k in o.writes:
            self.last_writer[k] = o
            self.readers[k] = []
        for k in o.reads:
            lst = self.readers.setdefault(k, [])
            if o.dma_key is None:
                lst[:] = [r for r in lst if not (r.dma_key is None and r.eng == o.eng)]
            lst.append(o)
        self.ops.append(o)
        return o

    def dma(self, eng, out, in_, key, reads=(), writes=()):
        reads = list(reads)
        try:
            nm = in_.tensor.name
            if nm in self.dram_dep:
                reads.append(self.dram_dep[nm])
        except AttributeError:
            pass
        return self.op(eng, lambda e: e.dma_start(out=out, in_=in_), reads, writes, dma_key=key)

    def emit(self):
        nc = self.nc
        ops = self.ops
        for o in ops:
            real = []
            for d in o.deps:
                if d.dma_key is None and o.dma_key is None and d.eng == o.eng and d.eng == "pe":
                    continue
                real.append(d)
            o.deps = real
            for d in real:
                d.has_dep = True
        cnt = {e: 0 for e in ENGS}
        dcnt = {}
        self.cc_inc = lambda k: 1 if (isinstance(k, str) and k.endswith("_sem")) else 16
        for o in ops:
            if o.dma_key is not None:
                dcnt[o.dma_key] = dcnt.get(o.dma_key, 0) + self.cc_inc(o.dma_key)
                o.seq = dcnt[o.dma_key]
            elif o.has_dep:
                cnt[o.eng] += 1
                o.seq = cnt[o.eng]
        with ExitStack() as st:
            esem = {e: st.enter_context(nc.semaphore("s_" + e)) for e in ENGS}
            dsem = {}
            for i, k in enumerate(dcnt.keys()):
                dsem[k] = st.enter_context(nc.semaphore("d%d" % i))
            block = st.enter_context(nc.Block())
            by_eng = {e: [o for o in ops if o.eng == e] for e in ENGS}

            def run(eng_name, eng):
                waited = {}
                for o in by_eng[eng_name]:
                    for d in o.deps:
                        if d.dma_key is not None:
                            s, v, key = dsem[d.dma_key], d.seq, ("d", d.dma_key)
                        else:
                            s, v, key = esem[d.eng], d.seq, ("e", d.eng)
                        if waited.get(key, 0) >= v:
                            continue
                        waited[key] = v
                        eng.wait_ge(s, v)
                    ins = o.fn(eng)
                    if o.dma_key is not None:
                        ins.then_inc(dsem[o.dma_key], self.cc_inc(o.dma_key))
                    elif o.has_dep:
                        ins.then_inc(esem[o.eng], 1)
                fin = {}
                for o in by_eng[eng_name]:
                    if o.dma_key is not None:
                        fin[o.dma_key] = max(fin.get(o.dma_key, 0), o.seq)
                for k, v in fin.items():
                    if waited.get(("d", k), 0) < v:
                        eng.wait_ge(dsem[k], v)

            @block.tensor
            def _(e):
                run("pe", e)

            @block.scalar
            def _(e):
                run("act", e)

            @block.vector
            def _(e):
                run("dve", e)

            @block.gpsimd
            def _(e):
                run("pool", e)

            @block.sync
            def _(e):
                run("sp", e)


class Ctx:
    pass


def _mm(C, out, lhsT, rhs, start, stop, reads, writes, skip=False):
    if skip:
        C.P.op("pe", lambda e: e.matmul(out, lhsT, rhs, start=start, stop=stop, skip_group_check=True), reads, writes)
    else:
        C.P.op("pe", lambda e: e.matmul(out, lhsT, rhs, start=start, stop=stop), reads, writes)


def _act(C, out, in_, func, reads, writes, bias=None, scale=None):
    kw = {}
    if bias is not None:
        kw["bias"] = bias
    if scale is not None:
        kw["scale"] = scale
    C.P.op("act", lambda e: e.activation(out, in_, func, **kw), reads, writes)


def _tt(C, eng, out, in0, in1, op, reads, writes):
    C.P.op(eng, lambda e: e.tensor_tensor(out, in0, in1, op), reads, writes)


def _stt(C, eng, out, in0, scalar, in1, op0, op1, reads, writes):
    C.P.op(eng, lambda e: e.scalar_tensor_tensor(out, in0, scalar, in1, op0, op1), reads, writes)


def _ts(C, eng, out, in0, s1, s2, op0, op1, reads, writes):
    if s2 is None:
        C.P.op(eng, lambda e: e.tensor_scalar(out, in0, s1, None, op0), reads, writes)
    else:
        C.P.op(eng, lambda e: e.tensor_scalar(out, in0, s1, s2, op0, op1), reads, writes)


def _cp(C, eng, out, in_, reads, writes):
    if eng == "act":
        C.P.op(eng, lambda e: e.copy(out, in_), reads, writes)
    else:
        C.P.op(eng, lambda e: e.tensor_copy(out, in_), reads, writes)


def setup_ctx(nc):
    C = Ctx()
    C.nc = nc
    C.P = Prog(nc)
    A = nc.alloc_sbuf_tensor
    C.acc = A("sb_acc", [128, 16, T], F32)
    C.xbf = A("sb_xbf", [128, 16, T], BF16)
    C.kv = A("sb_kv", [128, 16384], BF16)
    C.W = A("sb_W", [128, 20480], BF16)
    C.wk = A("sb_wk", [128, 5120], F32)
    C.cst = A("sb_cst", [128, 8, 128], F32)
    C.cbf = A("sb_cbf", [128, 4, 128], BF16)
    C.lnp = A("sb_lnp", [128, 2 * 3 * 2 * 16], F32)
    C.sel = A("sb_sel", [128, 8], F32)
    C.sm = A("sb_sm", [128, 1664], F32)
    C.ps = [nc.alloc_psum_tensor("ps%d" % i, [128, 512], F32) for i in range(8)]
    C.cnt = 0
    return C


def wk_tile(C, i):
    return C.wk[:, i * 512:(i + 1) * 512]


def wk_bf(C, i):
    return C.wk.bitcast(BF16)[:, i * 1024:(i + 1) * 1024]


def wkv(C, t, a, b):
    return C.wk[:, t * 512 + a:t * 512 + b]


def wkb(C, t, a, b):
    return C.wk.bitcast(BF16)[:, t * 1024 + a:t * 1024 + b]


def WK(t):
    return ("wk", t)


def psk(i):
    return ("ps", i)


def acck(c, h):
    return ("acc", c, h)


def xbk(c, h):
    return ("xbf", c, h)


def emit_consts(C, cst_d, cbf_d, lnp_d, msk_d, sel_d):
    P = C.P
    P.dma("sp", C.cst[:, :, :], cst_d, key="cst", writes=["cst"])
    P.dma("sp", C.cbf[:, :, :], cbf_d, key="cbf", writes=["cbf"])
    P.dma("sp", C.lnp[:, :], lnp_d, key="lnp", writes=["lnp"])
    P.dma("sp", C.sel[:, :], sel_d, key="sel", writes=["sel"])


CI_ONESD, CI_TINCL, CI_TGT, CI_ONESS, CI_MASKINC01, CI_ONES128 = 0, 1, 2, 3, 4, 5
BI_ONES, BI_ZERO, BI_MSTRICT = 0, 1, 2


def host_consts(core):
    cst = np.zeros((128, 8, 128), np.float32)
    s = np.arange(128)[:, None]
    t = np.arange(128)[None, :]
    cst[:, CI_ONESD, :] = 1.0 / DM
    cst[:, CI_TINCL, :] = np.where(s <= t, -1.0 / 16, 0.0)
    cst[:, CI_TGT, :] = np.where(s > t, -1.0 / 16, 0.0)
    cst[:, CI_ONESS, :] = -1.0 / 16
    cst[:, CI_MASKINC01, :] = np.where(s <= t, 1.0, 0.0)
    cst[:, CI_ONES128, :] = 1.0 / 128
    cst[:, 6, :] = np.where(s <= t, 1.0, 0.0)
    cst[:, 7, :] = 1.0
    cbf = np.zeros((128, 4, 128), np.float32)
    cbf[:, BI_ONES, :] = 1.0
    cbf[:, BI_MSTRICT, :] = np.where(s > t, 1.0, 0.0)
    cbf = cbf.astype(ml_dtypes.bfloat16)
    msk = np.zeros((128, 3, 8, 128), np.float32)
    for m in range(8):
        if m < core:
            inc = np.zeros((128, 128)); stn = np.zeros((128, 128)); st01 = np.ones((128, 128))
        elif m == core:
            inc = np.where(s <= t, 0.0, NEG); stn = np.where(s < t, 0.0, NEG); st01 = np.where(s < t, 1.0, 0.0)
        else:
            inc = np.full((128, 128), NEG); stn = np.full((128, 128), NEG); st01 = np.zeros((128, 128))
        msk[:, 0, m, :] = inc
        msk[:, 1, m, :] = stn
        msk[:, 2, m, :] = st01
    sel = np.zeros((128, 8), np.float32)
    sel[:, :core] = 1.0
    return cst, cbf, msk.astype(ml_dtypes.bfloat16), sel


def emit_ffn(C, wgu_d, wd_d):
    P = C.P
    Wv = C.W
    wgu = [Wv[:, s * 4096:(s + 1) * 4096].rearrange("p (a c f) -> p a c f", a=2, c=16) for s in range(3)]
    wd = [Wv[:, (3 + s) * 4096:(4 + s) * 4096].rearrange("p (j n) -> p j n", j=4) for s in range(2)]
    hT = [C.kv[:, s * 4096:(s + 1) * 4096].rearrange("p (j t) -> p j t", j=4) for s in range(2)]
    for c in range(16):
        for h in range(2):
            _ts(C, "dve", C.acc[:, c, h * 512:(h + 1) * 512], C.acc[:, c, h * 512:(h + 1) * 512], ALPHA, None, ALU.mult, None,
                reads=[acck(c, h)], writes=[acck(c, h)])
    dcount = 0
    ugcount = 0
    for g in range(NG):
        hs = g % 2
        for j in range(4):
            fc = 4 * g + j
            sl = fc % 3
            P.dma("pool", wgu[sl], wgu_d[fc], key=("wgu", sl), writes=[("W", 2 * sl), ("W", 2 * sl + 1)])
            for h in range(2):
                pi = ugcount % 2
                ugcount += 1
                psg, psu = C.ps[2 * pi], C.ps[2 * pi + 1]
                hsl = slice(h * 512, (h + 1) * 512)
                for c in range(16):
                    _mm(C, psg[:, :], wgu[sl][:, 0, c, :], C.xbf[:, c, hsl], c == 0, c == 15,
                        reads=[("W", 2 * sl), xbk(c, h)], writes=[psk(2 * pi)])
                for c in range(16):
                    _mm(C, psu[:, :], wgu[sl][:, 1, c, :], C.xbf[:, c, hsl], c == 0, c == 15,
                        reads=[("W", 2 * sl + 1), xbk(c, h)], writes=[psk(2 * pi + 1)])
                sg = wk_tile(C, pi)
                _act(C, sg, psg[:, :], AF.Silu, reads=[psk(2 * pi)], writes=[("wk", pi)])
                _tt(C, "dve", hT[hs][:, j, hsl], sg, psu[:, :], ALU.mult,
                    reads=[("wk", pi), psk(2 * pi + 1)], writes=[("hT", hs, j, h)])
        for dh in range(2):
            sl = dh
            P.dma("pool", wd[sl], wd_d[g, dh], key=("wd", sl), writes=[("W", 6 + 2 * sl), ("W", 7 + 2 * sl)])
            for dcl in range(8):
                dc = dh * 8 + dcl
                for h in range(2):
                    pb = 4 + dcount % 4
                    dcount += 1
                    hsl = slice(h * 512, (h + 1) * 512)
                    for j in range(4):
                        _mm(C, C.ps[pb][:, :], wd[sl][:, j, dcl * 128:(dcl + 1) * 128], hT[hs][:, j, hsl], j == 0, j == 3,
                            reads=[("W", 6 + 2 * sl), ("W", 7 + 2 * sl), ("hT", hs, j, h)], writes=[psk(pb)])
                    _stt(C, "dve", C.acc[:, dc, hsl], C.ps[pb][:, :], 0.5, C.acc[:, dc, hsl], ALU.mult, ALU.add,
                         reads=[psk(pb), acck(dc, h)], writes=[acck(dc, h)])


def emit_ln(C, li, want_bf=True):
    gcol = lambda c: C.lnp[:, (li * 2 + 0) * 16 + c:(li * 2 + 0) * 16 + c + 1]
    bcol = lambda c: C.lnp[:, (li * 2 + 1) * 16 + c:(li * 2 + 1) * 16 + c + 1]
    onesD = C.cst[:, CI_ONESD, :]
    for h in range(2):
        hsl = slice(h * 512, (h + 1) * 512)
        o = 0
        s1, s2, tq, mean, rstd, tmp = [wk_tile(C, o + i) for i in range(6)]
        k = lambda i: ("wk", o + i)
        _tt(C, "dve", s1, C.acc[:, 0, hsl], C.acc[:, 1, hsl], ALU.add, reads=[acck(0, h), acck(1, h)], writes=[k(0)])
        for c in range(2, 16):
            _tt(C, "dve", s1, s1, C.acc[:, c, hsl], ALU.add, reads=[k(0), acck(c, h)], writes=[k(0)])
        _act(C, s2, C.acc[:, 0, hsl], AF.Square, reads=[acck(0, h)], writes=[k(1)])
        for c in range(1, 16):
            _act(C, tq, C.acc[:, c, hsl], AF.Square, reads=[acck(c, h)], writes=[k(2)])
            _tt(C, "dve", s2, s2, tq, ALU.add, reads=[k(1), k(2)], writes=[k(1)])
        pm, pe2 = 2 * h, 2 * h + 1
        _mm(C, C.ps[pm][:, :], onesD, s1, True, True, reads=["cst", k(0)], writes=[psk(pm)])
        _mm(C, C.ps[pe2][:, :], onesD, s2, True, True, reads=["cst", k(1)], writes=[psk(pe2)])
        _cp(C, "act", mean, C.ps[pm][:, :], reads=[psk(pm)], writes=[k(3)])
        _tt(C, "dve", tmp, mean, mean, ALU.mult, reads=[k(3)], writes=[k(5)])
        _tt(C, "dve", tmp, C.ps[pe2][:, :], tmp, ALU.subtract, reads=[psk(pe2), k(5)], writes=[k(5)])
        _ts(C, "dve", tmp, tmp, LN_EPS, None, ALU.add, None, reads=[k(5)], writes=[k(5)])
        _act(C, tmp, tmp, AF.Sqrt, reads=[k(5)], writes=[k(5)])
        C.P.op("dve", lambda e, a=rstd, b=tmp: e.reciprocal(a, b), reads=[k(5)], writes=[k(4)])
        for c in range(16):
            tb = [s1, s2][c % 2]
            kb = [k(0), k(1)][c % 2]
            _tt(C, "dve", tb, C.acc[:, c, hsl], mean, ALU.subtract, reads=[acck(c, h), k(3)], writes=[kb])
            _tt(C, "dve", tb, tb, rstd, ALU.mult, reads=[kb, k(4)], writes=[kb])
            _act(C, C.acc[:, c, hsl], tb, AF.Identity, reads=[kb, "lnp"], writes=[acck(c, h)], bias=bcol(c), scale=gcol(c))
            if want_bf:
                _cp(C, "dve", C.xbf[:, c, hsl], C.acc[:, c, hsl], reads=[acck(c, h)], writes=[xbk(c, h)])


WALL = [("W", i) for i in range(12)]


def emit_gla_load_weights(C, wl):
    P = C.P
    Wv = C.W
    C.wgfm = Wv[:, 0:16 * 528].rearrange("p (c n) -> p c n", c=16)
    C.wgtm = Wv[:, 10240:10240 + 16 * 512].rearrange("p (c n) -> p c n", c=16)
    P.dma("pool", C.wgfm, wl["w_gfm"], key="wgfm", writes=[("W", i) for i in range(5)])
    P.dma("pool", C.wgtm, wl["w_gtm"], key="wgtm", writes=[("W", i) for i in range(5, 9)])
    C.wa2 = C.sm[0:16, 0:256]
    C.babc = C.sm[:, 256:512]
    P.dma("sp", C.wa2, wl["wa2"], key="wa2", writes=["wa2"])
    P.dma("sp", C.babc, wl["ba"].partition_broadcast(128), key="babc", writes=["babc"])
    C.k_wgfm = [("W", i) for i in range(5)]
    C.k_wgtm = [("W", i) for i in range(5, 9)]


def emit_gla_tile(C, i, need_q):
    tsl = slice(i * 128, (i + 1) * 128)
    gaT = C.wk[0:16, 0:128]
    dcol = wkv(C, 0, 128, 130)
    xa = wkv(C, 1, 0, 256)
    lap = wkv(C, 1, 256, 512)
    ek = wkv(C, 2, 0, 256)
    kf = wkv(C, 2, 256, 512)
    ebT = wkv(C, 3, 0, 256)
    enbT = wkv(C, 3, 256, 512)
    qT = wkv(C, 4, 0, 256)
    kT = wkv(C, 4, 256, 512)
    khat = wkb(C, 5, 0, 256)
    vtm = wkb(C, 5, 256, 768)
    qtil = wkb(C, 8, 0, 256)
    ktil = wkb(C, 8, 256, 512)
    ps0, ps1, ps2, ps3 = C.ps[0], C.ps[1], C.ps[2], C.ps[3]
    xr = [xbk(c, i // 4) for c in range(16)]
    for u in range(4):
        if u < 2 and not need_q:
            continue
        for c in range(16):
            _mm(C, ps0[:, u * 128:(u + 1) * 128], C.wgfm[:, c, u * 128:(u + 1) * 128], C.xbf[:, c, tsl], c == 0, c == 15,
                reads=C.k_wgfm + xr, writes=[psk(0)])
    for c in range(16):
        _mm(C, ps1[0:16, 0:128], C.wgfm[:, c, 512:528], C.xbf[:, c, tsl], c == 0, c == 15, reads=C.k_wgfm + xr, writes=[psk(1)])
    for c in range(16):
        _mm(C, ps2[:, 0:256], C.xbf[:, c, tsl], C.wgfm[:, c, 256:512], c == 0, c == 15, reads=C.k_wgfm + xr, writes=[psk(2)])
    for c in range(16):
        _mm(C, ps3[:, :], C.xbf[:, c, tsl], C.wgtm[:, c, :], c == 0, c == 15, reads=C.k_wgtm + xr, writes=[psk(3)])
    _cp(C, "act", gaT, ps1[0:16, 0:128], reads=[psk(1)], writes=[WK(0)])
    if need_q:
        _act(C, qT, ps0[:, 0:256], AF.Identity, reads=[psk(0)], writes=[WK(4)], scale=0.125)
    _cp(C, "dve", kT, ps0[:, 256:512], reads=[psk(0)], writes=[WK(4)])
    _cp(C, "dve", kf, ps2[:, 0:256], reads=[psk(2)], writes=[WK(2)])
    _cp(C, "act", vtm, ps3[:, :], reads=[psk(3)], writes=[WK(5)])
    _mm(C, ps1[:, 256:512], gaT, C.wa2, True, True, reads=[WK(0), "wa2"], writes=[psk(1)])
    _tt(C, "dve", xa, ps1[:, 256:512], C.babc, ALU.add, reads=[psk(1), "babc"], writes=[WK(1)])
    _act(C, xa, xa, AF.Exp, reads=[WK(1)], writes=[WK(1)], scale=-1.0)
    _act(C, lap, xa, AF.Ln, reads=[WK(1)], writes=[WK(1)], bias=1.0)
    _mm(C, ps2[:, 256:512], C.cst[:, CI_TGT, :], lap, True, True, reads=["cst", WK(1)], writes=[psk(2)])
    _act(C, ek, ps2[:, 256:512], AF.Exp, reads=[psk(2)], writes=[WK(2)])
    _tt(C, "dve", khat, kf, ek, ALU.mult, reads=[WK(2)], writes=[WK(5)])
    for kc in range(2):
        _mm(C, ps1[:, 128 + kc:129 + kc], lap[:, kc * 128:(kc + 1) * 128], C.cst[:, CI_ONESS, 0:1], True, True,
            reads=["cst", WK(1)], writes=[psk(1)])
    _act(C, dcol, ps1[:, 128:130], AF.Exp, reads=[psk(1)], writes=[WK(0)])
    res = dict(khat=khat, vtm=vtm, dcol=dcol)
    if need_q:
        for kc in range(2):
            _mm(C, ps0[:, kc * 128:(kc + 1) * 128], lap[:, kc * 128:(kc + 1) * 128], C.cst[:, CI_TINCL, :], True, True,
                reads=["cst", WK(1)], writes=[psk(0)])
        _act(C, ebT, ps0[:, 0:256], AF.Exp, reads=[psk(0)], writes=[WK(3)])
        _act(C, enbT, ps0[:, 0:256], AF.Exp, reads=[psk(0)], writes=[WK(3)], scale=-1.0)
        _tt(C, "dve", qtil, qT, ebT, ALU.mult, reads=[WK(4), WK(3)], writes=[WK(8)])
        _tt(C, "dve", ktil, kT, enbT, ALU.mult, reads=[WK(4), WK(3)], writes=[WK(8)])
        res.update(qtil=qtil, ktil=ktil)
    return res


def emit_gla_dS(C, r, psa, psb):
    for kc in range(2):
        for hh in range(2):
            hd = 2 * kc + hh
            pb = psa if hh == 0 else psb
            _mm(C, C.ps[pb][:, kc * 128:(kc + 1) * 128], r["khat"][:, kc * 128:(kc + 1) * 128], r["vtm"][:, hd * 128:(hd + 1) * 128],
                True, True, reads=[WK(5)], writes=[psk(pb)])


def emit_phaseA_proj(C, wl, KT_o, V_o, lf_o, dS_o, dc_o):
    P = C.P
    Wv = C.W
    P.fence()
    wslot = [Wv[:, s * 2048:(s + 1) * 2048].rearrange("p (c f) -> p c f", c=16) for s in range(2)]
    for kc in range(12):
        sl = kc % 2
        P.dma("pool", wslot[sl], wl["w_kT"][kc], key=("Wm", sl), writes=[("W", sl)])
        for h in range(2):
            pb = (2 * kc + h) % 4
            hsl = slice(h * 512, (h + 1) * 512)
            for c in range(16):
                _mm(C, C.ps[pb][:, :], wslot[sl][:, c, :], C.xbf[:, c, hsl], c == 0, c == 15,
                    reads=[("W", sl), xbk(c, h)], writes=[psk(pb)])
            ot = wk_bf(C, pb)[:, 0:512]
            _cp(C, "act", ot, C.ps[pb][:, :], reads=[psk(pb)], writes=[WK(pb)])
            P.dma("sp", KT_o[kc, :, hsl], ot, key=("kto", pb), reads=[WK(pb)])
    wv = Wv[:, 4096:4096 + 8192].rearrange("p (c n) -> p c n", c=16)
    kwv = [("W", i) for i in range(2, 6)]
    for vt in range(3):
        P.dma("pool", wv, wl["w_v"][vt], key="wv", writes=kwv)
        for i in range(8):
            pb = 4 + (vt * 8 + i) % 4
            tsl = slice(i * 128, (i + 1) * 128)
            for c in range(16):
                _mm(C, C.ps[pb][:, :], C.xbf[:, c, tsl], wv[:, c, :], c == 0, c == 15,
                    reads=kwv + [xbk(c, i // 4)], writes=[psk(pb)])
            ot = wk_bf(C, pb)[:, 0:512]
            _cp(C, "dve", ot, C.ps[pb][:, :], reads=[psk(pb)], writes=[WK(pb)])
            P.dma("sp", V_o[vt * 4:(vt + 1) * 4, :, i, :].rearrange("h p d -> p h d"), ot.rearrange("p (h d) -> p h d", h=4),
                  key=("vo", pb), reads=[WK(pb)])
    wff = Wv[:, 12288:12288 + 96].rearrange("p (c n) -> p c n", c=16)
    P.dma("pool", wff, wl["w_ff"], key="wff", writes=[("W", 6)])
    bfbc = C.sm[:, 512:518]
    P.dma("sp", bfbc, wl["bf"].partition_broadcast(128), key="bfbc", writes=["bfbc"])
    for i in range(8):
        tsl = slice(i * 128, (i + 1) * 128)
        for c in range(16):
            _mm(C, C.ps[0][:, i * 6:(i + 1) * 6], C.xbf[:, c, tsl], wff[:, c, :], c == 0, c == 15,
                reads=[("W", 6), xbk(c, i // 4)], writes=[psk(0)])
    lft = C.sm[:, 520:568]
    for i in range(8):
        _tt(C, "dve", lft[:, i * 6:(i + 1) * 6], C.ps[0][:, i * 6:(i + 1) * 6], bfbc, ALU.add, reads=[psk(0), "bfbc"], writes=["lft"])
    _act(C, lft, lft, AF.Exp, reads=["lft"], writes=["lft"], scale=-1.0)
    _act(C, lft, lft, AF.Ln, reads=["lft"], writes=["lft"], bias=1.0)
    _ts(C, "dve", lft, lft, -1.0, None, ALU.mult, None, reads=["lft"], writes=["lft"])
    P.dma("sp", lf_o, lft, key="lfo", reads=["lft"])
    P.fence()
    emit_gla_load_weights(C, wl)
    for i in range(8):
        r = emit_gla_tile(C, i, need_q=False)
        emit_gla_dS(C, r, 4, 5)
        dst = C.sm[:, 1024 + (i % 2) * 256:1024 + (i % 2) * 256 + 256]
        kk = ("dSs", i % 2)
        for hh in range(2):
            psl = slice(hh * 64, (hh + 1) * 64)
            _cp(C, "dve", dst[psl, :], C.ps[4 + hh][psl, 0:256], reads=[psk(4 + hh)], writes=[kk])
        P.dma("sp", dS_o[i], dst, key=("dSo", i % 2), reads=[kk])
        dcs = C.sm[:, 1536 + 2 * i:1538 + 2 * i]
        _cp(C, "dve", dcs, r["dcol"], reads=[WK(0)], writes=["dcs"])
    P.dma("sp", dc_o, C.sm[:, 1536:1552], key="dco", reads=["dcs"])


def emit_mixer(C, wl, KT_all, V_all, lf_all, lfown_d, dS_all, dc_all):
    P = C.P
    Wv = C.W
    accb = C.acc.bitcast(BF16)
    oT = accb[:, 0:8, :].rearrange("p a (b t) -> p (a b) t", b=2)
    rowf = lambda r: C.acc[:, 8 + r, :]
    rowb = lambda r: accb[:, 8 + r, :]
    P.fence()
    emit_gla_load_weights(C, wl)
    Sf = rowf(0)[:, 0:256]
    Sg = rowf(0)[:, 256:512]
    tmpS = rowf(0)[:, 512:768]
    Sbf_all = rowb(1)
    QT = rowb(2)[:, 0:1024]
    gb = C.acc[:, 11:13, :].rearrange("p a t -> p (a t)")
    Aq = rowf(5)
    r67 = C.acc[:, 14:16, :].rearrange("p a t -> p (a t)")
    C.msk = r67[:, 512:2048].bitcast(BF16).rearrange("p (a m t) -> p a m t", a=3, m=8)
    P.dma("sp", C.msk, C.msk_d, key="msk", writes=["msk"])
    dcg = C.sm[:, 1024:1024 + 128].rearrange("p (r i k) -> p r i k", r=8, i=8)
    P.dma("sp", C.sm[:, 1024:1024 + 128].rearrange("p (r x) -> p r x", r=8), dc_all.rearrange("r p x -> p r x"), key="dcg", writes=["dcg"])
    P.op("dve", lambda e: e.memset(Sf, 0.0), reads=[], writes=["Sf"])
    oms = C.sm[:, 1160:1168]
    _ts(C, "dve", oms, C.sel[:, :], -1.0, 1.0, ALU.mult, ALU.add, reads=["sel"], writes=["oms"])
    dm = C.sm[:, 1168:1170]
    for i in range(8 if "scan" in C.stages else 0):
        for m in range(8):
            P.dma("sp", gb[:, m * 256:(m + 1) * 256], dS_all[m, i], key=("dSl", m), writes=[("gb", m)])
        _cp(C, "dve", Sg, Sf, reads=["Sf"], writes=["Sg"])
        for m in range(8):
            _stt(C, "dve", dm, dcg[:, m, i, :], C.sel[:, m:m + 1], oms[:, m:m + 1].to_broadcast([128, 2]), ALU.mult, ALU.add,
                 reads=["dcg", "sel", "oms"], writes=["dm"])
            _ts(C, "dve", tmpS, gb[:, m * 256:(m + 1) * 256], C.sel[:, m:m + 1], None, ALU.mult, None,
                reads=[("gb", m), "sel"], writes=["tmpS"])
            for kc in range(2):
                ksl = slice(kc * 128, (kc + 1) * 128)
                _stt(C, "dve", Sg[:, ksl], Sg[:, ksl], dm[:, kc:kc + 1], tmpS[:, ksl], ALU.mult, ALU.add,
                     reads=["Sg", "dm", "tmpS"], writes=["Sg"])
                _stt(C, "dve", Sf[:, ksl], Sf[:, ksl], dcg[:, m, i, kc:kc + 1], gb[:, m * 256 + kc * 128:m * 256 + (kc + 1) * 128],
                     ALU.mult, ALU.add, reads=["Sf", "dcg", ("gb", m)], writes=["Sf"])
        _cp(C, "dve", Sbf_all[:, i * 256:(i + 1) * 256], Sg, reads=["Sg"], writes=[("Sbf", i)])
    ng = C.sm[:, 1170:1171]
    P.dma("sp", ng, wl["ng"], key="ng", writes=["ng"])
    for i in range(8 if "tiles" in C.stages else 0):
        r = emit_gla_tile(C, i, need_q=True)
        tsl = slice(i * 128, (i + 1) * 128)
        attm = wkb(C, 9, 0, 512)
        if "tiles1" in C.stages:
            continue
        ab = lambda hh: 4 if hh == 0 else 7
        ob = lambda hh: 5 if hh == 0 else 6
        for hd in range(4):
            kc, hh = hd // 2, hd % 2
            psl = slice(hh * 64, (hh + 1) * 64)
            _mm(C, C.ps[ab(hh)][:, hd * 128:(hd + 1) * 128], r["ktil"][psl, kc * 128:(kc + 1) * 128], r["qtil"][psl, kc * 128:(kc + 1) * 128],
                True, True, reads=[WK(8)], writes=[psk(ab(hh))])
        for hd in range(4):
            hh = hd % 2
            _tt(C, "dve", attm[:, hd * 128:(hd + 1) * 128], C.ps[ab(hh)][:, hd * 128:(hd + 1) * 128], C.cst[:, CI_MASKINC01, :], ALU.mult,
                reads=[psk(ab(hh)), "cst"], writes=[WK(9)])
        if "t_att" in C.stages:
            continue
        for hd in range(4):
            kc, hh = hd // 2, hd % 2
            psl = slice(hh * 64, (hh + 1) * 64)
            _mm(C, C.ps[ob(hh)][:, hd * 128:(hd + 1) * 128], r["vtm"][:, hd * 128:(hd + 1) * 128], attm[:, hd * 128:(hd + 1) * 128],
                True, False, reads=[WK(5), WK(9)], writes=[psk(ob(hh))])
            _mm(C, C.ps[ob(hh)][:, hd * 128:(hd + 1) * 128], Sbf_all[psl, i * 256 + kc * 128:i * 256 + (kc + 1) * 128],
                r["qtil"][psl, kc * 128:(kc + 1) * 128], False, True, reads=[("Sbf", i), WK(8)], writes=[psk(ob(hh))])
        if "t_o" in C.stages:
            continue
        sq = wk_tile(C, 6)
        for hd in range(4):
            hh = hd % 2
            _act(C, sq[:, hd * 128:(hd + 1) * 128], C.ps[ob(hh)][:, hd * 128:(hd + 1) * 128], AF.Square, reads=[psk(ob(hh))], writes=[WK(6)])
        _mm(C, C.ps[3][:, :], C.cst[:, CI_ONES128, :], sq, True, True, reads=["cst", WK(6)], writes=[psk(3)])
        rs = wk_tile(C, 7)
        _ts(C, "dve", rs, C.ps[3][:, :], LN_EPS, None, ALU.add, None, reads=[psk(3)], writes=[WK(7)])
        _act(C, rs, rs, AF.Sqrt, reads=[WK(7)], writes=[WK(7)])
        P.op("dve", lambda e, a=rs: e.reciprocal(a, a), reads=[WK(7)], writes=[WK(7)])
        for hd in range(4):
            hh = hd % 2
            _tt(C, "dve", rs[:, hd * 128:(hd + 1) * 128], C.ps[ob(hh)][:, hd * 128:(hd + 1) * 128], rs[:, hd * 128:(hd + 1) * 128], ALU.mult,
                reads=[psk(ob(hh)), WK(7)], writes=[WK(7)])
        for hd in range(4):
            _ts(C, "dve", oT[:, hd, tsl], rs[:, hd * 128:(hd + 1) * 128], ng, None, ALU.mult, None,
                reads=[WK(7), "ng"], writes=[("oTg", hd, i)])
    P.fence()
    wslot = [Wv[:, s * 2048:(s + 1) * 2048].rearrange("p (c f) -> p c f", c=16) for s in range(10)]
    for hd in range(4 if "gr" in C.stages else 0):
        sl = hd % 2
        P.dma("pool", wslot[sl], wl["w_gr"][hd], key=("Wm", sl), writes=[("W", sl)])
        for h in range(2):
            pb = (2 * hd + h) % 4
            hsl = slice(h * 512, (h + 1) * 512)
            for c in range(16):
                _mm(C, C.ps[pb][:, :], wslot[sl][:, c, :], C.xbf[:, c, hsl], c == 0, c == 15, reads=[("W", sl), xbk(c, h)], writes=[psk(pb)])
            sg = wk_tile(C, pb)
            _act(C, sg, C.ps[pb][:, :], AF.Silu, reads=[psk(pb)], writes=[WK(pb)])
            _tt(C, "dve", oT[:, hd, hsl], oT[:, hd, hsl], sg, ALU.mult, reads=[WK(pb), ("oTh", hd, h)], writes=[("oTh", hd, h)])

    if "heads" not in C.stages:
        return oT
    lfn4 = C.sm[:, 568:568 + 384].rearrange("p (i r h) -> p i r h", i=8, r=8)
    for r_ in range(8):
        P.dma("sp", lfn4[:, :, r_, :], lf_all[r_].rearrange("p (i h) -> p i h", h=6), key=("lfn", r_), writes=["lfn"])
    lff = C.sm[:, 568:568 + 384]
    ea = r67[:, 0:384]
    eb = wk_tile(C, 9)[:, 0:384]
    _cp(C, "dve", ea, lff, reads=["lfn"], writes=["ea"])
    src, dst, ks, kd = ea, eb, "ea", WK(9)
    sh = 1
    while sh < 64:
        n = sh * 6
        _cp(C, "dve", dst[:, 0:n], src[:, 0:n], reads=[ks], writes=[kd])
        _tt(C, "dve", dst[:, n:384], src[:, n:384], src[:, 0:384 - n], ALU.add, reads=[ks], writes=[kd])
        src, dst, ks, kd = dst, src, kd, ks
        sh *= 2
    _tt(C, "dve", dst, src, lff, ALU.subtract, reads=[ks, "lfn"], writes=[kd])
    E, kE = dst, kd
    _mm(C, C.ps[0][:, 0:384], C.cst[:, 6, :], lff, True, False, reads=["cst", "lfn"], writes=[psk(0)])
    _mm(C, C.ps[0][:, 0:384], C.cst[:, 7, :], E, False, True, reads=["cst", kE], writes=[psk(0)])
    Eown = C.sm[:, 1224:1272].rearrange("p (i h) -> p i h", h=6)
    E4 = E.rearrange("p (i r h) -> p i r h", i=8, r=8)
    _cp(C, "dve", Eown, E4[:, :, 0, :], reads=[kE], writes=["Eown"])
    for m in range(8):
        _stt(C, "dve", Eown, lfn4[:, :, m, :], C.sel[:, m:m + 1], Eown, ALU.mult, ALU.add, reads=["lfn", "sel", "Eown"], writes=["Eown"])
    nctm = C.sm[:, 1272:1272 + 384].rearrange("p (j h) -> p j h", h=6)
    _ts(C, "dve", C.sm[:, 1272:1272 + 384], C.ps[0][:, 0:384], -1.0, None, ALU.mult, None, reads=[psk(0)], writes=["nctm"])
    lfo = C.sm[:, 1176:1224].rearrange("p (i h) -> p i h", h=6)
    P.dma("sp", C.sm[:, 1176:1224], lfown_d, key="lfo", writes=["lfo"])

    P.fence()
    KTs = C.kv[:, 0:8192]
    Vs = C.kv[:, 8192:16384]
    for hh in range(12):
        is_sb = hh >= 6
        hd = hh % 6
        osl = 4 + hh
        P.dma("sp", KTs.rearrange("p (r t) -> p r t", r=8), KT_all[:, hh].rearrange("r d t -> d r t"), key="ktl", writes=["KTs"])
        P.dma("sp", Vs.rearrange("p (r x) -> p r x", r=8), V_all[:, hh].rearrange("r p i d -> p r (i d)"), key="vl", writes=["Vs"])
        sl = hh % 2
        P.dma("pool", wslot[sl], wl["w_qT"][hh], key=("Wm", sl), writes=[("W", sl)])
        for h in range(2):
            hsl = slice(h * 512, (h + 1) * 512)
            for c in range(16):
                _mm(C, C.ps[h][:, :], wslot[sl][:, c, :], C.xbf[:, c, hsl], c == 0, c == 15, reads=[("W", sl), xbk(c, h)], writes=[psk(h)])
            _act(C, QT[:, hsl], C.ps[h][:, :], AF.Identity, reads=[psk(h)], writes=[("QT", h)], scale=128 ** -0.5)
        if not is_sb:
            for i in range(8):
                lb_ = wkv(C, 4 + i % 2, 0, 128)
                eb_ = wkv(C, 4 + i % 2, 128, 256)
                _cp(C, "dve", lb_, lfo[:, i, hd:hd + 1].to_broadcast([128, 128]), reads=["lfo"], writes=[WK(4 + i % 2)])
                _cp(C, "dve", eb_, Eown[:, i, hd:hd + 1].to_broadcast([128, 128]), reads=["Eown"], writes=[WK(4 + i % 2)])
                pa = 2 + i // 4
                osl_ = slice((i % 4) * 128, (i % 4 + 1) * 128)
                _mm(C, C.ps[pa][:, osl_], lb_, C.cst[:, 6, :], True, False, reads=[WK(4 + i % 2), "cst"], writes=[psk(pa)])
                _mm(C, C.ps[pa][:, osl_], eb_, C.cst[:, 7, :], False, True, reads=[WK(4 + i % 2), "cst"], writes=[psk(pa)])
            for h in range(2):
                _cp(C, "act", Aq[:, h * 512:(h + 1) * 512], C.ps[2 + h][:, :], reads=[psk(2 + h)], writes=[("Aq", h)])
        for ch in range(2):
            lo = ch * 512
            jmax = 32 if ch == 0 else 64
            steps = []
            for j in range(jmax):
                cs = max(lo, 128 * (j // 8))
                steps.append((j, cs, lo + 512, 128 * (j // 8) >= lo))
            if is_sb:
                steps = steps[::-1]
            po, pl = 6, 7
            _mm(C, C.ps[po][:, :], C.cbf[:, BI_ZERO, :], QT[:, lo:lo + 512], True, False, reads=["cbf", ("QT", ch)], writes=[psk(po)], skip=True)
            _mm(C, C.ps[pl][:, :], C.cbf[:, BI_ZERO, :], QT[:, lo:lo + 512], True, False, reads=["cbf", ("QT", ch)], writes=[psk(pl)], skip=True)
            for si, (j, cs, ce, diag) in enumerate(steps):
                r_, i_ = j % 8, j // 8
                m = r_
                n = ce - cs
                kt = KTs[:, r_ * 1024 + i_ * 128: r_ * 1024 + (i_ + 1) * 128]
                vt = Vs[:, r_ * 1024 + i_ * 128: r_ * 1024 + (i_ + 1) * 128]
                dd = si % 4
                pz = (0, 1, 4, 5)[dd]
                hkey = ("wkh", 8 + dd // 2, dd % 2)
                hbuf = wkb(C, 8 + dd // 2, (dd % 2) * 512, (dd % 2) * 512 + n)
                cl = slice(cs - lo, ce - lo)
                last = si == len(steps) - 1
                _mm(C, C.ps[pz][:, 0:n], kt, QT[:, cs:ce], True, True, reads=["KTs", ("QT", ch)], writes=[psk(pz)])
                if not is_sb:
                    tt_ = wk_tile(C, dd)[:, 0:n]
                    pt = hbuf
                    _tt(C, "dve", tt_, C.ps[pz][:, 0:n], Aq[:, cs:ce], ALU.add, reads=[psk(pz), ("Aq", ch)], writes=[WK(dd)])
                    if diag:
                        _tt(C, "dve", tt_[:, 0:128], tt_[:, 0:128], C.msk[:, 0, m, :], ALU.add, reads=[WK(dd), "msk"], writes=[WK(dd)])
                    _act(C, pt, tt_, AF.Exp, reads=[WK(dd), "nctm"], writes=[hkey], bias=nctm[:, j, hd:hd + 1])
                    _mm(C, C.ps[po][:, cl], vt, pt, False, last, reads=["Vs", hkey], writes=[psk(po)], skip=True)
                    _mm(C, C.ps[pl][:, cl], C.cbf[:, BI_ONES, :], pt, False, last, reads=["cbf", hkey], writes=[psk(pl)], skip=True)
                else:
                    ta, tb = 2 * dd, 2 * dd + 1
                    e_ = wk_tile(C, ta)[:, 0:n]
                    zl = wk_tile(C, tb)[:, 0:n]
                    lb = hbuf
                    _act(C, e_, C.ps[pz][:, 0:n], AF.Exp, reads=[psk(pz)], writes=[WK(ta)])
                    _act(C, e_, e_, AF.Ln, reads=[WK(ta)], writes=[WK(ta)], bias=1.0)
                    _act(C, lb, e_, AF.Identity, reads=[WK(ta)], writes=[hkey], scale=-1.0)
                    _tt(C, "dve", zl, C.ps[pz][:, 0:n], e_, ALU.subtract, reads=[psk(pz), WK(ta)], writes=[WK(tb)])
                    if diag:
                        _tt(C, "dve", lb[:, 0:128], lb[:, 0:128], C.msk[:, 2, m, :], ALU.mult, reads=[hkey, "msk"], writes=[hkey])
                        _tt(C, "dve", zl[:, 0:128], zl[:, 0:128], C.msk[:, 1, m, :], ALU.add, reads=[WK(tb), "msk"], writes=[WK(tb)])
                    pr = 2 + si % 2
                    _mm(C, C.ps[pr][:, 0:n], C.cbf[:, BI_MSTRICT, :], lb, True, True, reads=["cbf", hkey], writes=[psk(pr)])
                    _tt(C, "dve", zl, C.ps[pr][:, 0:n], zl, ALU.add, reads=[psk(pr), WK(tb)], writes=[WK(tb)])
                    _tt(C, "dve", zl, C.ps[pl][:, cl], zl, ALU.add, reads=[psk(pl), WK(tb)], writes=[WK(tb)])
                    _mm(C, C.ps[pl][:, cl], C.cbf[:, BI_ONES, :], lb, False, last, reads=["cbf", hkey], writes=[psk(pl)], skip=True)
                    _act(C, lb, zl, AF.Exp, reads=[WK(tb)], writes=[hkey])
                    _mm(C, C.ps[po][:, cl], vt, lb, False, last, reads=["Vs", hkey], writes=[psk(po)], skip=True)
            if not is_sb:
                rl = wk_tile(C, 3)
                P.op("dve", lambda e, a=rl, b=C.ps[pl][:, :]: e.reciprocal(a, b), reads=[psk(pl)], writes=[WK(3)])
                _tt(C, "dve", oT[:, osl, lo:lo + 512], C.ps[po][:, :], rl, ALU.mult, reads=[psk(po), WK(3)], writes=[("oTh", osl, ch)])
            else:
                _cp(C, "dve", oT[:, osl, lo:lo + 512], C.ps[po][:, :], reads=[psk(po)], writes=[("oTh", osl, ch)])
    return oT


def emit_merge_out(C, wl, oT, x1_d):
    P = C.P
    Wv = C.W
    P.fence()
    mT = C.kv.rearrange("p (c t) -> p c t", c=16)
    wgate = [Wv[:, s * 6144:(s + 1) * 6144].rearrange("p (b c f) -> p b c f", b=3, c=16) for s in range(2)]
    wbr = [Wv[:, 12288 + s * 2048:12288 + (s + 1) * 2048].rearrange("p (c f) -> p c f", c=16) for s in range(2)]
    wout = [Wv[:, 16384 + s * 2048:16384 + (s + 1) * 2048].rearrange("p (c f) -> p c f", c=16) for s in range(2)]
    cnt = 0
    for dc in range(16):
        s = dc % 2
        kg = [("W", 3 * s + q) for q in range(3)]
        P.dma("pool", wgate[s], wl["w_gate"][dc], key=("wg", s), writes=kg)
        P.dma("pool", wbr[s], wl["w_br"][dc], key=("wb", s), writes=[("W", 6 + s)])
        for h in range(2):
            hsl = slice(h * 512, (h + 1) * 512)
            q = cnt % 2
            cnt += 1
            gb0 = 3 * q
            for b in range(3):
                for c in range(16):
                    _mm(C, C.ps[gb0 + b][:, :], wgate[s][:, b, c, :], C.xbf[:, c, hsl], c == 0, c == 15,
                        reads=kg + [xbk(c, h)], writes=[psk(gb0 + b)])
                _act(C, wk_tile(C, gb0 + b), C.ps[gb0 + b][:, :], AF.Sigmoid, reads=[psk(gb0 + b)], writes=[WK(gb0 + b)])
            tA = wk_tile(C, 6 + 2 * q)
            tB = wk_tile(C, 7 + 2 * q)
            branches = [(0, range(0, 4)), (1, range(4, 10)), (2, range(10, 16))]
            for b, chs in branches:
                pb = 6 + (b % 2)
                chs = list(chs)
                for ci, ch_ in enumerate(chs):
                    _mm(C, C.ps[pb][:, :], wbr[s][:, ch_, :], oT[:, ch_, hsl], ci == 0, ci == len(chs) - 1,
                        reads=[("W", 6 + s), ("oTh", ch_, h)], writes=[psk(pb)])
                if b == 0:
                    _tt(C, "dve", tA, C.ps[pb][:, :], wk_tile(C, gb0 + b), ALU.mult, reads=[psk(pb), WK(gb0 + b)], writes=[WK(6 + 2 * q)])
                elif b == 1:
                    _tt(C, "dve", tB, C.ps[pb][:, :], wk_tile(C, gb0 + b), ALU.mult, reads=[psk(pb), WK(gb0 + b)], writes=[WK(7 + 2 * q)])
                    _tt(C, "dve", tA, tA, tB, ALU.add, reads=[WK(6 + 2 * q), WK(7 + 2 * q)], writes=[WK(6 + 2 * q)])
                else:
                    _tt(C, "dve", tB, C.ps[pb][:, :], wk_tile(C, gb0 + b), ALU.mult, reads=[psk(pb), WK(gb0 + b)], writes=[WK(7 + 2 * q)])
                    _tt(C, "dve", mT[:, dc, hsl], tA, tB, ALU.add, reads=[WK(6 + 2 * q), WK(7 + 2 * q)], writes=[("mT", dc, h)])
    P.fence()
    for dc in range(16):
        P.dma("sp", C.acc[:, dc, :], x1_d[dc * 128:(dc + 1) * 128, :], key=("x1l", dc), writes=[acck(dc, 0), acck(dc, 1)])
    cnt = 0
    for dc in range(16):
        s = dc % 2
        P.dma("pool", wout[s], wl["w_out"][dc], key=("wo", s), writes=[("W", 8 + s)])
        for h in range(2):
            hsl = slice(h * 512, (h + 1) * 512)
            pb = cnt % 4
            cnt += 1
            for c in range(16):
                _mm(C, C.ps[pb][:, :], wout[s][:, c, :], mT[:, c, hsl], c == 0, c == 15, reads=[("W", 8 + s), ("mT", c, h)], writes=[psk(pb)])
            _stt(C, "dve", C.acc[:, dc, hsl], C.acc[:, dc, hsl], ALPHA, C.ps[pb][:, :], ALU.mult, ALU.add,
                 reads=[acck(dc, h), psk(pb)], writes=[acck(dc, h)])
    P.fence()


W_A = dict(f_wgu=[NFC, 128, 2, 16, 128], f_wd=[NG, 2, 128, 4, 1024], w_kT=[12, 128, 16, 128], w_v=[3, 128, 16, 512],
           w_ff=[128, 16, 6], bf=[1, 6], w_gfm=[128, 16, 528], w_gtm=[128, 16, 512], wa2=[16, 256], ba=[1, 256])
W_B = dict(f_wgu=[NFC, 128, 2, 16, 128], f_wd=[NG, 2, 128, 4, 1024], w_qT=[12, 128, 16, 128], w_gr=[4, 128, 16, 128],
           w_gfm=[128, 16, 528], w_gtm=[128, 16, 512], wa2=[16, 256], ba=[1, 256], ng=[128, 1],
           w_gate=[16, 128, 3, 16, 128], w_br=[16, 128, 16, 128], w_out=[16, 128, 16, 128])


def _decl_common(nc):
    d = {}
    d["cst"] = nc.dram_tensor("cst", [128, 8, 128], F32, kind="ExternalInput").ap()
    d["cbf"] = nc.dram_tensor("cbf", [128, 4, 128], BF16, kind="ExternalInput").ap()
    d["lnp"] = nc.dram_tensor("lnp", [128, 192], F32, kind="ExternalInput").ap()
    d["msk"] = nc.dram_tensor("msk", [128, 3, 8, 128], BF16, kind="ExternalInput").ap()
    d["sel"] = nc.dram_tensor("sel", [128, 8], F32, kind="ExternalInput").ap()
    return d


def build_A(layer):
    nc = bass.Bass("TRN2", target_bir_lowering=False)
    d = _decl_common(nc)
    xT = nc.dram_tensor("xT", [DM, T], F32, kind="ExternalInput").ap()
    wl = {k: nc.dram_tensor(k, shp, F32, kind="ExternalInput").ap() for k, shp in W_A.items()}
    x1_o = nc.dram_tensor("x1_o", [DM, T], F32, kind="ExternalOutput").ap()
    KT_o = nc.dram_tensor("KT_o", [12, 128, T], BF16, kind="ExternalOutput").ap()
    V_o = nc.dram_tensor("V_o", [12, 128, 8, 128], BF16, kind="ExternalOutput").ap()
    lf_o = nc.dram_tensor("lf_o", [128, 48], F32, kind="ExternalOutput").ap()
    dS_o = nc.dram_tensor("dS_o", [8, 128, 256], F32, kind="ExternalOutput").ap()
    dc_o = nc.dram_tensor("dc_o", [128, 16], F32, kind="ExternalOutput").ap()
    C = setup_ctx(nc)
    P = C.P
    C.msk_d = d["msk"]
    emit_consts(C, d["cst"], d["cbf"], d["lnp"], d["msk"], d["sel"])
    xv = xT.rearrange("(c p) t -> p c t", p=128)
    for q in range(4):
        ks = [acck(c, h) for c in range(4 * q, 4 * q + 4) for h in range(2)]
        P.dma("sp", C.acc[:, 4 * q:4 * q + 4, :], xv[:, 4 * q:4 * q + 4, :], key=("xl", q), writes=ks)
        kb = [xbk(c, h) for c in range(4 * q, 4 * q + 4) for h in range(2)]
        P.dma("pool", C.xbf[:, 4 * q:4 * q + 4, :], xv[:, 4 * q:4 * q + 4, :], key=("xlb", q), writes=kb)
    emit_ffn(C, wl["f_wgu"], wl["f_wd"])
    emit_ln(C, layer * 3 + 0)
    xo = x1_o.rearrange("(c p) t -> p c t", p=128)
    for q in range(4):
        ks = [acck(c, h) for c in range(4 * q, 4 * q + 4) for h in range(2)]
        P.dma("sp", xo[:, 4 * q:4 * q + 4, :], C.acc[:, 4 * q:4 * q + 4, :], key=("xo", q), reads=ks)
    emit_phaseA_proj(C, wl, KT_o, V_o, lf_o, dS_o, dc_o)
    P.emit()
    return nc


class LazyW(dict):
    def __init__(self, nc, shapes):
        super().__init__()
        self.nc = nc
        self.shapes = shapes

    def __missing__(self, k):
        t = self.nc.dram_tensor(k, self.shapes[k], F32, kind="ExternalInput").ap()
        self[k] = t
        return t


def build_B(layer, dbg=False, stages=("gla", "scan", "tiles", "gr", "heads", "merge", "ffn")):
    nc = bass.Bass("TRN2", target_bir_lowering=False)
    d = _decl_common(nc)
    x1 = nc.dram_tensor("x1", [DM, T], F32, kind="ExternalInput").ap()
    wl = LazyW(nc, W_B)
    KT_all = V_all = lf_all = lf_own = None
    if "heads" in stages:
        KT_all = nc.dram_tensor("KT_all", [8, 12, 128, T], BF16, kind="ExternalInput").ap()
        V_all = nc.dram_tensor("V_all", [8, 12, 128, 8, 128], BF16, kind="ExternalInput").ap()
        lf_all = nc.dram_tensor("lf_all", [8, 128, 48], F32, kind="ExternalInput").ap()
        lf_own = nc.dram_tensor("lf_own", [128, 48], F32, kind="ExternalInput").ap()
    dS_all = nc.dram_tensor("dS_all", [8, 8, 128, 256], F32, kind="ExternalInput").ap()
    dc_all = nc.dram_tensor("dc_all", [8, 128, 16], F32, kind="ExternalInput").ap()
    x3_o = nc.dram_tensor("x3_o", [DM, T], F32, kind="ExternalOutput").ap()
    C = setup_ctx(nc)
    C.stages = stages
    P = C.P
    C.msk_d = d["msk"]
    emit_consts(C, d["cst"], d["cbf"], d["lnp"], d["msk"], d["sel"])
    xv = x1.rearrange("(c p) t -> p c t", p=128)
    for q in range(4):
        kb = [xbk(c, h) for c in range(4 * q, 4 * q + 4) for h in range(2)]
        P.dma("pool", C.xbf[:, 4 * q:4 * q + 4, :], xv[:, 4 * q:4 * q + 4, :], key=("xlb", q), writes=kb)
    oT = emit_mixer(C, wl, KT_all, V_all, lf_all, lf_own, dS_all, dc_all)
    if dbg:
        P.fence()
        dbg_oT = nc.dram_tensor("dbg_oT", [128, 16, T], BF16, kind="ExternalOutput").ap()
        P.dma("sp", dbg_oT, oT, key="dbg1", reads=[("oTh", c, h) for c in range(16) for h in range(2)])
        P.fence()
    if "merge" in stages:
        emit_merge_out(C, wl, oT, x1)
    if dbg:
        dbg_y = nc.dram_tensor("dbg_y", [128, 16, T], F32, kind="ExternalOutput").ap()
        P.dma("sp", dbg_y, C.acc[:, :, :], key="dbg2", reads=[acck(c, h) for c in range(16) for h in range(2)])
        P.fence()
    if "ffn" in stages:
        emit_ln(C, layer * 3 + 1)
        P.fence()
        emit_ffn(C, wl["f_wgu"], wl["f_wd"])
        emit_ln(C, layer * 3 + 2, want_bf=False)
    P.fence()
    xo = x3_o.rearrange("(c p) t -> p c t", p=128)
    for q in range(4):
        ks = [acck(c, h) for c in range(4 * q, 4 * q + 4) for h in range(2)]
        P.dma("sp", xo[:, 4 * q:4 * q + 4, :], C.acc[:, 4 * q:4 * q + 4, :], key=("xo", q), reads=ks)
    P.emit()
    return nc


RG = [list(range(NCORES))]
TINY_W = ("bf", "ba", "ng", "wa2")


def w2d(k, shp):
    nm = 3 if k == "f_wd" else (2 if len(shp) >= 4 else 1)
    rows = int(np.prod(shp[:nm]))
    cols = int(np.prod(shp[nm:]))
    assert rows % NCORES == 0
    return rows, cols


def build_fused():
    nc = bass.Bass("TRN2", target_bir_lowering=False)
    d = _decl_common(nc)
    xT = nc.dram_tensor("xT", [DM, T], F32, kind="ExternalInput").ap()
    x_o = nc.dram_tensor("x_o", [DM, T], F32, kind="ExternalOutput").ap()
    C = setup_ctx(nc)
    C.stages = ("gla", "scan", "tiles", "gr", "heads", "merge", "ffn")
    P = C.P
    C.msk_d = d["msk"]
    emit_consts(C, d["cst"], d["cbf"], d["lnp"], d["msk"], d["sel"])
    xv = xT.rearrange("(c p) t -> p c t", p=128)
    for q in range(4):
        ks = [acck(c, h) for c in range(4 * q, 4 * q + 4) for h in range(2)]
        P.dma("sp", C.acc[:, 4 * q:4 * q + 4, :], xv[:, 4 * q:4 * q + 4, :], key=("xl", q), writes=ks)
        kb = [xbk(c, h) for c in range(4 * q, 4 * q + 4) for h in range(2)]
        P.dma("pool", C.xbf[:, 4 * q:4 * q + 4, :], xv[:, 4 * q:4 * q + 4, :], key=("xlb", q), writes=kb)
    wsets = []
    pend = []
    for l in range(DEPTH):
        for pre, shapes in (("a%d_" % l, W_A), ("b%d_" % l, W_B)):
            ws = {}
            for k, shp in shapes.items():
                if k in TINY_W:
                    ws[k] = nc.dram_tensor(pre + k, shp, F32, kind="ExternalInput").ap()
                    continue
                rows, cols = w2d(k, shp)
                sh = nc.dram_tensor(pre + k, [rows // NCORES, cols], F32, kind="ExternalInput").ap()
                loc = nc.dram_tensor(pre + k + "_s", [rows // NCORES, cols], F32).ap()
                full = nc.dram_tensor(pre + k + "_g", [rows, cols], F32)
                P.dma("sp", loc, sh, key="wcp", writes=[("wsh", len(pend))])
                pend.append((pre + k, loc, full))
                ws[k] = full.reshape(list(shp)).ap()
            wsets.append(ws)
    allsh = [("wsh", i) for i in range(len(pend))]
    for nm, loc, full in pend:
        P.op("pool", lambda e, s_=loc, d_=full.ap(): e.collective_compute("AllGather", ALU.bypass, replica_groups=RG, ins=[s_.opt()], outs=[d_.opt()]),
             reads=allsh, writes=[("ccw", nm)], dma_key="ccw_sem")
        P.dram_dep[nm + "_g"] = ("ccw", nm)
    for l in range(DEPTH):
        wa = wsets[2 * l]
        wb = wsets[2 * l + 1]
        x1_d = nc.dram_tensor("x1_d%d" % l, [DM, T], F32).ap()
        KT_o = nc.dram_tensor("KT_o%d" % l, [12 * 128, T], BF16).ap()
        V_o = nc.dram_tensor("V_o%d" % l, [12 * 128, 8 * 128], BF16).ap()
        lf_o = nc.dram_tensor("lf_o%d" % l, [128, 48], F32).ap()
        dS_o = nc.dram_tensor("dS_o%d" % l, [8 * 128, 256], F32).ap()
        dc_o = nc.dram_tensor("dc_o%d" % l, [128, 16], F32).ap()
        KT_g = nc.dram_tensor("KT_g%d" % l, [8 * 12 * 128, T], BF16).ap()
        V_g = nc.dram_tensor("V_g%d" % l, [8 * 12 * 128, 8 * 128], BF16).ap()
        lf_g = nc.dram_tensor("lf_g%d" % l, [8 * 128, 48], F32).ap()
        dS_g = nc.dram_tensor("dS_g%d" % l, [8 * 8 * 128, 256], F32).ap()
        dc_g = nc.dram_tensor("dc_g%d" % l, [8 * 128, 16], F32).ap()
        if l > 0:
            P.fence()
        emit_ffn(C, wa["f_wgu"], wa["f_wd"])
        emit_ln(C, l * 3 + 0)
        xo = x1_d.rearrange("(c p) t -> p c t", p=128)
        for q in range(4):
            ks = [acck(c, h) for c in range(4 * q, 4 * q + 4) for h in range(2)]
            P.dma("sp", xo[:, 4 * q:4 * q + 4, :], C.acc[:, 4 * q:4 * q + 4, :], key=("xo", q), reads=ks)
        emit_phaseA_proj(C, wa, KT_o.rearrange("(h p) t -> h p t", p=128), V_o.rearrange("(h p) (i d) -> h p i d", p=128, d=128),
                         lf_o, dS_o.rearrange("(i p) x -> i p x", p=128), dc_o)
        P.fence()
        for nm, src, dst in (("KT", KT_o, KT_g), ("V", V_o, V_g), ("lf", lf_o, lf_g), ("dS", dS_o, dS_g), ("dc", dc_o, dc_g)):
            P.op("pool", lambda e, s_=src, d_=dst: e.collective_compute("AllGather", ALU.bypass, replica_groups=RG, ins=[s_.opt()], outs=[d_.opt()]),
                 reads=[], writes=[("cc", nm)], dma_key="cc_sem")
        oT = emit_mixer(C, wb, KT_g.rearrange("(r h p) t -> r h p t", r=8, p=128),
                        V_g.rearrange("(r h p) (i d) -> r h p i d", r=8, p=128, d=128),
                        lf_g.rearrange("(r p) x -> r p x", p=128), lf_o,
                        dS_g.rearrange("(r i p) x -> r i p x", r=8, p=128), dc_g.rearrange("(r p) x -> r p x", p=128))
        emit_merge_out(C, wb, oT, x1_d)
        emit_ln(C, l * 3 + 1)
        P.fence()
        emit_ffn(C, wb["f_wgu"], wb["f_wd"])
        emit_ln(C, l * 3 + 2, want_bf=(l < DEPTH - 1))
    P.fence()
    xo = x_o.rearrange("(c p) t -> p c t", p=128)
    for q in range(4):
        ks = [acck(c, h) for c in range(4 * q, 4 * q + 4) for h in range(2)]
        P.dma("sp", xo[:, 4 * q:4 * q + 4, :], C.acc[:, 4 * q:4 * q + 4, :], key=("xo", q), reads=ks)
    P.emit()
    return nc


def _tile_fm(w):
    n = w.shape[1] // 128
    return np.ascontiguousarray(w.reshape(16, 128, n, 128).transpose(2, 1, 0, 3))


def _tile_tm(w):
    return np.ascontiguousarray(w.reshape(16, 128, w.shape[1]).transpose(1, 0, 2))


def prep_ffn(wg, wu, wd):
    g = wg.reshape(16, 128, NFC, 128).transpose(2, 1, 0, 3)
    u = wu.reshape(16, 128, NFC, 128).transpose(2, 1, 0, 3)
    wgu = np.ascontiguousarray(np.stack([g, u], axis=2))
    d = wd.reshape(NG, 4, 128, 2, 1024).transpose(0, 3, 2, 1, 4)
    return wgu, np.ascontiguousarray(d)


def prep_layer(inp, l):
    w_in = inp["w_in"][l]
    o = {}
    o["f1_wgu"], o["f1_wd"] = prep_ffn(inp["ffn1_w_gate"][l], inp["ffn1_w_up"][l], inp["ffn1_w_down"][l])
    o["f2_wgu"], o["f2_wd"] = prep_ffn(inp["ffn2_w_gate"][l], inp["ffn2_w_up"][l], inp["ffn2_w_down"][l])
    o["w_kT"] = _tile_fm(np.concatenate([w_in[:, O_FK:O_FK + 768], w_in[:, O_SK:O_SK + 768]], axis=1))
    o["w_qT"] = _tile_fm(np.concatenate([w_in[:, O_FQ:O_FQ + 768], w_in[:, O_SQ:O_SQ + 768]], axis=1))
    wv = np.concatenate([w_in[:, O_FV:O_FV + 768], w_in[:, O_SV:O_SV + 768]], axis=1)
    o["w_v"] = np.ascontiguousarray(wv.reshape(16, 128, 3, 512).transpose(2, 1, 0, 3))
    o["w_ff"] = _tile_tm(w_in[:, O_FF:O_FF + 6])
    o["bf"] = np.ascontiguousarray(inp["fox_b_f"][l].reshape(1, 6))
    o["w_gfm"] = _tile_tm(np.concatenate([w_in[:, O_GQ:O_GQ + 256], w_in[:, O_GK:O_GK + 256], w_in[:, O_GA:O_GA + 16]], axis=1))
    o["w_gtm"] = _tile_tm(w_in[:, O_GV:O_GV + 512])
    o["wa2"] = np.ascontiguousarray(inp["gla_w_a2"][l])
    o["ba"] = np.ascontiguousarray(inp["gla_b_a"][l].reshape(1, 256))
    o["ng"] = np.ascontiguousarray(inp["gla_norm_g"][l].reshape(128, 1))
    o["w_gr"] = _tile_fm(w_in[:, O_GR:O_GR + 512])
    gts = w_in[:, O_GATE:O_GATE + 3 * DM].reshape(16, 128, 3, 16, 128)
    o["w_gate"] = np.ascontiguousarray(gts.transpose(3, 1, 2, 0, 4))
    wbr = np.concatenate([inp["w_br_gla"][l], inp["w_br_fox"][l], inp["w_br_sb"][l]], axis=0)
    o["w_br"] = _tile_fm(wbr)
    o["w_out"] = _tile_fm(inp["w_out"][l])
    return o


def prep_lnp(inp):
    arr = np.zeros((128, 2 * 3 * 2 * 16), np.float32)
    for l in range(DEPTH):
        for wi, nm in enumerate(["ln1", "ln2", "ln3"]):
            li = l * 3 + wi
            arr[:, (li * 2 + 0) * 16:(li * 2 + 1) * 16] = inp[nm + "_g"][l].reshape(16, 128).T
            arr[:, (li * 2 + 1) * 16:(li * 2 + 2) * 16] = inp[nm + "_b"][l].reshape(16, 128).T
    return arr


_PROG = {}


FUSED = False


def kernel(**inputs):
    inp = {k: np.asarray(v) for k, v in inputs.items()}
    x = inp["x"][0]
    xt = x.reshape(8, 8, 128, DM)
    lnp = prep_lnp(inp)
    common = []
    for c in range(NCORES):
        cst, cbf, msk, sel = host_consts(c)
        common.append(dict(cst=cst, cbf=cbf, lnp=lnp, msk=msk, sel=sel))
    cur = [np.ascontiguousarray(xt[:, c].reshape(T, DM).T) for c in range(NCORES)]
    cores = list(range(NCORES))
    if FUSED:
        if "F" not in _PROG:
            _PROG["F"] = build_fused()
        wall = {}
        for l in range(DEPTH):
            wl = prep_layer(inp, l)
            for k in W_A:
                src = {"f_wgu": "f1_wgu", "f_wd": "f1_wd"}.get(k, k)
                wall["a%d_%s" % (l, k)] = wl[src]
            for k in W_B:
                src = {"f_wgu": "f2_wgu", "f_wd": "f2_wd"}.get(k, k)
                wall["b%d_%s" % (l, k)] = wl[src]
        in_maps = []
        for c in cores:
            m = dict(common[c], xT=cur[c])
            for nm, arr in wall.items():
                k = nm.split("_", 1)[1]
                if k in TINY_W:
                    m[nm] = arr
                else:
                    rows, cols = w2d(k, arr.shape)
                    m[nm] = arr.reshape(rows, cols)[c * (rows // NCORES):(c + 1) * (rows // NCORES)]
            in_maps.append(m)
        res = run_bass_kernel_spmd(_PROG["F"], in_maps, core_ids=cores).results
        cur = [r["x_o"] for r in res]
    else:
        for l in range(DEPTH):
            wl = prep_layer(inp, l)
            if ("A", l) not in _PROG:
                _PROG[("A", l)] = build_A(l)
            wa = {k: wl[k] for k in W_A if not k.startswith("f_")}
            wa["f_wgu"], wa["f_wd"] = wl["f1_wgu"], wl["f1_wd"]
            in_maps = [dict(common[c], xT=cur[c], **wa) for c in cores]
            resA = run_bass_kernel_spmd(_PROG[("A", l)], in_maps, core_ids=cores).results
            KT_all = np.stack([r["KT_o"] for r in resA])
            V_all = np.stack([r["V_o"] for r in resA])
            lf_all = np.stack([r["lf_o"] for r in resA])
            dS_all = np.stack([r["dS_o"] for r in resA])
            dc_all = np.stack([r["dc_o"] for r in resA])
            if ("B", l) not in _PROG:
                _PROG[("B", l)] = build_B(l)
            wb = {k: wl[k] for k in W_B if not k.startswith("f_")}
            wb["f_wgu"], wb["f_wd"] = wl["f2_wgu"], wl["f2_wd"]
            in_maps = [dict(common[c], x1=resA[c]["x1_o"], KT_all=KT_all, V_all=V_all, lf_all=lf_all, lf_own=resA[c]["lf_o"],
                            dS_all=dS_all, dc_all=dc_all, **wb) for c in cores]
            resB = run_bass_kernel_spmd(_PROG[("B", l)], in_maps, core_ids=cores).results
            cur = [r["x3_o"] for r in resB]
    out = np.zeros((8, 8, 128, DM), np.float32)
    for c in range(NCORES):
        out[:, c] = cur[c].T.reshape(8, 128, DM)
    return out.reshape(1, S, DM)
```
